# Optimizing a Trainium2 kernel written in Bass

```python
import math
import jax, jax.numpy as jnp
from jax import lax
import numpy as np

D_MODEL = 1024
BATCH = 8
SEQ = 4096
DEPTH = 2

N_MIXERS = 2
N_A = (DEPTH + 1) // 2
N_B = DEPTH // 2
MLA_HEADS = 16
MLA_Q_RANK = 512
MLA_KV_RANK = 256
MLA_NOPE = 64
MLA_ROPE = 32
MLA_V = 64
MLA_IN = MLA_Q_RANK + MLA_KV_RANK + MLA_ROPE
DSA_HEADS = 16
DSA_KV_HEADS = 4
DSA_GROUP = DSA_HEADS // DSA_KV_HEADS
DSA_HEAD_DIM = 64
IDX_HEADS = 8
IDX_DIM = 64
TOPK_MAX = 256
DSA_IN = (DSA_HEADS * DSA_HEAD_DIM + 2 * DSA_KV_HEADS * DSA_HEAD_DIM
          + IDX_HEADS * IDX_DIM + IDX_DIM + IDX_HEADS)
MEM_LEN = 256
XA_HEADS = 4
XA_HEAD_DIM = D_MODEL // XA_HEADS
FFN_HIDDEN = -(-8 * D_MODEL // (3 * 256)) * 256
REL_BUCKETS = 32
REL_MAX_DIST = 128
Q_BLOCK = 128
ROPE_BASE = 10000.0
LN_EPS = 1e-5
RMS_EPS = 1e-6
ALPHA = (2 * DEPTH) ** 0.25
BETA = (8 * DEPTH) ** -0.25

kernel_name = "hybrid_mla_dsa_deepnorm_block"


def layer_norm(x, g, b):
    x32 = x.astype(jnp.float32)
    mu = jnp.mean(x32, axis=-1, keepdims=True)
    var = jnp.mean(jnp.square(x32 - mu), axis=-1, keepdims=True)
    y = (x32 - mu) * lax.rsqrt(var + LN_EPS) * g.astype(jnp.float32) + b.astype(jnp.float32)
    return y.astype(x.dtype)


def rms_norm(x, g):
    x32 = x.astype(jnp.float32)
    y = x32 * lax.rsqrt(jnp.mean(jnp.square(x32), axis=-1, keepdims=True) + RMS_EPS)
    return (y * g.astype(jnp.float32)).astype(x.dtype)


def apply_rope(x, pos):
    r = x.shape[-1]
    inv_freq = ROPE_BASE ** (-jnp.arange(0, r, 2, dtype=jnp.float32) / r)
    ang = pos.astype(jnp.float32)[..., None] * inv_freq
    cos = jnp.cos(ang)[:, :, None, :]
    sin = jnp.sin(ang)[:, :, None, :]
    x32 = x.astype(jnp.float32)
    x1, x2 = x32[..., : r // 2], x32[..., r // 2:]
    out = jnp.concatenate([x1 * cos - x2 * sin, x1 * sin + x2 * cos], axis=-1)
    return out.astype(x.dtype)


def t5_bucket(dist):
    max_exact = REL_BUCKETS // 2
    n = jnp.maximum(dist, 0)
    nf = jnp.maximum(n, 1).astype(jnp.float32)
    large = max_exact + (jnp.log(nf / max_exact) / math.log(REL_MAX_DIST / max_exact)
                         * (REL_BUCKETS - max_exact)).astype(jnp.int32)
    large = jnp.minimum(large, REL_BUCKETS - 1)
    return jnp.where(n < max_exact, n, large)


def to_blocks(a, nb):
    return jnp.moveaxis(a.reshape(a.shape[0], nb, Q_BLOCK, *a.shape[2:]), 1, 0)


def from_blocks(a):
    a = jnp.moveaxis(a, 0, 1)
    return a.reshape(a.shape[0], a.shape[1] * a.shape[2], *a.shape[3:])


gather_rows = jax.vmap(lambda table, idx: table[idx])


def dense_causal_attention(q, k, v, scale):
    s_len = q.shape[1]
    nb = s_len // Q_BLOCK
    key_idx = jnp.arange(s_len)

    def one_block(args):
        q_blk, start = args
        logits = jnp.einsum('bqhd,bkhd->bhqk', q_blk, k).astype(jnp.float32) * scale
        q_idx = start + jnp.arange(Q_BLOCK)
        causal = key_idx[None, :] <= q_idx[:, None]
        logits = jnp.where(causal[None, None], logits, -jnp.inf)
        p = jax.nn.softmax(logits, axis=-1).astype(v.dtype)
        return jnp.einsum('bhqk,bkhd->bqhd', p, v)

    out = lax.map(one_block, (to_blocks(q, nb), jnp.arange(nb) * Q_BLOCK))
    return from_blocks(out)


def mla_mixer(x, pos, w_in, q_norm, w_uq, kv_norm, w_ukv, w_o):
    b, s, _ = x.shape
    h = x @ w_in
    c_q = h[..., :MLA_Q_RANK]
    c_kv = h[..., MLA_Q_RANK:MLA_Q_RANK + MLA_KV_RANK]
    k_rope = h[..., MLA_Q_RANK + MLA_KV_RANK:]
    q = (rms_norm(c_q, q_norm) @ w_uq).reshape(b, s, MLA_HEADS, MLA_NOPE + MLA_ROPE)
    q = jnp.concatenate([q[..., :MLA_NOPE], apply_rope(q[..., MLA_NOPE:], pos)], axis=-1)
    kv = (rms_norm(c_kv, kv_norm) @ w_ukv).reshape(b, s, MLA_HEADS, MLA_NOPE + MLA_V)
    k_nope, v = kv[..., :MLA_NOPE], kv[..., MLA_NOPE:]
    k_rope = apply_rope(k_rope[:, :, None, :], pos)
    k = jnp.concatenate([k_nope, jnp.broadcast_to(k_rope, (b, s, MLA_HEADS, MLA_ROPE))], axis=-1)
    out = dense_causal_attention(q, k, v, (MLA_NOPE + MLA_ROPE) ** -0.5)
    return out.reshape(b, s, MLA_HEADS * MLA_V) @ w_o


def dsa_mixer(x, pos, rel_bias, w_in, w_o):
    b, s, _ = x.shape
    nb = s // Q_BLOCK
    topk = min(TOPK_MAX, s // 4)
    h = x @ w_in
    o1 = DSA_HEADS * DSA_HEAD_DIM
    o2 = o1 + DSA_KV_HEADS * DSA_HEAD_DIM
    o3 = o2 + DSA_KV_HEADS * DSA_HEAD_DIM
    o4 = o3 + IDX_HEADS * IDX_DIM
    o5 = o4 + IDX_DIM
    q = h[..., :o1].reshape(b, s, DSA_HEADS, DSA_HEAD_DIM)
    k = h[..., o1:o2].reshape(b, s, DSA_KV_HEADS, DSA_HEAD_DIM)
    v = h[..., o2:o3].reshape(b, s, DSA_KV_HEADS, DSA_HEAD_DIM)
    q_idx = h[..., o3:o4].reshape(b, s, IDX_HEADS, IDX_DIM)
    k_idx = h[..., o4:o5]
    w_idx = h[..., o5:] * (IDX_HEADS ** -0.5)
    key_idx = jnp.arange(s)

    def one_block(args):
        q_blk, qi_blk, wi_blk, pos_blk, start = args
        t_idx = start + jnp.arange(Q_BLOCK)
        causal = key_idx[None, :] <= t_idx[:, None]
        dots = jnp.einsum('bqhd,bkd->bqhk', qi_blk, k_idx).astype(jnp.float32) * (IDX_DIM ** -0.5)
        score = jnp.einsum('bqh,bqhk->bqk', wi_blk.astype(jnp.float32), jax.nn.relu(dots))
        score = jnp.where(causal[None], score, -jnp.inf)
        _, sel = lax.top_k(score, topk)
        sel_ok = sel <= t_idx[None, :, None]
        k_sel = gather_rows(k, sel)
        v_sel = gather_rows(v, sel)
        pos_sel = gather_rows(pos, sel)
        qg = q_blk.reshape(b, Q_BLOCK, DSA_KV_HEADS, DSA_GROUP, DSA_HEAD_DIM)
        logits = jnp.einsum('bqngd,bqknd->bqngk', qg, k_sel).astype(jnp.float32) * (DSA_HEAD_DIM ** -0.5)
        bucket = t5_bucket(pos_blk[:, :, None] - pos_sel)
        bias = rel_bias[bucket].astype(jnp.float32)
        bias = bias.reshape(b, Q_BLOCK, topk, DSA_KV_HEADS, DSA_GROUP).transpose(0, 1, 3, 4, 2)
        logits = jnp.where(sel_ok[:, :, None, None, :], logits + bias, -jnp.inf)
        p = jax.nn.softmax(logits, axis=-1).astype(v.dtype)
        o = jnp.einsum('bqngk,bqknd->bqngd', p, v_sel)
        return o.reshape(b, Q_BLOCK, DSA_HEADS * DSA_HEAD_DIM)

    out = lax.map(one_block, (to_blocks(q, nb), to_blocks(q_idx, nb), to_blocks(w_idx, nb),
                              to_blocks(pos, nb), jnp.arange(nb) * Q_BLOCK))
    return from_blocks(out) @ w_o


def memory_cross_attention(x, mem, w_q, w_kv, w_o):
    b, s, _ = x.shape
    m = mem.shape[1]
    q = (x @ w_q).reshape(b, s, XA_HEADS, XA_HEAD_DIM)
    kv = (mem @ w_kv).reshape(b, m, 2, XA_HEADS, XA_HEAD_DIM)
    k, v = kv[:, :, 0], kv[:, :, 1]
    logits = jnp.einsum('bshd,bmhd->bhsm', q, k).astype(jnp.float32) * (XA_HEAD_DIM ** -0.5)
    p = jax.nn.softmax(logits, axis=-1).astype(v.dtype)
    o = jnp.einsum('bhsm,bmhd->bshd', p, v)
    return o.reshape(b, s, D_MODEL) @ w_o


def swiglu_ffn(x, w_in, w_down):
    h = x @ w_in
    g, u = h[..., :FFN_HIDDEN], h[..., FFN_HIDDEN:]
    return (jax.nn.silu(g) * u) @ w_down


def setup_inputs(seed: int = 0) -> dict:
    key = jax.random.key(seed)
    ks = jax.random.split(key, 24)
    f32 = jnp.float32

    def w(k, shape, fan_in, scale=1.0):
        return jax.random.normal(k, shape, f32) * (fan_in ** -0.5) * scale

    x = jax.random.normal(ks[0], (BATCH, SEQ, D_MODEL), f32)
    mem = jax.random.normal(ks[1], (BATCH, MEM_LEN, D_MODEL), f32)
    offset = jax.random.randint(ks[2], (BATCH, 1), 0, 1024, dtype=jnp.int32)
    positions = (offset + jnp.arange(SEQ, dtype=jnp.int32)[None, :]).astype(jnp.int32)
    rel_bias = jax.random.normal(ks[3], (REL_BUCKETS, DSA_HEADS), f32) * 0.5
    return {
        "x": x,
        "mem": mem,
        "positions": positions,
        "rel_bias": rel_bias,
        "mla_w_in": w(ks[4], (N_A, D_MODEL, MLA_IN), D_MODEL),
        "mla_q_norm": 1.0 + 0.01 * jax.random.normal(ks[5], (N_A, MLA_Q_RANK), f32),
        "mla_w_uq": w(ks[6], (N_A, MLA_Q_RANK, MLA_HEADS * (MLA_NOPE + MLA_ROPE)), MLA_Q_RANK),
        "mla_kv_norm": 1.0 + 0.01 * jax.random.normal(ks[7], (N_A, MLA_KV_RANK), f32),
        "mla_w_ukv": w(ks[8], (N_A, MLA_KV_RANK, MLA_HEADS * (MLA_NOPE + MLA_V)), MLA_KV_RANK),
        "mla_w_o": w(ks[9], (N_A, MLA_HEADS * MLA_V, D_MODEL), MLA_HEADS * MLA_V, BETA),
        "dsa_w_in": w(ks[10], (N_B, D_MODEL, DSA_IN), D_MODEL),
        "dsa_w_o": w(ks[11], (N_B, DSA_HEADS * DSA_HEAD_DIM, D_MODEL), DSA_HEADS * DSA_HEAD_DIM, BETA),
        "xa_w_q": w(ks[12], (DEPTH, D_MODEL, D_MODEL), D_MODEL),
        "xa_w_kv": w(ks[13], (DEPTH, D_MODEL, 2 * D_MODEL), D_MODEL),
        "xa_w_o": w(ks[14], (DEPTH, D_MODEL, D_MODEL), D_MODEL, BETA),
        "ffn_w_in": w(ks[15], (DEPTH, D_MODEL, 2 * FFN_HIDDEN), D_MODEL),
        "ffn_w_down": w(ks[16], (DEPTH, FFN_HIDDEN, D_MODEL), FFN_HIDDEN, BETA),
        "ln_g": 1.0 + 0.01 * jax.random.normal(ks[17], (DEPTH, 3, D_MODEL), f32),
        "ln_b": 0.01 * jax.random.normal(ks[18], (DEPTH, 3, D_MODEL), f32),
    }


def reference(x, mem, positions, rel_bias, mla_w_in, mla_q_norm, mla_w_uq, mla_kv_norm,
              mla_w_ukv, mla_w_o, dsa_w_in, dsa_w_o, xa_w_q, xa_w_kv, xa_w_o,
              ffn_w_in, ffn_w_down, ln_g, ln_b):
    for i in range(DEPTH):
        j = i // N_MIXERS
        if i % N_MIXERS == 0:
            y = mla_mixer(x, positions, mla_w_in[j], mla_q_norm[j], mla_w_uq[j],
                          mla_kv_norm[j], mla_w_ukv[j], mla_w_o[j])
        else:
            y = dsa_mixer(x, positions, rel_bias, dsa_w_in[j], dsa_w_o[j])
        x = layer_norm(ALPHA * x + y, ln_g[i, 0], ln_b[i, 0])
        y = memory_cross_attention(x, mem, xa_w_q[i], xa_w_kv[i], xa_w_o[i])
        x = layer_norm(ALPHA * x + y, ln_g[i, 1], ln_b[i, 1])
        y = swiglu_ffn(x, ffn_w_in[i], ffn_w_down[i])
        x = layer_norm(ALPHA * x + y, ln_g[i, 2], ln_b[i, 2])
    return x
```

```python
import contextlib
import math

import numpy as np
import concourse.bass as bass
import concourse.mybir as mybir
from concourse.bass_utils import run_bass_kernel_spmd

F32 = mybir.dt.float32
BF16 = mybir.dt.bfloat16
I32 = mybir.dt.int32
AF = mybir.ActivationFunctionType
ALU = mybir.AluOpType

S = 4096
D = 1024
NT = S // 128
NS = S // 512
ALPHA = 4 ** 0.25
LN_EPS = 1e-5
RMS_EPS = 1e-6
FFN_H = 2816
NEG = -30000.0
TOPK = 256
N_BISECT = 24


class Buf:
    __slots__ = ("name", "w", "r")

    def __init__(self, name=""):
        self.name = name
        self.w = None
        self.r = []


class Eng:
    def __init__(self, name, inorder_safe=False):
        self.name = name
        self.ops = []
        self.sem = None
        self.count = 0
        self.seen = {}
        self.inorder_safe = inorder_safe
        self.pending_unsignaled = False
        self.dma_sems = []
        self.dma_n = 0
        self.dma_evs = []


class FW:
    SEM_EPOCH = 30000

    def __init__(self, nc, stack, n_dma_sems=(32, 12)):
        self.nc = nc
        self.stack = stack
        self.pe = Eng("pe", inorder_safe=True)
        self.act = Eng("act")
        self.dve = Eng("dve")
        self.pool = Eng("pool")
        self.sp = Eng("sp")
        self.engs = [self.pe, self.act, self.dve, self.pool, self.sp]
        self.nsem = 0
        for e in self.engs:
            self._new_sem(e)
        for e, n in ((self.sp, n_dma_sems[0]), (self.pool, n_dma_sems[1])):
            for i in range(n):
                e.dma_sems.append(self._alloc_sem(f"dma_{e.name}_{i}"))
                e.dma_evs.append(None)
        self.out_evs = []
        self.n_ops = 0

    def _alloc_sem(self, name):
        self.nsem += 1
        return self.stack.enter_context(self.nc.semaphore(name))

    def _new_sem(self, e):
        e.sem = self._alloc_sem(f"s_{e.name}_{self.nsem}")
        e.count = 0

    def _need(self, eng, ev, waits):
        if ev is None:
            return
        sem, val, src = ev
        if src is eng and eng.inorder_safe:
            return
        if src is eng:
            assert not (sem is eng.sem and val > eng.count), "self-wait on unsignaled op"
        k = id(sem)
        if eng.seen.get(k, 0) >= val:
            return
        if k in waits:
            if waits[k][1] < val:
                waits[k] = (sem, val)
        else:
            waits[k] = (sem, val)

    def _collect(self, eng, reads, writes):
        waits = {}
        for b in reads:
            self._need(eng, b.w, waits)
        for b in writes:
            self._need(eng, b.w, waits)
            for ev in b.r:
                self._need(eng, ev, waits)
        return waits

    def _commit_waits(self, eng, waits):
        lst = []
        for k, (sem, val) in waits.items():
            eng.seen[k] = max(eng.seen.get(k, 0), val)
            lst.append((sem, val))
        return lst

    def _record(self, ev, reads, writes):
        for b in reads:
            b.r.append(ev)
        for b in writes:
            b.w = ev
            b.r = []

    def op(self, eng, meth, *args, reads=(), writes=(), signal=True, **kw):
        def fn(h, meth=meth, args=args, kw=kw):
            return getattr(h, meth)(*args, **kw)
        waits = self._collect(eng, reads, writes)
        if signal and eng.count >= self.SEM_EPOCH and not eng.pending_unsignaled:
            self._new_sem(eng)
        lst = self._commit_waits(eng, waits)
        if signal:
            eng.count += 1
            ev = (eng.sem, eng.count, eng)
            upd = (eng.sem, 1)
            eng.pending_unsignaled = False
        else:
            eng.pending_unsignaled = True
            ev = (eng.sem, eng.count + 1, eng)
            upd = None
        eng.ops.append((lst, fn, upd))
        self._record(ev, reads, writes)
        self.n_ops += 1
        return ev

    def dma(self, eng, out, in_, reads=(), writes=(), is_output=False):
        waits = self._collect(eng, reads, writes)
        slot = eng.dma_n % len(eng.dma_sems)
        self._need(eng, eng.dma_evs[slot], waits)
        lst = self._commit_waits(eng, waits)
        sem = eng.dma_sems[slot]
        val = 16 * (eng.dma_n // len(eng.dma_sems) + 1)
        ev = (sem, val, None)
        eng.dma_evs[slot] = ev
        eng.dma_n += 1

        def fn(h, out=out, in_=in_):
            return h.dma_start(out=out, in_=in_)
        eng.ops.append((lst, fn, (sem, 16)))
        self._record(ev, reads, writes)
        if is_output:
            self.out_evs.append(ev)
        self.n_ops += 1
        return ev

    def wait_events(self, eng, evs):
        waits = {}
        for ev in evs:
            self._need(eng, ev, waits)
        lst = self._commit_waits(eng, waits)
        if lst:
            eng.ops.append((lst, None, None))

    def barrier(self):
        evs = []
        for e in self.engs:
            assert not e.pending_unsignaled
            if e.count > 0:
                evs.append((e.sem, e.count, None))
            for d in e.dma_evs:
                if d is not None:
                    evs.append(d)
        for e in self.engs:
            self.wait_events(e, evs)

    def emit(self):
        nc = self.nc
        handles = {"pe": "tensor", "act": "scalar", "dve": "vector", "pool": "gpsimd", "sp": "sync"}
        with nc.Block() as block:
            def runner(eng):
                ops = eng.ops

                def body(h):
                    for (waits, fn, upd) in ops:
                        for (sem, val) in waits:
                            h.wait_ge(sem, val)
                        if fn is not None:
                            ins = fn(h)
                            if upd is not None:
                                ins.then_inc(upd[0], upd[1])
                return body
            for e in self.engs:
                getattr(block, handles[e.name])(runner(e))
        for e in self.engs:
            e.ops = []


class Rot:
    def __init__(self, items):
        self.items = items
        self.i = 0

    def next(self):
        it = self.items[self.i % len(self.items)]
        self.i += 1
        return it


class Ctx:
    pass


def build_program(debug=()):
    nc = bass.Bass("TRN2", target_bir_lowering=False)
    g = Ctx()
    g.nc = nc
    g.debug = debug

    def din(name, shape, dt=F32):
        return nc.dram_tensor(name, list(shape), dt, kind="ExternalInput").ap()

    def dscr(name, shape, dt):
        kind = "ExternalOutput" if name in debug else "Internal"
        return nc.dram_tensor(name, list(shape), dt, kind=kind).ap()

    I = Ctx()
    g.I = I
    I.x = din("x", [S, D])
    I.mem = din("mem", [256, D])
    I.pos = din("pos", [1, S], I32)
    I.rel_bias = din("rel_bias", [32, 16])
    I.mla_in = din("mla_in", [D, 832])
    I.mla_qn = din("mla_qn", [1, 512])
    I.mla_kvn = din("mla_kvn", [1, 256])
    I.mla_uq = din("mla_uq", [512, 3072])
    I.mla_ukv = din("mla_ukv", [256, 2048])
    I.mla_wo = din("mla_wo", [D, D])
    I.dsa_in = din("dsa_in", [D, 2120])
    I.dsa_wo = din("dsa_wo", [D, D])
    I.xa_wq = din("xa_wq", [2, D, D])
    I.xa_wkv = din("xa_wkv", [2, D, 2 * D])
    I.xa_wo = din("xa_wo", [2, D, D])
    I.ffn_in = din("ffn_in", [2, D, 2 * FFN_H])
    I.ffn_dn = din("ffn_dn", [2, FFN_H, D])
    I.ln_g = din("ln_g", [6, D])
    I.ln_b = din("ln_b", [6, D])
    I.c_ident = din("c_ident", [128, 128])
    I.c_tri = din("c_tri", [128, 128])
    I.c_trif = din("c_trif", [128, 128])
    I.c_invf = din("c_invf", [32, 2])
    I.c_oh = din("c_oh", [32, 384])
    I.c_pow2 = din("c_pow2", [1, 32])
    I.c_antiident = din("c_antiident", [128, 128])

    g.out = nc.dram_tensor("out", [S, D], F32, kind="ExternalOutput").ap()
    g.xa = dscr("xa", [S, D], F32)
    g.xb = dscr("xb", [S, D], F32)
    g.xc = dscr("xc", [S, D], F32)
    g.QT = dscr("QT", [16, 96, S], BF16)
    g.KT = dscr("KT", [16, 64, S], BF16)
    g.KR = dscr("KR", [32, S], BF16)
    g.V = dscr("V", [S, 16, 65], BF16)
    g.OT = dscr("OT", [16, 64, S], BF16)
    g.QI = dscr("QI", [512, S], BF16)
    g.KI = dscr("KI", [64, S], BF16)
    g.WI = dscr("WI", [S, 8], F32)
    g.MBT = dscr("MBT", [NT, 128, S], BF16)
    g.TB = dscr("TB", [16, 384], F32)
    g.b_TB = Buf("TB")
    g.QT1 = dscr("QT1", [1024, S], BF16)
    g.KT1 = dscr("KT1", [256, S], BF16)
    g.V1 = dscr("V1", [S, 4, 65], BF16)

    with contextlib.ExitStack() as gst:
        fw = FW(nc, gst)
        g.fw = fw
        g.gst = gst
        setup_globals(g)
        layer0(g)
        layer1(g)
        fw.wait_events(fw.sp, fw.out_evs)
        fw.emit()
    g.n_ops = fw.n_ops
    return nc


def _uname(g, name):
    g.uid = getattr(g, "uid", 0) + 1
    return f"{name}_{g.uid}"


def sb(g, st, name, shape, dt):
    return st.enter_context(g.nc.sbuf_tensor(_uname(g, name), list(shape), dt))


def ps(g, st, name, shape, dt=F32):
    return st.enter_context(g.nc.psum_tensor(_uname(g, name), list(shape), dt))


def rot_sb(g, st, name, shape, dt, n):
    return Rot([(sb(g, st, f"{name}{i}", shape, dt), Buf(f"{name}{i}")) for i in range(n)])


def rot_ps(g, st, name, shape, dt, n):
    return Rot([(ps(g, st, f"{name}{i}", shape, dt), Buf(f"{name}{i}")) for i in range(n)])


def end_phase(g):
    g.fw.barrier()
    g.fw.emit()


def MM(g, out, lhsT, rhs, start, stop, reads, writes, signal=None):
    g.fw.op(g.fw.pe, "matmul", out, lhsT, rhs, start=start, stop=stop, reads=reads, writes=writes,
            signal=(stop if signal is None else signal))


def ACT(g, out, in_, func, reads, writes, **kw):
    g.fw.op(g.fw.act, "activation", out=out, in_=in_, func=func, reads=reads, writes=writes, **kw)


def TT(g, eng, out, in0, in1, op, reads, writes):
    g.fw.op(eng, "tensor_tensor", out=out, in0=in0, in1=in1, op=op, reads=reads, writes=writes)


def STT(g, eng, out, in0, scalar, in1, op0, op1, reads, writes):
    g.fw.op(eng, "scalar_tensor_tensor", out=out, in0=in0, scalar=scalar, in1=in1, op0=op0, op1=op1, reads=reads, writes=writes)


def TS(g, eng, out, in0, s1, s2, op0, op1, reads, writes, **kw):
    if op1 is None:
        g.fw.op(eng, "tensor_scalar", out=out, in0=in0, scalar1=s1, scalar2=s2, op0=op0, reads=reads, writes=writes, **kw)
    else:
        g.fw.op(eng, "tensor_scalar", out=out, in0=in0, scalar1=s1, scalar2=s2, op0=op0, op1=op1, reads=reads, writes=writes, **kw)


def CP(g, eng, out, in_, reads, writes):
    if eng is g.fw.act:
        g.fw.op(eng, "copy", out, in_, reads=reads, writes=writes)
    else:
        g.fw.op(eng, "tensor_copy", out, in_, reads=reads, writes=writes)


def load_w(g, dst, bdst, src, K, ncols, col0=0, split=1):
    fw = g.fw
    nch = K // 128
    v = src.rearrange("(c p) n -> p c n", p=128)
    step = (nch + split - 1) // split
    for c0 in range(0, nch, step):
        c1 = min(nch, c0 + step)
        fw.dma(fw.pool, dst[:, c0:c1, :], v[:, c0:c1, col0:col0 + ncols], writes=[bdst])


def setup_globals(g):
    fw, st = g.fw, g.gst
    g.ident = sb(g, st, "ident", [128, 128], BF16)
    g.b_ident = Buf("ident")
    g.tri = sb(g, st, "tri", [128, 128], BF16)
    g.b_tri = Buf("tri")
    g.ones = sb(g, st, "ones", [128, 64], BF16)
    g.b_ones = Buf("ones")
    fw.dma(fw.pool, g.ident[:], g.I.c_ident[:, :], writes=[g.b_ident])
    fw.dma(fw.pool, g.tri[:], g.I.c_tri[:, :], writes=[g.b_tri])
    fw.op(fw.dve, "memset", g.ones[:], 1.0, writes=[g.b_ones])


def transpose_into(g, src, bsrc, xbf_rot, tp_rot, dstT, bdstT, j, nch, src_is_bf16=False):
    fw = g.fw
    if src_is_bf16:
        xb, bxb = src, bsrc
    else:
        xb, bxb = xbf_rot.next()
        CP(g, fw.act, xb[:, 0:nch * 128], src[:, 0:nch * 128], reads=[bsrc], writes=[bxb])
    tp, btp = tp_rot.next()
    for c in range(nch):
        fw.op(fw.pe, "transpose", tp[:, c, :], xb[:, c * 128:(c + 1) * 128], g.ident[:],
              reads=[bxb, g.b_ident], writes=[btp], signal=(c == nch - 1))
    CP(g, fw.dve, dstT[:, 0:nch, j * 128:(j + 1) * 128], tp[:, 0:nch, :], reads=[btp], writes=[bdstT])


class XLoader:
    def __init__(self, g, st, n_xs=4, n_xT=1, n_xbf=2):
        self.g = g
        self.xs = rot_sb(g, st, "xs", [128, D], F32, n_xs)
        self.xbf = rot_sb(g, st, "xbf", [128, D], BF16, n_xbf)
        self.xT = Rot([(sb(g, st, f"xT{i}", [128, 8, 512], BF16), [Buf(f"xT{i}_{j}") for j in range(4)]) for i in range(n_xT)])
        self.tp = rot_ps(g, st, "tpx", [128, 8, 128], BF16, 2)

    def load(self, src, s):
        g, fw = self.g, self.g.fw
        xT, bxT = self.xT.next()
        tiles = []
        for j in range(4):
            t0 = s * 512 + j * 128
            xs, bxs = self.xs.next()
            fw.dma(fw.sp, xs[:], src[t0:t0 + 128, :], writes=[bxs])
            tiles.append((xs, bxs))
            transpose_into(g, xs, bxs, self.xbf, self.tp, xT, bxT[j], j, 8)
        return xT, bxT, tiles


class LNUnit:
    def __init__(self, g, st, ln_idx):
        self.g = g
        fw = g.fw
        self.gam = sb(g, st, "ln_gam", [128, D], F32)
        self.bet = sb(g, st, "ln_bet", [128, D], F32)
        self.bgam, self.bbet = Buf("ln_g"), Buf("ln_b")
        fw.dma(fw.sp, self.gam[:], g.I.ln_g[ln_idx:ln_idx + 1, :].partition_broadcast(128), writes=[self.bgam])
        fw.dma(fw.sp, self.bet[:], g.I.ln_b[ln_idx:ln_idx + 1, :].partition_broadcast(128), writes=[self.bbet])
        self.stats = rot_sb(g, st, "ln_st", [128, 2, 6], F32, 3)
        self.mv = rot_sb(g, st, "ln_mv", [128, 4], F32, 3)

    def run(self, xs, bxs, y0, by0, y1, by1, dst, is_output=False):
        g, fw = self.g, self.g.fw
        stt, bst = self.stats.next()
        mv, bmv = self.mv.next()
        for hf, (y, by) in enumerate(((y0, by0), (y1, by1))):
            sl = slice(hf * 512, (hf + 1) * 512)
            STT(g, fw.dve, xs[:, sl], xs[:, sl], ALPHA, y[:, 0:512], ALU.mult, ALU.add, reads=[bxs, by], writes=[bxs])
        for hf in range(2):
            fw.op(fw.dve, "bn_stats", stt[:, hf, :], xs[:, hf * 512:(hf + 1) * 512], reads=[bxs], writes=[bst])
        fw.op(fw.dve, "bn_aggr", mv[:, 0:2], stt[:], reads=[bst], writes=[bmv])
        ACT(g, mv[:, 2:3], mv[:, 1:2], AF.Ln, reads=[bmv], writes=[bmv], bias=LN_EPS, scale=1.0)
        ACT(g, mv[:, 2:3], mv[:, 2:3], AF.Exp, reads=[bmv], writes=[bmv], scale=-0.5)
        STT(g, fw.dve, mv[:, 3:4], mv[:, 0:1], -1.0, mv[:, 2:3], ALU.mult, ALU.mult, reads=[bmv], writes=[bmv])
        ACT(g, xs[:], xs[:], AF.Identity, reads=[bxs, bmv], writes=[bxs], bias=mv[:, 3:4], scale=mv[:, 2:3])
        TT(g, fw.pool, xs[:], xs[:], self.gam[:], ALU.mult, reads=[bxs, self.bgam], writes=[bxs])
        TT(g, fw.pool, xs[:], xs[:], self.bet[:], ALU.add, reads=[bxs, self.bbet], writes=[bxs])
        fw.dma(fw.sp, dst, xs[:], reads=[bxs], is_output=is_output)


def proj_out_ln(g, st, OTrows, w_src, ln_idx, x_src, x_dst, is_output=False):
    fw = g.fw
    Wo = sb(g, st, "Wo", [128, 8, D], BF16)
    bWo = Buf("Wo")
    load_w(g, Wo, bWo, w_src, D, D, split=2)
    ln = LNUnit(g, st, ln_idx)
    ots = rot_sb(g, st, "ots", [128, 8, 512], BF16, 3)
    xsr = rot_sb(g, st, "pxs", [128, D], F32, 5)
    py = rot_ps(g, st, "py", [128, 512], F32, 4)
    otv = OTrows.rearrange("(p r) t -> r p t", r=128)
    ot_tiles = {}
    x_tiles = {}

    def prefetch(idx):
        if idx >= NT:
            return
        s_, j_ = divmod(idx, 4)
        if j_ == 0:
            ot_tiles[s_] = ots.next()
            fw.dma(fw.sp, ot_tiles[s_][0][:], otv[:, :, s_ * 512:(s_ + 1) * 512], writes=[ot_tiles[s_][1]])
        x_tiles[idx] = xsr.next()
        fw.dma(fw.sp, x_tiles[idx][0][:], x_src[idx * 128:(idx + 1) * 128, :], writes=[x_tiles[idx][1]])

    prefetch(0)
    prefetch(1)
    for idx in range(NT):
        prefetch(idx + 2)
        s_, j = divmod(idx, 4)
        ot, bot = ot_tiles[s_]
        xs, bxs = x_tiles.pop(idx)
        t0 = idx * 128
        ys = []
        for hf in range(2):
            y, by = py.next()
            for p in range(8):
                MM(g, y[:], ot[:, p, j * 128:(j + 1) * 128], Wo[:, p, hf * 512:(hf + 1) * 512], p == 0, p == 7, reads=[bot, bWo], writes=[by])
            ys.append((y, by))
        ln.run(xs, bxs, ys[0][0], ys[0][1], ys[1][0], ys[1][1], x_dst[t0:t0 + 128, :], is_output=is_output)


def attn_bufs(g, st, n_po=4, filler=False):
    A = Ctx()
    A.pJ = ps(g, st, "pJunk", [128, 512], F32) if filler else None
    A.bJ = Buf("pJunk")
    A.pS = rot_ps(g, st, "pS", [128, 512], F32, 3)
    A.pO = rot_ps(g, st, "pO", [65, 512], F32, n_po)
    A.pb = rot_ps(g, st, "pB", [64, 512], F32, 1)
    A.pT = rot_sb(g, st, "pT", [128, 512], BF16, 4)
    A.rd = rot_sb(g, st, "rd", [65, 512], F32, 2)
    A.rh = rot_sb(g, st, "rh", [65, 512], BF16, 2)
    A.rl = rot_sb(g, st, "rl", [65, 512], BF16, 2)
    A.rf = rot_sb(g, st, "rf", [65, 512], F32, 2)
    A.rb = rot_sb(g, st, "rb", [64, 512], F32, 2)
    A.ot = rot_sb(g, st, "ot", [64, 512], BF16, 2)
    A.deferred = []
    return A


def attn_epilogue(g, A, po, bpo, h, qb):
    fw = g.fw
    rd, brd = A.rd.next()
    fw.op(fw.dve, "reciprocal", rd[64:65, :], po[64:65, :], reads=[bpo], writes=[brd])
    rh, brh = A.rh.next()
    rl, brl = A.rl.next()
    rf, brf = A.rf.next()
    CP(g, fw.dve, rh[64:65, :], rd[64:65, :], reads=[brd], writes=[brh])
    CP(g, fw.dve, rf[64:65, :], rh[64:65, :], reads=[brh], writes=[brf])
    TT(g, fw.dve, rl[64:65, :], rd[64:65, :], rf[64:65, :], ALU.subtract, reads=[brd, brf], writes=[brl])
    def part2():
        pb, bpb = A.pb.next()
        MM(g, pb[:], g.ones[64:65, 0:64], rh[64:65, :], True, False, reads=[g.b_ones, brh], writes=[bpb], signal=False)
        MM(g, pb[:], g.ones[64:65, 0:64], rl[64:65, :], False, True, reads=[g.b_ones, brl], writes=[bpb])
        rb, brb = A.rb.next()
        CP(g, fw.dve, rb[:], pb[:], reads=[bpb], writes=[brb])
        ot, bot = A.ot.next()
        TT(g, fw.dve, ot[:], po[0:64, :], rb[:], ALU.mult, reads=[bpo, brb], writes=[bot])
        fw.dma(fw.sp, g.OT[h, :, qb * 512:(qb + 1) * 512], ot[:], reads=[bot])
    A.deferred.append(part2)


def attn_flush(A):
    while A.deferred:
        A.deferred.pop(0)()


def rope_tables(g, st):
    fw = g.fw
    C = sb(g, st, "ropeC", [96, S], F32)[64:96, :]
    Sm = sb(g, st, "ropeS", [96, S], F32)[64:96, :]
    bC, bS = Buf("ropeC"), Buf("ropeS")
    with contextlib.ExitStack() as t:
        posi = sb(g, t, "posi", [96, S], I32)[64:96, :]
        a = sb(g, t, "r_a", [96, S], F32)[64:96, :]
        kf = sb(g, t, "r_k", [96, S], F32)[64:96, :]
        ki = sb(g, t, "r_ki", [96, S], I32)[64:96, :]
        invf = sb(g, t, "invf", [96, 2], F32)[64:96, :]
        bposi, ba, bkf, bki, binvf = Buf(), Buf(), Buf(), Buf(), Buf()
        dve = fw.dve
        fw.dma(fw.sp, posi, g.I.pos[0:1, :].partition_broadcast(32), writes=[bposi])
        fw.dma(fw.sp, invf[:], g.I.c_invf, writes=[binvf])
        CP(g, dve, a, posi, reads=[bposi], writes=[ba])
        TS(g, dve, a, a, invf[:, 0:1], None, ALU.mult, None, reads=[ba, binvf], writes=[ba])
        TS(g, dve, kf, a, 1.0 / (2 * math.pi), 0.5, ALU.mult, ALU.add, reads=[ba], writes=[bkf])
        CP(g, dve, ki, kf, reads=[bkf], writes=[bki])
        CP(g, dve, kf, ki, reads=[bki], writes=[bkf])
        C1 = 6.28125
        C2 = 2 * math.pi - C1
        STT(g, dve, a, kf, -C1, a, ALU.mult, ALU.add, reads=[bkf, ba], writes=[ba])
        STT(g, dve, a, kf, -C2, a, ALU.mult, ALU.add, reads=[bkf, ba], writes=[ba])
        TS(g, dve, kf, a, -math.pi, 2 * math.pi, ALU.is_lt, ALU.mult, reads=[ba], writes=[bkf])
        TT(g, dve, a, a, kf, ALU.add, reads=[ba, bkf], writes=[ba])
        TS(g, dve, kf, a, math.pi, -2 * math.pi, ALU.is_gt, ALU.mult, reads=[ba], writes=[bkf])
        TT(g, dve, a, a, kf, ALU.add, reads=[ba, bkf], writes=[ba])
        TS(g, dve, a, a, -3.1415925, 3.1415925, ALU.max, ALU.min, reads=[ba], writes=[ba])
        ACT(g, Sm, a, AF.Sin, reads=[ba], writes=[bS])
        STT(g, dve, kf, a, -1.0, a, ALU.mult, ALU.max, reads=[ba], writes=[bkf])
        TS(g, dve, kf, kf, -1.0, math.pi / 2, ALU.mult, ALU.add, reads=[bkf], writes=[bkf])
        ACT(g, C, kf, AF.Sin, reads=[bkf], writes=[bC])
        TS(g, dve, Sm, Sm, invf[:, 1:2], None, ALU.mult, None, reads=[bS, binvf], writes=[bS])
        end_phase(g)
    return C, bC, Sm, bS


def l0_p1(g):
    fw = g.fw
    with contextlib.ExitStack() as st:
        C, bC, Sm, bS = rope_tables(g, st)
        Win = sb(g, st, "Win", [128, 8, 832], BF16)
        Wuq = sb(g, st, "Wuq", [128, 4, 3072], BF16)
        Wukv = sb(g, st, "Wukv", [128, 2, 2048], BF16)
        bWin, bWuq, bWukv = Buf(), Buf(), Buf()
        load_w(g, Win, bWin, g.I.mla_in, D, 832)
        load_w(g, Wuq, bWuq, g.I.mla_uq, 512, 3072)
        load_w(g, Wukv, bWukv, g.I.mla_ukv, 256, 2048)
        gq = sb(g, st, "gq", [128, 512], F32)
        gkv = sb(g, st, "gkv", [128, 256], F32)
        bgq, bgkv = Buf(), Buf()
        fw.dma(fw.sp, gq[:], g.I.mla_qn[0:1, :].partition_broadcast(128), writes=[bgq])
        fw.dma(fw.sp, gkv[:], g.I.mla_kvn[0:1, :].partition_broadcast(128), writes=[bgkv])
        xl = XLoader(g, st, n_xs=3, n_xT=2)
        ph = rot_ps(g, st, "ph", [128, 512], F32, 2)
        pv = rot_ps(g, st, "pv", [128, 512], F32, 2)
        pq = rot_ps(g, st, "pq", [128, 512], F32, 2)
        nrm = rot_sb(g, st, "nrm", [128, 768], BF16, 2)
        junk = rot_sb(g, st, "junk", [128, 512], BF16, 2)
        ss = rot_sb(g, st, "ss", [128, 4], F32, 4)
        nT = Rot([(sb(g, st, f"nT{i}", [128, 6, 512], BF16), [Buf() for j in range(4)]) for i in range(2)])
        vsb = rot_sb(g, st, "vsb", [128, 16, 65], BF16, 2)
        for (t, b) in vsb.items:
            fw.op(fw.pool, "memset", t[:], 1.0, writes=[b])
        t1 = rot_sb(g, st, "t1", [96, 512], F32, 2)
        t2 = rot_sb(g, st, "t2", [96, 512], F32, 2)
        qt = rot_sb(g, st, "qt", [96, 512], BF16, 3)
        kt = rot_sb(g, st, "kt", [128, 512], BF16, 2)
        kr = rot_sb(g, st, "kr", [96, 512], BF16, 2)
        qscale = 96 ** -0.5
        KTv = g.KT.rearrange("h d t -> (h d) t")
        pre = xl.load(g.I.x, 0)
        for s in range(NS):
            xT, bxT, tiles = pre
            nt, bnt = nT.next()
            cols = slice(s * 512, (s + 1) * 512)
            for j in range(4):
                tsl = slice(j * 128, (j + 1) * 128)
                hq, bhq = ph.next()
                hkv, bhkv = ph.next()
                for c in range(8):
                    MM(g, hq[:], xT[:, c, tsl], Win[:, c, 0:512], c == 0, c == 7, reads=[bxT[j], bWin], writes=[bhq])
                for c in range(8):
                    MM(g, hkv[:, 0:256], xT[:, c, tsl], Win[:, c, 512:768], c == 0, c == 7, reads=[bxT[j], bWin], writes=[bhkv])
                sq, bsq = ss.next()
                jk, bjk = junk.next()
                ACT(g, jk[:], hq[:], AF.Square, reads=[bhq], writes=[bjk, bsq], accum_out=sq[:, 0:1])
                ACT(g, jk[:, 0:256], hkv[:, 0:256], AF.Square, reads=[bhkv], writes=[bjk, bsq], accum_out=sq[:, 1:2])
                ACT(g, sq[:, 2:3], sq[:, 0:1], AF.Ln, reads=[bsq], writes=[bsq], bias=RMS_EPS, scale=1.0 / 512)
                ACT(g, sq[:, 3:4], sq[:, 1:2], AF.Ln, reads=[bsq], writes=[bsq], bias=RMS_EPS, scale=1.0 / 256)
                ACT(g, sq[:, 2:4], sq[:, 2:4], AF.Exp, reads=[bsq], writes=[bsq], scale=-0.5)
                nr, bnr = nrm.next()
                STT(g, fw.dve, nr[:, 0:512], hq[:], sq[:, 2:3], gq[:], ALU.mult, ALU.mult, reads=[bhq, bsq, bgq], writes=[bnr])
                STT(g, fw.dve, nr[:, 512:768], hkv[:, 0:256], sq[:, 3:4], gkv[:], ALU.mult, ALU.mult, reads=[bhkv, bsq, bgkv], writes=[bnr])
                transpose_into(g, nr, bnr, None, xl.tp, nt, bnt[j], j, 6, src_is_bf16=True)
                vt, bvt = vsb.next()
                for hf in range(2):
                    p_, bp_ = pv.next()
                    for r in range(2):
                        MM(g, p_[:], nt[:, 4 + r, tsl], Wukv[:, r, 1024 + hf * 512:1024 + (hf + 1) * 512], r == 0, r == 1, reads=[bnt[j], bWukv], writes=[bp_])
                    CP(g, fw.act, vt[:, hf * 8:(hf + 1) * 8, 0:64], p_[:].rearrange("p (a b) -> p a b", b=64), reads=[bp_], writes=[bvt])
                t0 = s * 512 + j * 128
                fw.dma(fw.sp, g.V[t0:t0 + 128, :, :], vt[:], reads=[bvt])
            if s + 1 < NS:
                pre = xl.load(g.I.x, s + 1)
            pa, bpa = pq.next()
            pb_, bpb_ = pq.next()
            for c in range(8):
                MM(g, pa[0:96, :], Win[:, c, 704:800], xT[:, c, :], c == 0, c == 7, reads=bxT + [bWin], writes=[bpa])
            for c in range(8):
                MM(g, pb_[0:96, :], Win[:, c, 736:832], xT[:, c, :], c == 0, c == 7, reads=bxT + [bWin], writes=[bpb_])
            a1, ba1 = t1.next()
            a2, ba2 = t2.next()
            TT(g, fw.dve, a1[64:96, :], pa[64:96, :], C[:, cols], ALU.mult, reads=[bpa, bC], writes=[ba1])
            TT(g, fw.dve, a2[64:96, :], pb_[64:96, :], Sm[:, cols], ALU.mult, reads=[bpb_, bS], writes=[ba2])
            krt, bkrt = kr.next()
            TT(g, fw.pool, krt[64:96, :], a1[64:96, :], a2[64:96, :], ALU.add, reads=[ba1, ba2], writes=[bkrt])
            fw.dma(fw.sp, g.KR[:, cols], krt[64:96, :], reads=[bkrt])
            for hd in range(16):
                pm, bpm = pq.next()
                psw, bpsw = pq.next()
                for r in range(4):
                    MM(g, pm[0:96, :], Wuq[:, r, hd * 96:(hd + 1) * 96], nt[:, r, :], r == 0, r == 3, reads=bnt + [bWuq], writes=[bpm])
                for r in range(4):
                    MM(g, psw[0:96, :], Wuq[:, r, 1536 + hd * 96:1536 + (hd + 1) * 96], nt[:, r, :], r == 0, r == 3, reads=bnt + [bWuq], writes=[bpsw])
                q_, bq_ = qt.next()
                a1, ba1 = t1.next()
                a2, ba2 = t2.next()
                ACT(g, q_[0:64, :], pm[0:64, :], AF.Copy, reads=[bpm], writes=[bq_], scale=qscale)
                STT(g, fw.dve, a1[64:96, :], pm[64:96, :], qscale, C[:, cols], ALU.mult, ALU.mult, reads=[bpm, bC], writes=[ba1])
                STT(g, fw.dve, a2[64:96, :], psw[64:96, :], qscale, Sm[:, cols], ALU.mult, ALU.mult, reads=[bpsw, bS], writes=[ba2])
                TT(g, fw.pool, q_[64:96, :], a1[64:96, :], a2[64:96, :], ALU.add, reads=[ba1, ba2], writes=[bq_])
                fw.dma(fw.sp, g.QT[hd, :, cols], q_[:], reads=[bq_])
            for p in range(8):
                pk, bpk = pq.next()
                for r in range(2):
                    MM(g, pk[:], Wukv[:, r, p * 128:(p + 1) * 128], nt[:, 4 + r, :], r == 0, r == 1, reads=bnt + [bWukv], writes=[bpk])
                k_, bk_ = kt.next()
                CP(g, fw.act, k_[:], pk[:], reads=[bpk], writes=[bk_])
                fw.dma(fw.sp, KTv[p * 128:(p + 1) * 128, cols], k_[:], reads=[bk_])
        end_phase(g)


def attn_qblock(g, A, steps, S_fn, V_fn, po, bpo):
    fw = g.fw
    n = len(steps)
    pend = []

    def do_pv(item):
        (i, kc, off), pT, bpT = item
        lhsT, vreads = V_fn(kc)
        MM(g, po[:, off:512], lhsT, pT[:, off:512], i == 0, i == n - 1, reads=[bpT] + vreads, writes=[bpo])

    for i, (kc, off) in enumerate(steps):
        pS, bpS = A.pS.next()
        S_fn(kc, off, pS, bpS)
        if A.pJ is not None:
            A.filler_fn(kc)
        pT, bpT = A.pT.next()
        ACT(g, pT[:, off:512], pS[:, off:512], AF.Exp, reads=[bpS], writes=[bpT])
        pend.append(((i, kc, off), pT, bpT))
        if len(pend) > 2:
            do_pv(pend.pop(0))
        if i == min(3, n - 1):
            attn_flush(A)
    while pend:
        do_pv(pend.pop(0))


def l0_p2(g):
    fw = g.fw
    with contextlib.ExitStack() as st:
        A = attn_bufs(g, st, n_po=4, filler=False)
        Kh = rot_sb(g, st, "Kh", [96, S], BF16, 2)
        Qh = rot_sb(g, st, "Qh", [96, S], BF16, 2)
        Vall = sb(g, st, "Vall", [128, NT, 16 * 65], BF16)
        bVall = Buf("Vall")
        Vv = g.V.rearrange("(c p) h e -> p c (h e)", p=128)
        for c0 in range(0, NT, 8):
            fw.dma(fw.sp, Vall[:, c0:c0 + 8, :], Vv[:, c0:c0 + 8, :], writes=[bVall])
        Kbufs = {}
        for hd in range(16):
            K_, bK = Kh.next()
            Q_, bQ = Qh.next()
            bKr = Kbufs.setdefault(id(K_), Buf("Kr"))
            fw.dma(fw.sp, K_[64:96, :], g.KR[:, :], writes=[bKr])
            fw.dma(fw.sp, K_[0:64, :], g.KT[hd, :, :], writes=[bK])
            fw.dma(fw.sp, Q_[:], g.QT[hd, :, :], writes=[bQ])
            for qb in range(NS):
                po, bpo = A.pO.next()
                steps = []
                for kc in range(4 * qb + 4):
                    r = kc - 4 * qb
                    steps.append((kc, 128 * r if r > 0 else 0))

                def S_fn(kc, off, pS, bpS, K_=K_, Q_=Q_, bK=bK, bKr=bKr, bQ=bQ, qb=qb):
                    r = kc - 4 * qb
                    ks = slice(kc * 128, (kc + 1) * 128)
                    q0 = qb * 512
                    if r < 0:
                        MM(g, pS[:, 0:512], K_[:, ks], Q_[:, q0:q0 + 512], True, True, reads=[bK, bKr, bQ], writes=[bpS])
                    else:
                        MM(g, pS[:, off:off + 128], K_[:, ks], Q_[:, q0 + off:q0 + off + 128], True, False, reads=[bK, bKr, bQ], writes=[bpS], signal=False)
                        last = off + 128 >= 512
                        MM(g, pS[:, off:off + 128], g.ident[:], g.tri[:], False, True, reads=[g.b_ident, g.b_tri], writes=[bpS], signal=last)
                        if not last:
                            MM(g, pS[:, off + 128:512], K_[:, ks], Q_[:, q0 + off + 128:q0 + 512], True, True, reads=[bK, bKr, bQ], writes=[bpS])

                def V_fn(kc, hd=hd):
                    return Vall[:, kc, hd * 65:(hd + 1) * 65], [bVall]

                def filler_fn(kc, K_=K_, Q_=Q_, bK=bK, bKr=bKr, bQ=bQ, qb=qb):
                    MM(g, A.pJ[:, 0:384], K_[:, kc * 128:(kc + 1) * 128], Q_[:, qb * 512:qb * 512 + 384], True, True,
                       reads=[bK, bKr, bQ], writes=[A.bJ], signal=False)
                A.filler_fn = filler_fn

                attn_qblock(g, A, steps, S_fn, V_fn, po, bpo)
                attn_epilogue(g, A, po, bpo, hd, qb)
        attn_flush(A)
        end_phase(g)


def layer0(g):
    l0_p1(g)
    if "stop_p1" in g.debug:
        return
    l0_p2(g)
    if "stop_p2" in g.debug:
        return
    with contextlib.ExitStack() as st:
        proj_out_ln(g, st, g.OT.rearrange("h d t -> (h d) t"), g.I.mla_wo, 0, g.I.x, g.xa)
        end_phase(g)
    if "stop_p3" in g.debug:
        return
    cross_attn(g, 0, g.xa, g.xb)
    if "stop_p4" in g.debug:
        return
    ffn(g, 0, g.xb, g.xc, is_output=False)


def cross_attn(g, li, x_src, x_dst):
    fw = g.fw
    with contextlib.ExitStack() as st:
        Wq = sb(g, st, "Wq", [128, 8, D], BF16)
        Wo = sb(g, st, "Wo", [128, 8, D], BF16)
        bWq, bWo = Buf(), Buf()
        KmT = sb(g, st, "KmT", [128, 4, 2, 256], BF16)
        Vm = sb(g, st, "Vm", [128, 2, 4, 257], BF16)
        bKmT, bVm = Buf(), Buf()
        xl = XLoader(g, st, n_xs=9, n_xT=2)
        pA = rot_ps(g, st, "pA", [128, 512], F32, 4)
        py = rot_ps(g, st, "py", [128, 512], F32, 2)
        with contextlib.ExitStack() as t:
            Wkv = sb(g, t, "Wkv", [128, 8, 2 * D], BF16)
            bWkv = Buf()
            load_w(g, Wkv, bWkv, g.I.xa_wkv[li], D, 2 * D, split=4)
            load_w(g, Wq, bWq, g.I.xa_wq[li], D, D, split=2)
            load_w(g, Wo, bWo, g.I.xa_wo[li], D, D, split=2)
            mt = rot_sb(g, t, "memt", [128, D], F32, 2)
            memT = sb(g, t, "memT", [128, 8, 256], BF16)
            bmemT = Buf()
            fw.op(fw.pool, "memset", Vm[:], 1.0, writes=[bVm])
            for mc in range(2):
                m_, bm_ = mt.next()
                fw.dma(fw.sp, m_[:], g.I.mem[mc * 128:(mc + 1) * 128, :], writes=[bm_])
                transpose_into(g, m_, bm_, xl.xbf, xl.tp, memT, bmemT, mc, 8)
            for hd in range(4):
                for e in range(2):
                    p_, bp_ = pA.next()
                    c0 = hd * 256 + e * 128
                    for c in range(8):
                        MM(g, p_[:, 0:256], Wkv[:, c, c0:c0 + 128], memT[:, c, :], c == 0, c == 7, reads=[bWkv, bmemT], writes=[bp_])
                    CP(g, fw.act, KmT[:, hd, e, :], p_[:, 0:256], reads=[bp_], writes=[bKmT])
            for mc in range(2):
                for hf in range(2):
                    p_, bp_ = pA.next()
                    for c in range(8):
                        MM(g, p_[:], memT[:, c, mc * 128:(mc + 1) * 128], Wkv[:, c, D + hf * 512:D + (hf + 1) * 512], c == 0, c == 7, reads=[bWkv, bmemT], writes=[bp_])
                    CP(g, fw.dve, Vm[:, mc, 2 * hf:2 * hf + 2, 0:256], p_[:].rearrange("p (a b) -> p a b", b=256), reads=[bp_], writes=[bVm])
            end_phase(g)
        ln = LNUnit(g, st, li * 3 + 1)
        qT = sb(g, st, "qT", [128, 8, 512], BF16)
        bqT = Buf()
        pTs = [(sb(g, st, f"xpT{i}", [128, 512], BF16), Buf()) for i in range(8)]
        o_r = rot_sb(g, st, "xo", [128, D], BF16, 2)
        rdn = rot_sb(g, st, "xrd", [128, 4], F32, 2)
        oT = rot_sb(g, st, "xoT", [128, 8, 128], BF16, 2)
        pre = xl.load(x_src, 0)
        for s in range(NS):
            xT, bxT, tiles = pre
            for cc in range(8):
                p_, bp_ = pA.next()
                for c in range(8):
                    MM(g, p_[:], Wq[:, c, cc * 128:(cc + 1) * 128], xT[:, c, :], c == 0, c == 7, reads=[bWq] + bxT, writes=[bp_])
                ACT(g, qT[:, cc, :], p_[:], AF.Copy, reads=[bp_], writes=[bqT], scale=1.0 / 16)
            for hd in range(4):
                for mc in range(2):
                    p_, bp_ = pA.next()
                    for e in range(2):
                        MM(g, p_[:], KmT[:, hd, e, mc * 128:(mc + 1) * 128], qT[:, 2 * hd + e, :], e == 0, e == 1, reads=[bKmT, bqT], writes=[bp_])
                    pT, bpT = pTs[hd * 2 + mc]
                    ACT(g, pT[:], p_[:], AF.Exp, reads=[bp_], writes=[bpT])
            if s + 1 < NS:
                pre = xl.load(x_src, s + 1)
            for j in range(4):
                xs, bxs = tiles[j]
                o_, bo_ = o_r.next()
                rd, brd = rdn.next()
                for hd in range(4):
                    p_, bp_ = pA.next()
                    for mc in range(2):
                        pT, bpT = pTs[hd * 2 + mc]
                        MM(g, p_[:, 0:257], pT[:, j * 128:(j + 1) * 128], Vm[:, mc, hd, :], mc == 0, mc == 1, reads=[bpT, bVm], writes=[bp_])
                    fw.op(fw.dve, "reciprocal", rd[:, hd:hd + 1], p_[:, 256:257], reads=[bp_], writes=[brd])
                    TS(g, fw.dve, o_[:, hd * 256:(hd + 1) * 256], p_[:, 0:256], rd[:, hd:hd + 1], None, ALU.mult, None, reads=[bp_, brd], writes=[bo_])
                ot, bot = oT.next()
                transpose_into(g, o_, bo_, None, xl.tp, ot, bot, 0, 8, src_is_bf16=True)
                ys = []
                for hf in range(2):
                    y, by = py.next()
                    for c in range(8):
                        MM(g, y[:], ot[:, c, :], Wo[:, c, hf * 512:(hf + 1) * 512], c == 0, c == 7, reads=[bot, bWo], writes=[by])
                    ys.append((y, by))
                t0 = s * 512 + j * 128
                ln.run(xs, bxs, ys[0][0], ys[0][1], ys[1][0], ys[1][1], x_dst[t0:t0 + 128, :])
        end_phase(g)


def ffn(g, li, x_src, x_dst, is_output):
    fw = g.fw
    NF = FFN_H // 128
    with contextlib.ExitStack() as st:
        W1 = sb(g, st, "W1", [128, 8, 2 * FFN_H], BF16)
        W2 = sb(g, st, "W2", [128, NF, D], BF16)
        bW1, bW2 = Buf(), Buf()
        load_w(g, W1, bW1, g.I.ffn_in[li], D, 2 * FFN_H, split=8)
        load_w(g, W2, bW2, g.I.ffn_dn[li], FFN_H, D, split=2)
        ln = LNUnit(g, st, li * 3 + 2)
        xl = XLoader(g, st, n_xs=4, n_xT=1)
        pA = rot_ps(g, st, "pA", [128, 512], F32, 4)
        py = rot_ps(g, st, "py", [128, 512], F32, 2)
        aT = sb(g, st, "aT", [128, NF, 512], BF16)
        baT = Buf()
        sg = rot_sb(g, st, "sg", [128, 512], F32, 2)
        for s in range(NS):
            xT, bxT, tiles = xl.load(x_src, s)
            for fc in range(NF):
                pg, bpg = pA.next()
                pu, bpu = pA.next()
                for c in range(8):
                    MM(g, pg[:], W1[:, c, fc * 128:(fc + 1) * 128], xT[:, c, :], c == 0, c == 7, reads=[bW1] + bxT, writes=[bpg])
                for c in range(8):
                    MM(g, pu[:], W1[:, c, FFN_H + fc * 128:FFN_H + (fc + 1) * 128], xT[:, c, :], c == 0, c == 7, reads=[bW1] + bxT, writes=[bpu])
                s_, bs_ = sg.next()
                ACT(g, s_[:], pg[:], AF.Silu, reads=[bpg], writes=[bs_])
                TT(g, fw.dve, aT[:, fc, :], s_[:], pu[:], ALU.mult, reads=[bs_, bpu], writes=[baT])
            for j in range(4):
                xs, bxs = tiles[j]
                ys = []
                for hf in range(2):
                    y, by = py.next()
                    for fc in range(NF):
                        MM(g, y[:], aT[:, fc, j * 128:(j + 1) * 128], W2[:, fc, hf * 512:(hf + 1) * 512], fc == 0, fc == NF - 1, reads=[baT, bW2], writes=[by])
                    ys.append((y, by))
                t0 = s * 512 + j * 128
                ln.run(xs, bxs, ys[0][0], ys[0][1], ys[1][0], ys[1][1], x_dst[t0:t0 + 128, :], is_output=is_output)
        end_phase(g)


def l1_p1(g, x_src):
    fw = g.fw
    with contextlib.ExitStack() as st:
        Win = sb(g, st, "Win1", [128, 8, 2120], BF16)
        bWin = Buf()
        load_w(g, Win, bWin, g.I.dsa_in, D, 2120, split=2)
        xl = XLoader(g, st, n_xs=3, n_xT=2)
        pq = rot_ps(g, st, "pq", [128, 512], F32, 4)
        pv = rot_ps(g, st, "pv", [128, 512], F32, 2)
        ev = rot_sb(g, st, "ev", [128, 512], BF16, 4)
        vsb = rot_sb(g, st, "vsb1", [128, 4, 65], BF16, 2)
        for (t, b) in vsb.items:
            fw.op(fw.pool, "memset", t[:], 1.0, writes=[b])
        wsb = rot_sb(g, st, "wsb", [128, 8], F32, 2)
        pre = xl.load(x_src, 0)
        for s in range(NS):
            xT, bxT, tiles = pre
            cols = slice(s * 512, (s + 1) * 512)
            for j in range(4):
                tsl = slice(j * 128, (j + 1) * 128)
                t0 = s * 512 + j * 128
                p_, bp_ = pv.next()
                for c in range(8):
                    MM(g, p_[:, 0:256], xT[:, c, tsl], Win[:, c, 1280:1536], c == 0, c == 7, reads=[bxT[j], bWin], writes=[bp_])
                vt, bvt = vsb.next()
                CP(g, fw.act, vt[:, :, 0:64], p_[:, 0:256].rearrange("p (a b) -> p a b", b=64), reads=[bp_], writes=[bvt])
                fw.dma(fw.sp, g.V1[t0:t0 + 128, :, :], vt[:], reads=[bvt])
                p_, bp_ = pv.next()
                for c in range(8):
                    MM(g, p_[:, 0:8], xT[:, c, tsl], Win[:, c, 2112:2120], c == 0, c == 7, reads=[bxT[j], bWin], writes=[bp_])
                wt, bwt = wsb.next()
                ACT(g, wt[:], p_[:, 0:8], AF.Copy, reads=[bp_], writes=[bwt], scale=8 ** -0.5)
                fw.dma(fw.sp, g.WI[t0:t0 + 128, :], wt[:], reads=[bwt])
            if s + 1 < NS:
                pre = xl.load(x_src, s + 1)
            jobs = [(0, 8, 0.125, g.QT1), (1024, 2, 1.0, g.KT1), (1536, 4, 0.125, g.QI)]
            for (c0, nch, sc, dst) in jobs:
                for p in range(nch):
                    p_, bp_ = pq.next()
                    for c in range(8):
                        MM(g, p_[:], Win[:, c, c0 + p * 128:c0 + (p + 1) * 128], xT[:, c, :], c == 0, c == 7, reads=bxT + [bWin], writes=[bp_])
                    e_, be_ = ev.next()
                    ACT(g, e_[:], p_[:], AF.Copy, reads=[bp_], writes=[be_], scale=sc)
                    fw.dma(fw.sp, dst[p * 128:(p + 1) * 128, cols], e_[:], reads=[be_])
            p_, bp_ = pq.next()
            for c in range(8):
                MM(g, p_[0:64, :], Win[:, c, 2048:2112], xT[:, c, :], c == 0, c == 7, reads=bxT + [bWin], writes=[bp_])
            e_, be_ = ev.next()
            ACT(g, e_[0:64, :], p_[0:64, :], AF.Copy, reads=[bp_], writes=[be_])
            fw.dma(fw.sp, g.KI[:, cols], e_[0:64, :], reads=[be_])
        end_phase(g)


def l1_p2(g):
    fw = g.fw
    NB = N_BISECT
    with contextlib.ExitStack() as st:
        KI2 = sb(g, st, "KI2", [128, S], BF16)
        bKI2 = Buf()
        fw.dma(fw.sp, KI2[0:64, :], g.KI[:, :], writes=[bKI2])
        fw.dma(fw.sp, KI2[64:128, :], g.KI[:, :], writes=[bKI2])
        trif = sb(g, st, "trif", [128, 128], F32)
        pw2 = sb(g, st, "pw2", [128, 32], F32)
        zt = sb(g, st, "zt", [128, 128], BF16)
        btrif, bpw2, bzt = Buf(), Buf(), Buf()
        fw.dma(fw.sp, trif[:], g.I.c_trif[:, :], writes=[btrif])
        fw.dma(fw.sp, pw2[:], g.I.c_pow2[0:1, :].partition_broadcast(128), writes=[bpw2])
        fw.op(fw.dve, "memset", zt[:], 0.0, writes=[bzt])
        MBv = g.MBT.rearrange("c k q -> k c q")
        fw.dma(fw.sp, g.MBT[0, :, 0:128], g.tri[:], reads=[g.b_tri])
        fw.dma(fw.sp, g.MBT[0, :, 128:256], zt[:], reads=[bzt])
        fw.dma(fw.sp, g.MBT[1, :, 128:256], g.tri[:], reads=[g.b_tri])

        qi_r = rot_sb(g, st, "qi", [128, 4, 128], BF16, 2)
        wt_r = rot_sb(g, st, "wt", [128, 8], F32, 2)
        dg_r = rot_sb(g, st, "dg", [128, 8, 128], BF16, 2)
        sc_r = rot_sb(g, st, "score", [128, S], F32, 2)
        rl_r = rot_sb(g, st, "rl", [128, 512], BF16, 4)
        junk = sb(g, st, "bjunk", [128, S], BF16)
        bjunk = Buf()
        sm_r = rot_sb(g, st, "bsm", [128, 40], F32, 2)
        mb_r = rot_sb(g, st, "mb", [128, S], BF16, 2)
        mbT_r = rot_sb(g, st, "mbT", [128, NT, 128], BF16, 2)
        pD = rot_ps(g, st, "pD", [128, 512], F32, 4)
        pSc = rot_ps(g, st, "pSc", [128, 512], F32, 2)
        ptp = rot_ps(g, st, "ptp", [128, 8, 128], BF16, 2)
        QIv = g.QI.rearrange("(ip r) t -> r ip t", r=128)

        def score_tile(i):
            nk = (i + 1) * 128
            qi, bqi = qi_r.next()
            wt, bwt = wt_r.next()
            dg, bdg = dg_r.next()
            sc, bsc = sc_r.next()
            fw.dma(fw.sp, qi[:], QIv[:, :, i * 128:(i + 1) * 128], writes=[bqi])
            fw.dma(fw.sp, wt[:], g.WI[i * 128:(i + 1) * 128, :], writes=[bwt])
            for h in range(8):
                ACT(g, dg[:, h, :], g.ident[:], AF.Copy, reads=[g.b_ident, bwt], writes=[bdg], scale=wt[:, h:h + 1])
            for kb in range((nk + 511) // 512):
                w = min(512, nk - kb * 512)
                ps_, bps_ = pSc.next()
                for h in range(8):
                    half = slice((h % 2) * 64, (h % 2) * 64 + 64)
                    pd, bpd = pD.next()
                    MM(g, pd[:, 0:w], qi[half, h // 2, :], KI2[half, kb * 512:kb * 512 + w], True, True, reads=[bqi, bKI2], writes=[bpd])
                    rl, brl = rl_r.next()
                    ACT(g, rl[:, 0:w], pd[:, 0:w], AF.Relu, reads=[bpd], writes=[brl])
                    MM(g, ps_[:, 0:w], dg[:, h, :], rl[:, 0:w], h == 0, h == 7, reads=[bdg, brl], writes=[bps_])
                CP(g, fw.act, sc[:, kb * 512:kb * 512 + w], ps_[:, 0:w], reads=[bps_], writes=[bsc])
            return sc, bsc

        def bisect_tile(i, sc, bsc):
            nk = (i + 1) * 128
            sm, bsm = sm_r.next()
            dve = fw.dve
            TT(g, dve, sc[:, i * 128:nk], sc[:, i * 128:nk], trif[:], ALU.add, reads=[bsc, btrif], writes=[bsc])
            TS(g, dve, junk[:, 0:nk], sc[:, 0:nk], 1.0, -3.0e38, ALU.mult, ALU.max, reads=[bsc], writes=[bjunk, bsm], accum_out=sm[:, 0:1])
            TS(g, dve, junk[:, 0:i * 128], sc[:, 0:i * 128], -1.0, -3.0e38, ALU.mult, ALU.max, reads=[bsc], writes=[bjunk, bsm], accum_out=sm[:, 1:2])
            TS(g, dve, sm[:, 1:2], sm[:, 1:2], -1.0, None, ALU.mult, None, reads=[bsm], writes=[bsm])
            TT(g, dve, sm[:, 2:3], sm[:, 0:1], sm[:, 1:2], ALU.subtract, reads=[bsm], writes=[bsm])
            TS(g, dve, sm[:, 8:40], pw2[:], sm[:, 2:3], None, ALU.mult, None, reads=[bsm, bpw2], writes=[bsm])
            TT(g, dve, sm[:, 3:4], sm[:, 1:2], sm[:, 8:9], ALU.add, reads=[bsm], writes=[bsm])
            for n in range(NB):
                TS(g, dve, junk[:, 0:nk], sc[:, 0:nk], sm[:, 3:4], 0.0, ALU.is_ge, ALU.add, reads=[bsc, bsm], writes=[bjunk, bsm], accum_out=sm[:, 4:5])
                TS(g, dve, sm[:, 5:6], sm[:, 4:5], float(TOPK), 0.5, ALU.is_ge, ALU.subtract, reads=[bsm], writes=[bsm])
                STT(g, dve, sm[:, 3:4], sm[:, 5:6], sm[:, 8 + n:9 + n], sm[:, 3:4], ALU.mult, ALU.add, reads=[bsm], writes=[bsm])
            TT(g, dve, sm[:, 6:7], sm[:, 3:4], sm[:, 8 + NB:9 + NB], ALU.subtract, reads=[bsm], writes=[bsm])
            return sm, bsm

        def mask_tile(i, sc, bsc, sm, bsm):
            nk = (i + 1) * 128
            mb, bmb = mb_r.next()
            TS(g, fw.dve, mb[:, 0:nk], sc[:, 0:nk], sm[:, 6:7], NEG, ALU.is_lt, ALU.mult, reads=[bsc, bsm], writes=[bmb])
            mbT, bmbT = mbT_r.next()
            for c0 in range(0, i + 1, 8):
                n_ = min(8, i + 1 - c0)
                tp, btp = ptp.next()
                for c in range(n_):
                    fw.op(fw.pe, "transpose", tp[:, c, :], mb[:, (c0 + c) * 128:(c0 + c + 1) * 128], g.ident[:],
                          reads=[bmb, g.b_ident], writes=[btp], signal=(c == n_ - 1))
                CP(g, fw.act, mbT[:, c0:c0 + n_, :], tp[:, 0:n_, :], reads=[btp], writes=[bmbT])
            fw.dma(fw.sp, MBv[:, 0:i + 1, i * 128:(i + 1) * 128], mbT[:, 0:i + 1, :], reads=[bmbT])

        cur = score_tile(2)
        for i in range(2, NT):
            nxt = score_tile(i + 1) if i + 1 < NT else None
            sm, bsm = bisect_tile(i, cur[0], cur[1])
            mask_tile(i, cur[0], cur[1], sm, bsm)
            cur = nxt
        end_phase(g)


def l1_p3(g):
    fw = g.fw
    with contextlib.ExitStack() as st:
        A = attn_bufs(g, st)
        Bt = sb(g, st, "Bt", [128, 16, 2, 128], BF16)
        bBt = Buf()
        J = sb(g, st, "antiI", [128, 128], BF16)
        bJ = Buf()
        fw.dma(fw.pool, J[:], g.I.c_antiident[:, :], writes=[bJ])
        with contextlib.ExitStack() as t:
            rbf = sb(g, t, "rbf", [32, 16], F32)
            rbh = sb(g, t, "rbh", [32, 16], BF16)
            rbhf = sb(g, t, "rbhf", [32, 16], F32)
            rbl = sb(g, t, "rbl", [32, 16], BF16)
            oh = sb(g, t, "oh", [32, 384], BF16)
            tb = sb(g, t, "tb", [16, 384], F32)
            b1, b2, b3, b4, b5, b6 = Buf(), Buf(), Buf(), Buf(), Buf(), Buf()
            fw.dma(fw.sp, rbf[:], g.I.rel_bias[:, :], writes=[b1])
            fw.dma(fw.pool, oh[:], g.I.c_oh[:, :], writes=[b5])
            CP(g, fw.dve, rbh[:], rbf[:], reads=[b1], writes=[b2])
            CP(g, fw.dve, rbhf[:], rbh[:], reads=[b2], writes=[b3])
            TT(g, fw.dve, rbl[:], rbf[:], rbhf[:], ALU.subtract, reads=[b1, b3], writes=[b4])
            pt, bpt = A.pS.next()
            MM(g, pt[0:16, 0:384], rbh[:], oh[:], True, False, reads=[b2, b5], writes=[bpt], signal=False)
            MM(g, pt[0:16, 0:384], rbl[:], oh[:], False, True, reads=[b4, b5], writes=[bpt])
            CP(g, fw.dve, tb[:], pt[0:16, 0:384], reads=[bpt], writes=[b6])
            TS(g, fw.dve, tb[:], tb[:], tb[:, 383:384], None, ALU.subtract, None, reads=[b6], writes=[b6])
            fw.dma(fw.sp, g.TB[:, :], tb[:], reads=[b6], writes=[g.b_TB])
            for hd in range(16):
                for v, base in ((0, 1), (1, 129)):
                    src = bass.AP(tensor=g.TB.tensor, offset=hd * 384 + base, ap=[[1, 128], [1, 128]])
                    fw.dma(fw.pool, Bt[:, hd, v, :], src, reads=[g.b_TB], writes=[bBt])
            end_phase(g)
        KT2 = sb(g, st, "KT2", [128, 4, S], BF16)
        bKT2 = Buf()
        for n in range(4):
            fw.dma(fw.sp, KT2[0:64, n, :], g.KT1[n * 64:(n + 1) * 64, :], writes=[bKT2])
            fw.dma(fw.sp, KT2[64:128, n, :], g.KT1[n * 64:(n + 1) * 64, :], writes=[bKT2])
        V1 = sb(g, st, "V1all", [128, NT, 4 * 65], BF16)
        bV1 = Buf()
        Vv = g.V1.rearrange("(c p) h e -> p c (h e)", p=128)
        fw.dma(fw.sp, V1[:], Vv[:, :, :], writes=[bV1])
        Qe_r = rot_sb(g, st, "Qe", [128, 8, 512], BF16, 2)
        Qo_r = rot_sb(g, st, "Qo", [128, 8, 512], BF16, 2)
        for (t_, b_) in Qe_r.items + Qo_r.items:
            fw.op(fw.pool, "memset", t_[:], 0.0, writes=[b_])
        mb_r = rot_sb(g, st, "mbq", [128, NT, 512], BF16, 2)
        QTv = g.QT1.rearrange("(p r) t -> r p t", r=128)
        MBv = g.MBT.rearrange("c k q -> k c q")
        for qb in range(NS):
            nkc = 4 * qb + 4
            Qe, bQe = Qe_r.next()
            Qo, bQo = Qo_r.next()
            mbq, bmbq = mb_r.next()
            fw.dma(fw.sp, Qe[0:64, :, :], QTv[0:64, :, qb * 512:(qb + 1) * 512], writes=[bQe])
            fw.dma(fw.sp, Qo[64:128, :, :], QTv[64:128, :, qb * 512:(qb + 1) * 512], writes=[bQo])
            for c0 in range(0, nkc, 8):
                c1 = min(nkc, c0 + 8)
                fw.dma(fw.sp, mbq[:, c0:c1, :], MBv[:, c0:c1, qb * 512:(qb + 1) * 512], writes=[bmbq])
            steps = []
            for kc in range(nkc):
                r = kc - 4 * qb
                steps.append((kc, 128 * r if r > 0 else 0))
            for hd in range(16):
                n = hd // 4
                Qb, bQb = (Qe, bQe) if hd % 2 == 0 else (Qo, bQo)
                po, bpo = A.pO.next()

                def S_fn(kc, off, pS, bpS, hd=hd, n=n, Qb=Qb, bQb=bQb, mbq=mbq, bmbq=bmbq, qb=qb):
                    adds = []
                    for v in (0, 1):
                        j = kc + v - 4 * qb
                        if 0 <= j < 4 and j * 128 >= off:
                            adds.append((v, j))
                    MM(g, pS[:, off:512], KT2[:, n, kc * 128:(kc + 1) * 128], Qb[:, hd // 2, off:512], True, False,
                       reads=[bKT2, bQb], writes=[bpS], signal=False)
                    MM(g, pS[:, off:512], g.ident[:], mbq[:, kc, off:512], False, len(adds) == 0, reads=[g.b_ident, bmbq], writes=[bpS])
                    for ai, (v, j) in enumerate(adds):
                        MM(g, pS[:, j * 128:(j + 1) * 128], J[:], Bt[:, hd, v, :], False, ai == len(adds) - 1, reads=[bJ, bBt], writes=[bpS])

                def V_fn(kc, n=n):
                    return V1[:, kc, n * 65:(n + 1) * 65], [bV1]

                attn_qblock(g, A, steps, S_fn, V_fn, po, bpo)
                attn_epilogue(g, A, po, bpo, hd, qb)
        attn_flush(A)
        end_phase(g)


def layer1(g):
    if "skip_l1" in g.debug:
        return
    src = g.xc
    l1_p1(g, src)
    if "stop_q1" in g.debug:
        return
    l1_p2(g)
    if "stop_q2" in g.debug:
        return
    l1_p3(g)
    if "stop_q3" in g.debug:
        return
    with contextlib.ExitStack() as st:
        proj_out_ln(g, st, g.OT.rearrange("h d t -> (h d) t"), g.I.dsa_wo, 3, src, g.xa)
        end_phase(g)
    if "stop_q4" in g.debug:
        return
    cross_attn(g, 1, g.xa, g.xb)
    if "stop_q5" in g.debug:
        return
    ffn(g, 1, g.xb, g.out, is_output=True)


def t5_bucket_np(d):
    n = np.maximum(d, 0)
    nf = np.maximum(n, 1).astype(np.float32)
    large = 16 + (np.log(nf / np.float32(16)) / np.float32(math.log(128 / 16)) * np.float32(16)).astype(np.int32)
    large = np.minimum(large, 31)
    return np.where(n < 16, n, large)


def make_consts():
    c = {}
    c["c_ident"] = np.eye(128, dtype=np.float32)
    k = np.arange(128)[:, None]
    q = np.arange(128)[None, :]
    c["c_tri"] = np.where(k <= q, 0.0, NEG).astype(np.float32)
    c["c_trif"] = np.where(np.arange(128)[None, :] <= np.arange(128)[:, None], 0.0, -1e30).astype(np.float32)
    inv = (np.float32(10000.0) ** (-np.arange(0, 32, 2, dtype=np.float32) / np.float32(32))).astype(np.float32)
    invf = np.zeros((32, 2), np.float32)
    invf[:, 0] = np.concatenate([inv, inv])
    invf[:, 1] = np.concatenate([-np.ones(16), np.ones(16)])
    c["c_invf"] = invf
    d = np.arange(-128, 256)
    b = t5_bucket_np(d)
    oh = np.zeros((32, 384), np.float32)
    oh[b, np.arange(384)] = 1.0
    c["c_oh"] = oh
    c["c_antiident"] = np.ascontiguousarray(np.eye(128, dtype=np.float32)[::-1])
    c["c_pow2"] = (2.0 ** -(np.arange(32) + 1.0)).astype(np.float32)[None, :]
    return c


def prep_weights(inp):
    w = {}
    f = lambda a: np.ascontiguousarray(a, dtype=np.float32)
    win = inp["mla_w_in"][0]
    w["mla_in"] = f(np.concatenate([win, win[:, 784:800], win[:, 768:784]], axis=1))
    uq = inp["mla_w_uq"][0].reshape(512, 16, 96)
    nope, x1, x2 = uq[:, :, 0:64], uq[:, :, 64:80], uq[:, :, 80:96]
    main = np.concatenate([nope, x1, x2], axis=2).reshape(512, 1536)
    sw = np.concatenate([nope, x2, x1], axis=2).reshape(512, 1536)
    w["mla_uq"] = f(np.concatenate([main, sw], axis=1))
    ukv = inp["mla_w_ukv"][0].reshape(256, 16, 128)
    w["mla_ukv"] = f(np.concatenate([ukv[:, :, 0:64].reshape(256, 1024), ukv[:, :, 64:128].reshape(256, 1024)], axis=1))
    w["mla_qn"] = f(inp["mla_q_norm"][0][None, :])
    w["mla_kvn"] = f(inp["mla_kv_norm"][0][None, :])
    w["mla_wo"] = f(inp["mla_w_o"][0])
    w["dsa_in"] = f(inp["dsa_w_in"][0])
    w["dsa_wo"] = f(inp["dsa_w_o"][0])
    w["xa_wq"] = f(inp["xa_w_q"])
    w["xa_wkv"] = f(inp["xa_w_kv"])
    w["xa_wo"] = f(inp["xa_w_o"])
    w["ffn_in"] = f(inp["ffn_w_in"])
    w["ffn_dn"] = f(inp["ffn_w_down"])
    w["ln_g"] = f(inp["ln_g"].reshape(6, D))
    w["ln_b"] = f(inp["ln_b"].reshape(6, D))
    w["rel_bias"] = f(inp["rel_bias"])
    return w


_CACHE = {}


def run(inputs, debug=(), n_cores=8):
    key = tuple(debug)
    if key not in _CACHE:
        _CACHE[key] = build_program(debug)
    nc = _CACHE[key]
    shared = prep_weights(inputs)
    shared.update(make_consts())
    in_maps = []
    for b in range(n_cores):
        m = dict(shared)
        m["x"] = np.ascontiguousarray(inputs["x"][b], dtype=np.float32)
        m["mem"] = np.ascontiguousarray(inputs["mem"][b], dtype=np.float32)
        m["pos"] = np.ascontiguousarray(inputs["positions"][b][None, :], dtype=np.int32)
        in_maps.append(m)
    res = run_bass_kernel_spmd(nc, in_maps, core_ids=list(range(n_cores)))
    return res


def kernel(**inputs):
    res = run(inputs)
    return np.stack([np.asarray(r["out"], dtype=np.float32) for r in res.results], axis=0)
```

```python
import contextlib
import math

import numpy as np
import concourse.bass as bass
import concourse.mybir as mybir
from concourse.bass_utils import run_bass_kernel_spmd

F32 = mybir.dt.float32
BF16 = mybir.dt.bfloat16
I32 = mybir.dt.int32
AF = mybir.ActivationFunctionType
ALU = mybir.AluOpType

S = 4096
D = 1024
NT = S // 128
NS = S // 512
ALPHA = 4 ** 0.25
LN_EPS = 1e-5
RMS_EPS = 1e-6
FFN_H = 2816
NEG = -30000.0
TOPK = 256
N_BISECT = 24


class Buf:
    __slots__ = ("name", "w", "r")

    def __init__(self, name=""):
        self.name = name
        self.w = None
        self.r = []


class Eng:
    def __init__(self, name, inorder_safe=False):
        self.name = name
        self.ops = []
        self.sem = None
        self.count = 0
        self.seen = {}
        self.inorder_safe = inorder_safe
        self.pending_unsignaled = False
        self.dma_sems = []
        self.dma_n = 0
        self.dma_evs = []


class FW:
    SEM_EPOCH = 30000

    def __init__(self, nc, stack, n_dma_sems=(32, 12)):
        self.nc = nc
        self.stack = stack
        self.pe = Eng("pe", inorder_safe=True)
        self.act = Eng("act")
        self.dve = Eng("dve")
        self.pool = Eng("pool")
        self.sp = Eng("sp")
        self.engs = [self.pe, self.act, self.dve, self.pool, self.sp]
        self.nsem = 0
        for e in self.engs:
            self._new_sem(e)
        for e, n in ((self.sp, n_dma_sems[0]), (self.pool, n_dma_sems[1])):
            for i in range(n):
                e.dma_sems.append(self._alloc_sem(f"dma_{e.name}_{i}"))
                e.dma_evs.append(None)
        self.out_evs = []
        self.n_ops = 0

    def _alloc_sem(self, name):
        self.nsem += 1
        return self.stack.enter_context(self.nc.semaphore(name))

    def _new_sem(self, e):
        e.sem = self._alloc_sem(f"s_{e.name}_{self.nsem}")
        e.count = 0

    def _need(self, eng, ev, waits):
        if ev is None:
            return
        sem, val, src = ev
        if src is eng and eng.inorder_safe:
            return
        if src is eng:
            assert not (sem is eng.sem and val > eng.count), "self-wait on unsignaled op"
        k = id(sem)
        if eng.seen.get(k, 0) >= val:
            return
        if k in waits:
            if waits[k][1] < val:
                waits[k] = (sem, val)
        else:
            waits[k] = (sem, val)

    def _collect(self, eng, reads, writes):
        waits = {}
        for b in reads:
            self._need(eng, b.w, waits)
        for b in writes:
            self._need(eng, b.w, waits)
            for ev in b.r:
                self._need(eng, ev, waits)
        return waits

    def _commit_waits(self, eng, waits):
        lst = []
        for k, (sem, val) in waits.items():
            eng.seen[k] = max(eng.seen.get(k, 0), val)
            lst.append((sem, val))
        return lst

    def _record(self, ev, reads, writes):
        for b in reads:
            b.r.append(ev)
        for b in writes:
            b.w = ev
            b.r = []

    def op(self, eng, meth, *args, reads=(), writes=(), signal=True, **kw):
        def fn(h, meth=meth, args=args, kw=kw):
            return getattr(h, meth)(*args, **kw)
        waits = self._collect(eng, reads, writes)
        if signal and eng.count >= self.SEM_EPOCH and not eng.pending_unsignaled:
            self._new_sem(eng)
        lst = self._commit_waits(eng, waits)
        if signal:
            eng.count += 1
            ev = (eng.sem, eng.count, eng)
            upd = (eng.sem, 1)
            eng.pending_unsignaled = False
        else:
            eng.pending_unsignaled = True
            ev = (eng.sem, eng.count + 1, eng)
            upd = None
        eng.ops.append((lst, fn, upd))
        self._record(ev, reads, writes)
        self.n_ops += 1
        return ev

    def dma(self, eng, out, in_, reads=(), writes=(), is_output=False):
        waits = self._collect(eng, reads, writes)
        slot = eng.dma_n % len(eng.dma_sems)
        self._need(eng, eng.dma_evs[slot], waits)
        lst = self._commit_waits(eng, waits)
        sem = eng.dma_sems[slot]
        val = 16 * (eng.dma_n // len(eng.dma_sems) + 1)
        ev = (sem, val, None)
        eng.dma_evs[slot] = ev
        eng.dma_n += 1

        def fn(h, out=out, in_=in_):
            return h.dma_start(out=out, in_=in_)
        eng.ops.append((lst, fn, (sem, 16)))
        self._record(ev, reads, writes)
        if is_output:
            self.out_evs.append(ev)
        self.n_ops += 1
        return ev

    def wait_events(self, eng, evs):
        waits = {}
        for ev in evs:
            self._need(eng, ev, waits)
        lst = self._commit_waits(eng, waits)
        if lst:
            eng.ops.append((lst, None, None))

    def barrier(self):
        evs = []
        for e in self.engs:
            assert not e.pending_unsignaled
            if e.count > 0:
                evs.append((e.sem, e.count, None))
            for d in e.dma_evs:
                if d is not None:
                    evs.append(d)
        for e in self.engs:
            self.wait_events(e, evs)

    def emit(self):
        nc = self.nc
        handles = {"pe": "tensor", "act": "scalar", "dve": "vector", "pool": "gpsimd", "sp": "sync"}
        with nc.Block() as block:
            def runner(eng):
                ops = eng.ops

                def body(h):
                    for (waits, fn, upd) in ops:
                        for (sem, val) in waits:
                            h.wait_ge(sem, val)
                        if fn is not None:
                            ins = fn(h)
                            if upd is not None:
                                ins.then_inc(upd[0], upd[1])
                return body
            for e in self.engs:
                getattr(block, handles[e.name])(runner(e))
        for e in self.engs:
            e.ops = []


class Rot:
    def __init__(self, items):
        self.items = items
        self.i = 0

    def next(self):
        it = self.items[self.i % len(self.items)]
        self.i += 1
        return it


class Ctx:
    pass


def build_program(debug=()):
    nc = bass.Bass("TRN2", target_bir_lowering=False)
    g = Ctx()
    g.nc = nc
    g.debug = debug

    def din(name, shape, dt=F32):
        return nc.dram_tensor(name, list(shape), dt, kind="ExternalInput").ap()

    def dscr(name, shape, dt):
        kind = "ExternalOutput" if name in debug else "Internal"
        return nc.dram_tensor(name, list(shape), dt, kind=kind).ap()

    I = Ctx()
    g.I = I
    I.x = din("x", [S, D])
    I.mem = din("mem", [256, D])
    I.pos = din("pos", [1, S], I32)
    I.rel_bias = din("rel_bias", [32, 16])
    I.mla_in = din("mla_in", [D, 832])
    I.mla_qn = din("mla_qn", [1, 512])
    I.mla_kvn = din("mla_kvn", [1, 256])
    I.mla_uq = din("mla_uq", [512, 3072])
    I.mla_ukv = din("mla_ukv", [256, 2048])
    I.mla_wo = din("mla_wo", [D, D])
    I.dsa_in = din("dsa_in", [D, 2120])
    I.dsa_wo = din("dsa_wo", [D, D])
    I.xa_wq = din("xa_wq", [2, D, D])
    I.xa_wkv = din("xa_wkv", [2, D, 2 * D])
    I.xa_wo = din("xa_wo", [2, D, D])
    I.ffn_in = din("ffn_in", [2, D, 2 * FFN_H])
    I.ffn_dn = din("ffn_dn", [2, FFN_H, D])
    I.ln_g = din("ln_g", [6, D])
    I.ln_b = din("ln_b", [6, D])
    I.c_ident = din("c_ident", [128, 128])
    I.c_tri = din("c_tri", [128, 128])
    I.c_trif = din("c_trif", [128, 128])
    I.c_invf = din("c_invf", [32, 2])
    I.c_oh = din("c_oh", [32, 384])
    I.c_pow2 = din("c_pow2", [1, 32])
    I.c_antiident = din("c_antiident", [128, 128])

    g.out = nc.dram_tensor("out", [S, D], F32, kind="ExternalOutput").ap()
    g.xa = dscr("xa", [S, D], F32)
    g.xb = dscr("xb", [S, D], F32)
    g.xc = dscr("xc", [S, D], F32)
    g.QT = dscr("QT", [16, 96, S], BF16)
    g.KT = dscr("KT", [16, 64, S], BF16)
    g.KR = dscr("KR", [32, S], BF16)
    g.V = dscr("V", [S, 16, 65], BF16)
    g.OT = dscr("OT", [16, 64, S], BF16)
    g.QI = dscr("QI", [512, S], BF16)
    g.KI = dscr("KI", [64, S], BF16)
    g.WI = dscr("WI", [S, 8], F32)
    g.MBT = dscr("MBT", [NT, 128, S], BF16)
    g.TB = dscr("TB", [16, 384], F32)
    g.b_TB = Buf("TB")
    g.QT1 = dscr("QT1", [1024, S], BF16)
    g.KT1 = dscr("KT1", [256, S], BF16)
    g.V1 = dscr("V1", [S, 4, 65], BF16)

    with contextlib.ExitStack() as gst:
        fw = FW(nc, gst)
        g.fw = fw
        g.gst = gst
        setup_globals(g)
        layer0(g)
        layer1(g)
        fw.wait_events(fw.sp, fw.out_evs)
        fw.emit()
    g.n_ops = fw.n_ops
    return nc


def _uname(g, name):
    g.uid = getattr(g, "uid", 0) + 1
    return f"{name}_{g.uid}"


def sb(g, st, name, shape, dt):
    return st.enter_context(g.nc.sbuf_tensor(_uname(g, name), list(shape), dt))


def ps(g, st, name, shape, dt=F32):
    return st.enter_context(g.nc.psum_tensor(_uname(g, name), list(shape), dt))


def rot_sb(g, st, name, shape, dt, n):
    return Rot([(sb(g, st, f"{name}{i}", shape, dt), Buf(f"{name}{i}")) for i in range(n)])


def rot_ps(g, st, name, shape, dt, n):
    return Rot([(ps(g, st, f"{name}{i}", shape, dt), Buf(f"{name}{i}")) for i in range(n)])


def end_phase(g):
    g.fw.barrier()
    g.fw.emit()


def MM(g, out, lhsT, rhs, start, stop, reads, writes, signal=None):
    g.fw.op(g.fw.pe, "matmul", out, lhsT, rhs, start=start, stop=stop, reads=reads, writes=writes,
            signal=(stop if signal is None else signal))


def ACT(g, out, in_, func, reads, writes, **kw):
    g.fw.op(g.fw.act, "activation", out=out, in_=in_, func=func, reads=reads, writes=writes, **kw)


def TT(g, eng, out, in0, in1, op, reads, writes):
    g.fw.op(eng, "tensor_tensor", out=out, in0=in0, in1=in1, op=op, reads=reads, writes=writes)


def STT(g, eng, out, in0, scalar, in1, op0, op1, reads, writes):
    g.fw.op(eng, "scalar_tensor_tensor", out=out, in0=in0, scalar=scalar, in1=in1, op0=op0, op1=op1, reads=reads, writes=writes)


def TS(g, eng, out, in0, s1, s2, op0, op1, reads, writes, **kw):
    if op1 is None:
        g.fw.op(eng, "tensor_scalar", out=out, in0=in0, scalar1=s1, scalar2=s2, op0=op0, reads=reads, writes=writes, **kw)
    else:
        g.fw.op(eng, "tensor_scalar", out=out, in0=in0, scalar1=s1, scalar2=s2, op0=op0, op1=op1, reads=reads, writes=writes, **kw)


def CP(g, eng, out, in_, reads, writes):
    if eng is g.fw.act:
        g.fw.op(eng, "copy", out, in_, reads=reads, writes=writes)
    else:
        g.fw.op(eng, "tensor_copy", out, in_, reads=reads, writes=writes)


def load_w(g, dst, bdst, src, K, ncols, col0=0, split=1):
    fw = g.fw
    nch = K // 128
    v = src.rearrange("(c p) n -> p c n", p=128)
    step = (nch + split - 1) // split
    for c0 in range(0, nch, step):
        c1 = min(nch, c0 + step)
        fw.dma(fw.pool, dst[:, c0:c1, :], v[:, c0:c1, col0:col0 + ncols], writes=[bdst])


def setup_globals(g):
    fw, st = g.fw, g.gst
    g.ident = sb(g, st, "ident", [128, 128], BF16)
    g.b_ident = Buf("ident")
    g.tri = sb(g, st, "tri", [128, 128], BF16)
    g.b_tri = Buf("tri")
    g.ones = sb(g, st, "ones", [128, 64], BF16)
    g.b_ones = Buf("ones")
    fw.dma(fw.pool, g.ident[:], g.I.c_ident[:, :], writes=[g.b_ident])
    fw.dma(fw.pool, g.tri[:], g.I.c_tri[:, :], writes=[g.b_tri])
    fw.op(fw.dve, "memset", g.ones[:], 1.0, writes=[g.b_ones])


def transpose_into(g, src, bsrc, xbf_rot, tp_rot, dstT, bdstT, j, nch, src_is_bf16=False):
    fw = g.fw
    if src_is_bf16:
        xb, bxb = src, bsrc
    else:
        xb, bxb = xbf_rot.next()
        CP(g, fw.act, xb[:, 0:nch * 128], src[:, 0:nch * 128], reads=[bsrc], writes=[bxb])
    tp, btp = tp_rot.next()
    for c in range(nch):
        fw.op(fw.pe, "transpose", tp[:, c, :], xb[:, c * 128:(c + 1) * 128], g.ident[:],
              reads=[bxb, g.b_ident], writes=[btp], signal=(c == nch - 1))
    CP(g, fw.dve, dstT[:, 0:nch, j * 128:(j + 1) * 128], tp[:, 0:nch, :], reads=[btp], writes=[bdstT])


class XLoader:
    def __init__(self, g, st, n_xs=4, n_xT=1, n_xbf=2):
        self.g = g
        self.xs = rot_sb(g, st, "xs", [128, D], F32, n_xs)
        self.xbf = rot_sb(g, st, "xbf", [128, D], BF16, n_xbf)
        self.xT = Rot([(sb(g, st, f"xT{i}", [128, 8, 512], BF16), [Buf(f"xT{i}_{j}") for j in range(4)]) for i in range(n_xT)])
        self.tp = rot_ps(g, st, "tpx", [128, 8, 128], BF16, 2)

    def load(self, src, s):
        g, fw = self.g, self.g.fw
        xT, bxT = self.xT.next()
        tiles = []
        for j in range(4):
            t0 = s * 512 + j * 128
            xs, bxs = self.xs.next()
            fw.dma(fw.sp, xs[:], src[t0:t0 + 128, :], writes=[bxs])
            tiles.append((xs, bxs))
            transpose_into(g, xs, bxs, self.xbf, self.tp, xT, bxT[j], j, 8)
        return xT, bxT, tiles


class LNUnit:
    def __init__(self, g, st, ln_idx):
        self.g = g
        fw = g.fw
        self.gam = sb(g, st, "ln_gam", [128, D], F32)
        self.bet = sb(g, st, "ln_bet", [128, D], F32)
        self.bgam, self.bbet = Buf("ln_g"), Buf("ln_b")
        fw.dma(fw.sp, self.gam[:], g.I.ln_g[ln_idx:ln_idx + 1, :].partition_broadcast(128), writes=[self.bgam])
        fw.dma(fw.sp, self.bet[:], g.I.ln_b[ln_idx:ln_idx + 1, :].partition_broadcast(128), writes=[self.bbet])
        self.stats = rot_sb(g, st, "ln_st", [128, 2, 6], F32, 3)
        self.mv = rot_sb(g, st, "ln_mv", [128, 4], F32, 3)

    def run(self, xs, bxs, y0, by0, y1, by1, dst, is_output=False):
        g, fw = self.g, self.g.fw
        stt, bst = self.stats.next()
        mv, bmv = self.mv.next()
        for hf, (y, by) in enumerate(((y0, by0), (y1, by1))):
            sl = slice(hf * 512, (hf + 1) * 512)
            STT(g, fw.dve, xs[:, sl], xs[:, sl], ALPHA, y[:, 0:512], ALU.mult, ALU.add, reads=[bxs, by], writes=[bxs])
        for hf in range(2):
            fw.op(fw.dve, "bn_stats", stt[:, hf, :], xs[:, hf * 512:(hf + 1) * 512], reads=[bxs], writes=[bst])
        fw.op(fw.dve, "bn_aggr", mv[:, 0:2], stt[:], reads=[bst], writes=[bmv])
        ACT(g, mv[:, 2:3], mv[:, 1:2], AF.Ln, reads=[bmv], writes=[bmv], bias=LN_EPS, scale=1.0)
        ACT(g, mv[:, 2:3], mv[:, 2:3], AF.Exp, reads=[bmv], writes=[bmv], scale=-0.5)
        STT(g, fw.dve, mv[:, 3:4], mv[:, 0:1], -1.0, mv[:, 2:3], ALU.mult, ALU.mult, reads=[bmv], writes=[bmv])
        ACT(g, xs[:], xs[:], AF.Identity, reads=[bxs, bmv], writes=[bxs], bias=mv[:, 3:4], scale=mv[:, 2:3])
        TT(g, fw.pool, xs[:], xs[:], self.gam[:], ALU.mult, reads=[bxs, self.bgam], writes=[bxs])
        TT(g, fw.pool, xs[:], xs[:], self.bet[:], ALU.add, reads=[bxs, self.bbet], writes=[bxs])
        fw.dma(fw.sp, dst, xs[:], reads=[bxs], is_output=is_output)


def proj_out_ln(g, st, OTrows, w_src, ln_idx, x_src, x_dst, is_output=False):
    fw = g.fw
    Wo = sb(g, st, "Wo", [128, 8, D], BF16)
    bWo = Buf("Wo")
    load_w(g, Wo, bWo, w_src, D, D, split=2)
    ln = LNUnit(g, st, ln_idx)
    ots = rot_sb(g, st, "ots", [128, 8, 512], BF16, 3)
    xsr = rot_sb(g, st, "pxs", [128, D], F32, 5)
    py = rot_ps(g, st, "py", [128, 512], F32, 4)
    otv = OTrows.rearrange("(p r) t -> r p t", r=128)
    ot_tiles = {}
    x_tiles = {}

    def prefetch(idx):
        if idx >= NT:
            return
        s_, j_ = divmod(idx, 4)
        if j_ == 0:
            ot_tiles[s_] = ots.next()
            fw.dma(fw.sp, ot_tiles[s_][0][:], otv[:, :, s_ * 512:(s_ + 1) * 512], writes=[ot_tiles[s_][1]])
        x_tiles[idx] = xsr.next()
        fw.dma(fw.sp, x_tiles[idx][0][:], x_src[idx * 128:(idx + 1) * 128, :], writes=[x_tiles[idx][1]])

    prefetch(0)
    prefetch(1)
    for idx in range(NT):
        prefetch(idx + 2)
        s_, j = divmod(idx, 4)
        ot, bot = ot_tiles[s_]
        xs, bxs = x_tiles.pop(idx)
        t0 = idx * 128
        ys = []
        for hf in range(2):
            y, by = py.next()
            for p in range(8):
                MM(g, y[:], ot[:, p, j * 128:(j + 1) * 128], Wo[:, p, hf * 512:(hf + 1) * 512], p == 0, p == 7, reads=[bot, bWo], writes=[by])
            ys.append((y, by))
        ln.run(xs, bxs, ys[0][0], ys[0][1], ys[1][0], ys[1][1], x_dst[t0:t0 + 128, :], is_output=is_output)


def attn_bufs(g, st, n_po=4, filler=False):
    A = Ctx()
    A.pJ = ps(g, st, "pJunk", [128, 512], F32) if filler else None
    A.bJ = Buf("pJunk")
    A.pS = rot_ps(g, st, "pS", [128, 512], F32, 3)
    A.pO = rot_ps(g, st, "pO", [65, 512], F32, n_po)
    A.pb = rot_ps(g, st, "pB", [64, 512], F32, 1)
    A.pT = rot_sb(g, st, "pT", [128, 512], BF16, 6)
    A.rd = rot_sb(g, st, "rd", [65, 512], F32, 2)
    A.rh = rot_sb(g, st, "rh", [65, 512], BF16, 2)
    A.rl = rot_sb(g, st, "rl", [65, 512], BF16, 2)
    A.rf = rot_sb(g, st, "rf", [65, 512], F32, 2)
    A.rb = rot_sb(g, st, "rb", [64, 512], F32, 2)
    A.ot = rot_sb(g, st, "ot", [64, 512], BF16, 2)
    A.deferred = []
    return A


def attn_epilogue(g, A, po, bpo, h, qb):
    fw = g.fw
    rd, brd = A.rd.next()
    fw.op(fw.dve, "reciprocal", rd[64:65, :], po[64:65, :], reads=[bpo], writes=[brd])
    rh, brh = A.rh.next()
    rl, brl = A.rl.next()
    rf, brf = A.rf.next()
    CP(g, fw.dve, rh[64:65, :], rd[64:65, :], reads=[brd], writes=[brh])
    CP(g, fw.dve, rf[64:65, :], rh[64:65, :], reads=[brh], writes=[brf])
    TT(g, fw.dve, rl[64:65, :], rd[64:65, :], rf[64:65, :], ALU.subtract, reads=[brd, brf], writes=[brl])
    def part2():
        pb, bpb = A.pb.next()
        MM(g, pb[:], g.ones[64:65, 0:64], rh[64:65, :], True, False, reads=[g.b_ones, brh], writes=[bpb], signal=False)
        MM(g, pb[:], g.ones[64:65, 0:64], rl[64:65, :], False, True, reads=[g.b_ones, brl], writes=[bpb])
        rb, brb = A.rb.next()
        CP(g, fw.dve, rb[:], pb[:], reads=[bpb], writes=[brb])
        ot, bot = A.ot.next()
        TT(g, fw.dve, ot[:], po[0:64, :], rb[:], ALU.mult, reads=[bpo, brb], writes=[bot])
        fw.dma(fw.sp, g.OT[h, :, qb * 512:(qb + 1) * 512], ot[:], reads=[bot])
    A.deferred.append(part2)


def attn_flush(A):
    while A.deferred:
        A.deferred.pop(0)()


def rope_tables(g, st):
    fw = g.fw
    C = sb(g, st, "ropeC", [96, S], F32)[64:96, :]
    Sm = sb(g, st, "ropeS", [96, S], F32)[64:96, :]
    bC, bS = Buf("ropeC"), Buf("ropeS")
    with contextlib.ExitStack() as t:
        posi = sb(g, t, "posi", [96, S], I32)[64:96, :]
        a = sb(g, t, "r_a", [96, S], F32)[64:96, :]
        kf = sb(g, t, "r_k", [96, S], F32)[64:96, :]
        ki = sb(g, t, "r_ki", [96, S], I32)[64:96, :]
        invf = sb(g, t, "invf", [96, 2], F32)[64:96, :]
        bposi, ba, bkf, bki, binvf = Buf(), Buf(), Buf(), Buf(), Buf()
        dve = fw.dve
        fw.dma(fw.sp, posi, g.I.pos[0:1, :].partition_broadcast(32), writes=[bposi])
        fw.dma(fw.sp, invf[:], g.I.c_invf, writes=[binvf])
        CP(g, dve, a, posi, reads=[bposi], writes=[ba])
        TS(g, dve, a, a, invf[:, 0:1], None, ALU.mult, None, reads=[ba, binvf], writes=[ba])
        TS(g, dve, kf, a, 1.0 / (2 * math.pi), 0.5, ALU.mult, ALU.add, reads=[ba], writes=[bkf])
        CP(g, dve, ki, kf, reads=[bkf], writes=[bki])
        CP(g, dve, kf, ki, reads=[bki], writes=[bkf])
        C1 = 6.28125
        C2 = 2 * math.pi - C1
        STT(g, dve, a, kf, -C1, a, ALU.mult, ALU.add, reads=[bkf, ba], writes=[ba])
        STT(g, dve, a, kf, -C2, a, ALU.mult, ALU.add, reads=[bkf, ba], writes=[ba])
        TS(g, dve, kf, a, -math.pi, 2 * math.pi, ALU.is_lt, ALU.mult, reads=[ba], writes=[bkf])
        TT(g, dve, a, a, kf, ALU.add, reads=[ba, bkf], writes=[ba])
        TS(g, dve, kf, a, math.pi, -2 * math.pi, ALU.is_gt, ALU.mult, reads=[ba], writes=[bkf])
        TT(g, dve, a, a, kf, ALU.add, reads=[ba, bkf], writes=[ba])
        TS(g, dve, a, a, -3.1415925, 3.1415925, ALU.max, ALU.min, reads=[ba], writes=[ba])
        ACT(g, Sm, a, AF.Sin, reads=[ba], writes=[bS])
        STT(g, dve, kf, a, -1.0, a, ALU.mult, ALU.max, reads=[ba], writes=[bkf])
        TS(g, dve, kf, kf, -1.0, math.pi / 2, ALU.mult, ALU.add, reads=[bkf], writes=[bkf])
        ACT(g, C, kf, AF.Sin, reads=[bkf], writes=[bC])
        TS(g, dve, Sm, Sm, invf[:, 1:2], None, ALU.mult, None, reads=[bS, binvf], writes=[bS])
        end_phase(g)
    return C, bC, Sm, bS


def l0_p1(g):
    fw = g.fw
    with contextlib.ExitStack() as st:
        C, bC, Sm, bS = rope_tables(g, st)
        Win = sb(g, st, "Win", [128, 8, 832], BF16)
        Wuq = sb(g, st, "Wuq", [128, 4, 3072], BF16)
        Wukv = sb(g, st, "Wukv", [128, 2, 2048], BF16)
        bWin, bWuq, bWukv = Buf(), Buf(), Buf()
        load_w(g, Win, bWin, g.I.mla_in, D, 832)
        load_w(g, Wuq, bWuq, g.I.mla_uq, 512, 3072)
        load_w(g, Wukv, bWukv, g.I.mla_ukv, 256, 2048)
        gq = sb(g, st, "gq", [128, 512], F32)
        gkv = sb(g, st, "gkv", [128, 256], F32)
        bgq, bgkv = Buf(), Buf()
        fw.dma(fw.sp, gq[:], g.I.mla_qn[0:1, :].partition_broadcast(128), writes=[bgq])
        fw.dma(fw.sp, gkv[:], g.I.mla_kvn[0:1, :].partition_broadcast(128), writes=[bgkv])
        xl = XLoader(g, st, n_xs=3, n_xT=2)
        ph = rot_ps(g, st, "ph", [128, 512], F32, 2)
        pv = rot_ps(g, st, "pv", [128, 512], F32, 2)
        pq = rot_ps(g, st, "pq", [128, 512], F32, 2)
        nrm = rot_sb(g, st, "nrm", [128, 768], BF16, 2)
        junk = rot_sb(g, st, "junk", [128, 512], BF16, 2)
        ss = rot_sb(g, st, "ss", [128, 4], F32, 4)
        nT = Rot([(sb(g, st, f"nT{i}", [128, 6, 512], BF16), [Buf() for j in range(4)]) for i in range(2)])
        vsb = rot_sb(g, st, "vsb", [128, 16, 65], BF16, 2)
        for (t, b) in vsb.items:
            fw.op(fw.pool, "memset", t[:], 1.0, writes=[b])
        t1 = rot_sb(g, st, "t1", [96, 512], F32, 2)
        t2 = rot_sb(g, st, "t2", [96, 512], F32, 2)
        qt = rot_sb(g, st, "qt", [96, 512], BF16, 3)
        kt = rot_sb(g, st, "kt", [128, 512], BF16, 2)
        kr = rot_sb(g, st, "kr", [96, 512], BF16, 2)
        qscale = 96 ** -0.5
        KTv = g.KT.rearrange("h d t -> (h d) t")
        pre = xl.load(g.I.x, 0)
        for s in range(NS):
            xT, bxT, tiles = pre
            nt, bnt = nT.next()
            cols = slice(s * 512, (s + 1) * 512)
            for j in range(4):
                tsl = slice(j * 128, (j + 1) * 128)
                hq, bhq = ph.next()
                hkv, bhkv = ph.next()
                for c in range(8):
                    MM(g, hq[:], xT[:, c, tsl], Win[:, c, 0:512], c == 0, c == 7, reads=[bxT[j], bWin], writes=[bhq])
                for c in range(8):
                    MM(g, hkv[:, 0:256], xT[:, c, tsl], Win[:, c, 512:768], c == 0, c == 7, reads=[bxT[j], bWin], writes=[bhkv])
                sq, bsq = ss.next()
                jk, bjk = junk.next()
                ACT(g, jk[:], hq[:], AF.Square, reads=[bhq], writes=[bjk, bsq], accum_out=sq[:, 0:1])
                ACT(g, jk[:, 0:256], hkv[:, 0:256], AF.Square, reads=[bhkv], writes=[bjk, bsq], accum_out=sq[:, 1:2])
                ACT(g, sq[:, 2:3], sq[:, 0:1], AF.Ln, reads=[bsq], writes=[bsq], bias=RMS_EPS, scale=1.0 / 512)
                ACT(g, sq[:, 3:4], sq[:, 1:2], AF.Ln, reads=[bsq], writes=[bsq], bias=RMS_EPS, scale=1.0 / 256)
                ACT(g, sq[:, 2:4], sq[:, 2:4], AF.Exp, reads=[bsq], writes=[bsq], scale=-0.5)
                nr, bnr = nrm.next()
                STT(g, fw.dve, nr[:, 0:512], hq[:], sq[:, 2:3], gq[:], ALU.mult, ALU.mult, reads=[bhq, bsq, bgq], writes=[bnr])
                STT(g, fw.dve, nr[:, 512:768], hkv[:, 0:256], sq[:, 3:4], gkv[:], ALU.mult, ALU.mult, reads=[bhkv, bsq, bgkv], writes=[bnr])
                transpose_into(g, nr, bnr, None, xl.tp, nt, bnt[j], j, 6, src_is_bf16=True)
                vt, bvt = vsb.next()
                for hf in range(2):
                    p_, bp_ = pv.next()
                    for r in range(2):
                        MM(g, p_[:], nt[:, 4 + r, tsl], Wukv[:, r, 1024 + hf * 512:1024 + (hf + 1) * 512], r == 0, r == 1, reads=[bnt[j], bWukv], writes=[bp_])
                    CP(g, fw.act, vt[:, hf * 8:(hf + 1) * 8, 0:64], p_[:].rearrange("p (a b) -> p a b", b=64), reads=[bp_], writes=[bvt])
                t0 = s * 512 + j * 128
                fw.dma(fw.sp, g.V[t0:t0 + 128, :, :], vt[:], reads=[bvt])
            if s + 1 < NS:
                pre = xl.load(g.I.x, s + 1)
            pa, bpa = pq.next()
            pb_, bpb_ = pq.next()
            for c in range(8):
                MM(g, pa[0:96, :], Win[:, c, 704:800], xT[:, c, :], c == 0, c == 7, reads=bxT + [bWin], writes=[bpa])
            for c in range(8):
                MM(g, pb_[0:96, :], Win[:, c, 736:832], xT[:, c, :], c == 0, c == 7, reads=bxT + [bWin], writes=[bpb_])
            a1, ba1 = t1.next()
            a2, ba2 = t2.next()
            TT(g, fw.dve, a1[64:96, :], pa[64:96, :], C[:, cols], ALU.mult, reads=[bpa, bC], writes=[ba1])
            TT(g, fw.dve, a2[64:96, :], pb_[64:96, :], Sm[:, cols], ALU.mult, reads=[bpb_, bS], writes=[ba2])
            krt, bkrt = kr.next()
            TT(g, fw.pool, krt[64:96, :], a1[64:96, :], a2[64:96, :], ALU.add, reads=[ba1, ba2], writes=[bkrt])
            fw.dma(fw.sp, g.KR[:, cols], krt[64:96, :], reads=[bkrt])
            for hd in range(16):
                pm, bpm = pq.next()
                psw, bpsw = pq.next()
                for r in range(4):
                    MM(g, pm[0:96, :], Wuq[:, r, hd * 96:(hd + 1) * 96], nt[:, r, :], r == 0, r == 3, reads=bnt + [bWuq], writes=[bpm])
                for r in range(4):
                    MM(g, psw[0:96, :], Wuq[:, r, 1536 + hd * 96:1536 + (hd + 1) * 96], nt[:, r, :], r == 0, r == 3, reads=bnt + [bWuq], writes=[bpsw])
                q_, bq_ = qt.next()
                a1, ba1 = t1.next()
                a2, ba2 = t2.next()
                ACT(g, q_[0:64, :], pm[0:64, :], AF.Copy, reads=[bpm], writes=[bq_], scale=qscale)
                STT(g, fw.dve, a1[64:96, :], pm[64:96, :], qscale, C[:, cols], ALU.mult, ALU.mult, reads=[bpm, bC], writes=[ba1])
                STT(g, fw.dve, a2[64:96, :], psw[64:96, :], qscale, Sm[:, cols], ALU.mult, ALU.mult, reads=[bpsw, bS], writes=[ba2])
                TT(g, fw.pool, q_[64:96, :], a1[64:96, :], a2[64:96, :], ALU.add, reads=[ba1, ba2], writes=[bq_])
                fw.dma(fw.sp, g.QT[hd, :, cols], q_[:], reads=[bq_])
            for p in range(8):
                pk, bpk = pq.next()
                for r in range(2):
                    MM(g, pk[:], Wukv[:, r, p * 128:(p + 1) * 128], nt[:, 4 + r, :], r == 0, r == 1, reads=bnt + [bWukv], writes=[bpk])
                k_, bk_ = kt.next()
                CP(g, fw.act, k_[:], pk[:], reads=[bpk], writes=[bk_])
                fw.dma(fw.sp, KTv[p * 128:(p + 1) * 128, cols], k_[:], reads=[bk_])
        end_phase(g)


def attn_qblock(g, A, steps, S_fn, V_fn, po, bpo):
    fw = g.fw
    n = len(steps)
    pend = []

    def do_pv(item):
        (i, kc, off), pT, bpT = item
        lhsT, vreads = V_fn(kc)
        MM(g, po[:, off:512], lhsT, pT[:, off:512], i == 0, i == n - 1, reads=[bpT] + vreads, writes=[bpo])

    for i, (kc, off) in enumerate(steps):
        pS, bpS = A.pS.next()
        S_fn(kc, off, pS, bpS)
        if A.pJ is not None:
            A.filler_fn(kc)
        pT, bpT = A.pT.next()
        ACT(g, pT[:, off:512], pS[:, off:512], AF.Exp, reads=[bpS], writes=[bpT])
        pend.append(((i, kc, off), pT, bpT))
        if len(pend) > 3:
            do_pv(pend.pop(0))
        if i == min(3, n - 1):
            attn_flush(A)
    while pend:
        do_pv(pend.pop(0))


def l0_p2(g):
    fw = g.fw
    with contextlib.ExitStack() as st:
        A = attn_bufs(g, st, n_po=4, filler=False)
        Kh = rot_sb(g, st, "Kh", [96, S], BF16, 2)
        Qh = rot_sb(g, st, "Qh", [96, S], BF16, 2)
        Vall = sb(g, st, "Vall", [128, NT, 16 * 65], BF16)
        bVall = Buf("Vall")
        Vv = g.V.rearrange("(c p) h e -> p c (h e)", p=128)
        for c0 in range(0, NT, 8):
            fw.dma(fw.sp, Vall[:, c0:c0 + 8, :], Vv[:, c0:c0 + 8, :], writes=[bVall])
        Kbufs = {}
        for hd in range(16):
            K_, bK = Kh.next()
            Q_, bQ = Qh.next()
            bKr = Kbufs.setdefault(id(K_), Buf("Kr"))
            fw.dma(fw.sp, K_[64:96, :], g.KR[:, :], writes=[bKr])
            fw.dma(fw.sp, K_[0:64, :], g.KT[hd, :, :], writes=[bK])
            fw.dma(fw.sp, Q_[:], g.QT[hd, :, :], writes=[bQ])
            for qb in range(NS):
                po, bpo = A.pO.next()
                steps = []
                for kc in range(4 * qb + 4):
                    r = kc - 4 * qb
                    steps.append((kc, 128 * r if r > 0 else 0))

                def S_fn(kc, off, pS, bpS, K_=K_, Q_=Q_, bK=bK, bKr=bKr, bQ=bQ, qb=qb):
                    r = kc - 4 * qb
                    ks = slice(kc * 128, (kc + 1) * 128)
                    q0 = qb * 512
                    if r < 0:
                        MM(g, pS[:, 0:512], K_[:, ks], Q_[:, q0:q0 + 512], True, True, reads=[bK, bKr, bQ], writes=[bpS])
                    else:
                        MM(g, pS[:, off:off + 128], K_[:, ks], Q_[:, q0 + off:q0 + off + 128], True, False, reads=[bK, bKr, bQ], writes=[bpS], signal=False)
                        last = off + 128 >= 512
                        MM(g, pS[:, off:off + 128], g.ident[:], g.tri[:], False, True, reads=[g.b_ident, g.b_tri], writes=[bpS], signal=last)
                        if not last:
                            MM(g, pS[:, off + 128:512], K_[:, ks], Q_[:, q0 + off + 128:q0 + 512], True, True, reads=[bK, bKr, bQ], writes=[bpS])

                def V_fn(kc, hd=hd):
                    return Vall[:, kc, hd * 65:(hd + 1) * 65], [bVall]

                def filler_fn(kc, K_=K_, Q_=Q_, bK=bK, bKr=bKr, bQ=bQ, qb=qb):
                    MM(g, A.pJ[:, 0:384], K_[:, kc * 128:(kc + 1) * 128], Q_[:, qb * 512:qb * 512 + 384], True, True,
                       reads=[bK, bKr, bQ], writes=[A.bJ], signal=False)
                A.filler_fn = filler_fn

                attn_qblock(g, A, steps, S_fn, V_fn, po, bpo)
                attn_epilogue(g, A, po, bpo, hd, qb)
        attn_flush(A)
        end_phase(g)


def layer0(g):
    l0_p1(g)
    if "stop_p1" in g.debug:
        return
    l0_p2(g)
    if "stop_p2" in g.debug:
        return
    with contextlib.ExitStack() as st:
        proj_out_ln(g, st, g.OT.rearrange("h d t -> (h d) t"), g.I.mla_wo, 0, g.I.x, g.xa)
        end_phase(g)
    if "stop_p3" in g.debug:
        return
    cross_attn(g, 0, g.xa, g.xb)
    if "stop_p4" in g.debug:
        return
    ffn(g, 0, g.xb, g.xc, is_output=False)


def cross_attn(g, li, x_src, x_dst):
    fw = g.fw
    with contextlib.ExitStack() as st:
        Wq = sb(g, st, "Wq", [128, 8, D], BF16)
        Wo = sb(g, st, "Wo", [128, 8, D], BF16)
        bWq, bWo = Buf(), Buf()
        KmT = sb(g, st, "KmT", [128, 4, 2, 256], BF16)
        Vm = sb(g, st, "Vm", [128, 2, 4, 257], BF16)
        bKmT, bVm = Buf(), Buf()
        xl = XLoader(g, st, n_xs=9, n_xT=2)
        pA = rot_ps(g, st, "pA", [128, 512], F32, 4)
        py = rot_ps(g, st, "py", [128, 512], F32, 2)
        with contextlib.ExitStack() as t:
            Wkv = sb(g, t, "Wkv", [128, 8, 2 * D], BF16)
            bWkv = Buf()
            load_w(g, Wkv, bWkv, g.I.xa_wkv[li], D, 2 * D, split=4)
            load_w(g, Wq, bWq, g.I.xa_wq[li], D, D, split=2)
            load_w(g, Wo, bWo, g.I.xa_wo[li], D, D, split=2)
            mt = rot_sb(g, t, "memt", [128, D], F32, 2)
            memT = sb(g, t, "memT", [128, 8, 256], BF16)
            bmemT = Buf()
            fw.op(fw.pool, "memset", Vm[:], 1.0, writes=[bVm])
            for mc in range(2):
                m_, bm_ = mt.next()
                fw.dma(fw.sp, m_[:], g.I.mem[mc * 128:(mc + 1) * 128, :], writes=[bm_])
                transpose_into(g, m_, bm_, xl.xbf, xl.tp, memT, bmemT, mc, 8)
            for hd in range(4):
                for e in range(2):
                    p_, bp_ = pA.next()
                    c0 = hd * 256 + e * 128
                    for c in range(8):
                        MM(g, p_[:, 0:256], Wkv[:, c, c0:c0 + 128], memT[:, c, :], c == 0, c == 7, reads=[bWkv, bmemT], writes=[bp_])
                    CP(g, fw.act, KmT[:, hd, e, :], p_[:, 0:256], reads=[bp_], writes=[bKmT])
            for mc in range(2):
                for hf in range(2):
                    p_, bp_ = pA.next()
                    for c in range(8):
                        MM(g, p_[:], memT[:, c, mc * 128:(mc + 1) * 128], Wkv[:, c, D + hf * 512:D + (hf + 1) * 512], c == 0, c == 7, reads=[bWkv, bmemT], writes=[bp_])
                    CP(g, fw.dve, Vm[:, mc, 2 * hf:2 * hf + 2, 0:256], p_[:].rearrange("p (a b) -> p a b", b=256), reads=[bp_], writes=[bVm])
            end_phase(g)
        ln = LNUnit(g, st, li * 3 + 1)
        qT = sb(g, st, "qT", [128, 8, 512], BF16)
        bqT = Buf()
        pTs = [(sb(g, st, f"xpT{i}", [128, 512], BF16), Buf()) for i in range(8)]
        o_r = rot_sb(g, st, "xo", [128, D], BF16, 2)
        rdn = rot_sb(g, st, "xrd", [128, 4], F32, 2)
        oT = rot_sb(g, st, "xoT", [128, 8, 128], BF16, 2)
        pre = xl.load(x_src, 0)
        for s in range(NS):
            xT, bxT, tiles = pre
            for cc in range(8):
                p_, bp_ = pA.next()
                for c in range(8):
                    MM(g, p_[:], Wq[:, c, cc * 128:(cc + 1) * 128], xT[:, c, :], c == 0, c == 7, reads=[bWq] + bxT, writes=[bp_])
                ACT(g, qT[:, cc, :], p_[:], AF.Copy, reads=[bp_], writes=[bqT], scale=1.0 / 16)
            for hd in range(4):
                for mc in range(2):
                    p_, bp_ = pA.next()
                    for e in range(2):
                        MM(g, p_[:], KmT[:, hd, e, mc * 128:(mc + 1) * 128], qT[:, 2 * hd + e, :], e == 0, e == 1, reads=[bKmT, bqT], writes=[bp_])
                    pT, bpT = pTs[hd * 2 + mc]
                    ACT(g, pT[:], p_[:], AF.Exp, reads=[bp_], writes=[bpT])
            if s + 1 < NS:
                pre = xl.load(x_src, s + 1)
            for j in range(4):
                xs, bxs = tiles[j]
                o_, bo_ = o_r.next()
                rd, brd = rdn.next()
                for hd in range(4):
                    p_, bp_ = pA.next()
                    for mc in range(2):
                        pT, bpT = pTs[hd * 2 + mc]
                        MM(g, p_[:, 0:257], pT[:, j * 128:(j + 1) * 128], Vm[:, mc, hd, :], mc == 0, mc == 1, reads=[bpT, bVm], writes=[bp_])
                    fw.op(fw.dve, "reciprocal", rd[:, hd:hd + 1], p_[:, 256:257], reads=[bp_], writes=[brd])
                    TS(g, fw.dve, o_[:, hd * 256:(hd + 1) * 256], p_[:, 0:256], rd[:, hd:hd + 1], None, ALU.mult, None, reads=[bp_, brd], writes=[bo_])
                ot, bot = oT.next()
                transpose_into(g, o_, bo_, None, xl.tp, ot, bot, 0, 8, src_is_bf16=True)
                ys = []
                for hf in range(2):
                    y, by = py.next()
                    for c in range(8):
                        MM(g, y[:], ot[:, c, :], Wo[:, c, hf * 512:(hf + 1) * 512], c == 0, c == 7, reads=[bot, bWo], writes=[by])
                    ys.append((y, by))
                t0 = s * 512 + j * 128
                ln.run(xs, bxs, ys[0][0], ys[0][1], ys[1][0], ys[1][1], x_dst[t0:t0 + 128, :])
        end_phase(g)


def ffn(g, li, x_src, x_dst, is_output):
    fw = g.fw
    NF = FFN_H // 128
    with contextlib.ExitStack() as st:
        W1 = sb(g, st, "W1", [128, 8, 2 * FFN_H], BF16)
        W2 = sb(g, st, "W2", [128, NF, D], BF16)
        bW1, bW2 = Buf(), Buf()
        load_w(g, W1, bW1, g.I.ffn_in[li], D, 2 * FFN_H, split=8)
        load_w(g, W2, bW2, g.I.ffn_dn[li], FFN_H, D, split=2)
        ln = LNUnit(g, st, li * 3 + 2)
        xl = XLoader(g, st, n_xs=4, n_xT=1)
        pA = rot_ps(g, st, "pA", [128, 512], F32, 4)
        py = rot_ps(g, st, "py", [128, 512], F32, 2)
        aT = sb(g, st, "aT", [128, NF, 512], BF16)
        baT = Buf()
        sg = rot_sb(g, st, "sg", [128, 512], F32, 2)
        for s in range(NS):
            xT, bxT, tiles = xl.load(x_src, s)
            for fc in range(NF):
                pg, bpg = pA.next()
                pu, bpu = pA.next()
                for c in range(8):
                    MM(g, pg[:], W1[:, c, fc * 128:(fc + 1) * 128], xT[:, c, :], c == 0, c == 7, reads=[bW1] + bxT, writes=[bpg])
                for c in range(8):
                    MM(g, pu[:], W1[:, c, FFN_H + fc * 128:FFN_H + (fc + 1) * 128], xT[:, c, :], c == 0, c == 7, reads=[bW1] + bxT, writes=[bpu])
                s_, bs_ = sg.next()
                ACT(g, s_[:], pg[:], AF.Silu, reads=[bpg], writes=[bs_])
                TT(g, fw.dve, aT[:, fc, :], s_[:], pu[:], ALU.mult, reads=[bs_, bpu], writes=[baT])
            for j in range(4):
                xs, bxs = tiles[j]
                ys = []
                for hf in range(2):
                    y, by = py.next()
                    for fc in range(NF):
                        MM(g, y[:], aT[:, fc, j * 128:(j + 1) * 128], W2[:, fc, hf * 512:(hf + 1) * 512], fc == 0, fc == NF - 1, reads=[baT, bW2], writes=[by])
                    ys.append((y, by))
                t0 = s * 512 + j * 128
                ln.run(xs, bxs, ys[0][0], ys[0][1], ys[1][0], ys[1][1], x_dst[t0:t0 + 128, :], is_output=is_output)
        end_phase(g)


def l1_p1(g, x_src):
    fw = g.fw
    with contextlib.ExitStack() as st:
        Win = sb(g, st, "Win1", [128, 8, 2120], BF16)
        bWin = Buf()
        load_w(g, Win, bWin, g.I.dsa_in, D, 2120, split=2)
        xl = XLoader(g, st, n_xs=3, n_xT=2)
        pq = rot_ps(g, st, "pq", [128, 512], F32, 4)
        pv = rot_ps(g, st, "pv", [128, 512], F32, 2)
        ev = rot_sb(g, st, "ev", [128, 512], BF16, 4)
        vsb = rot_sb(g, st, "vsb1", [128, 4, 65], BF16, 2)
        for (t, b) in vsb.items:
            fw.op(fw.pool, "memset", t[:], 1.0, writes=[b])
        wsb = rot_sb(g, st, "wsb", [128, 8], F32, 2)
        pre = xl.load(x_src, 0)
        for s in range(NS):
            xT, bxT, tiles = pre
            cols = slice(s * 512, (s + 1) * 512)
            for j in range(4):
                tsl = slice(j * 128, (j + 1) * 128)
                t0 = s * 512 + j * 128
                p_, bp_ = pv.next()
                for c in range(8):
                    MM(g, p_[:, 0:256], xT[:, c, tsl], Win[:, c, 1280:1536], c == 0, c == 7, reads=[bxT[j], bWin], writes=[bp_])
                vt, bvt = vsb.next()
                CP(g, fw.act, vt[:, :, 0:64], p_[:, 0:256].rearrange("p (a b) -> p a b", b=64), reads=[bp_], writes=[bvt])
                fw.dma(fw.sp, g.V1[t0:t0 + 128, :, :], vt[:], reads=[bvt])
                p_, bp_ = pv.next()
                for c in range(8):
                    MM(g, p_[:, 0:8], xT[:, c, tsl], Win[:, c, 2112:2120], c == 0, c == 7, reads=[bxT[j], bWin], writes=[bp_])
                wt, bwt = wsb.next()
                ACT(g, wt[:], p_[:, 0:8], AF.Copy, reads=[bp_], writes=[bwt], scale=8 ** -0.5)
                fw.dma(fw.sp, g.WI[t0:t0 + 128, :], wt[:], reads=[bwt])
            if s + 1 < NS:
                pre = xl.load(x_src, s + 1)
            jobs = [(0, 8, 0.125, g.QT1), (1024, 2, 1.0, g.KT1), (1536, 4, 0.125, g.QI)]
            for (c0, nch, sc, dst) in jobs:
                for p in range(nch):
                    p_, bp_ = pq.next()
                    for c in range(8):
                        MM(g, p_[:], Win[:, c, c0 + p * 128:c0 + (p + 1) * 128], xT[:, c, :], c == 0, c == 7, reads=bxT + [bWin], writes=[bp_])
                    e_, be_ = ev.next()
                    ACT(g, e_[:], p_[:], AF.Copy, reads=[bp_], writes=[be_], scale=sc)
                    fw.dma(fw.sp, dst[p * 128:(p + 1) * 128, cols], e_[:], reads=[be_])
            p_, bp_ = pq.next()
            for c in range(8):
                MM(g, p_[0:64, :], Win[:, c, 2048:2112], xT[:, c, :], c == 0, c == 7, reads=bxT + [bWin], writes=[bp_])
            e_, be_ = ev.next()
            ACT(g, e_[0:64, :], p_[0:64, :], AF.Copy, reads=[bp_], writes=[be_])
            fw.dma(fw.sp, g.KI[:, cols], e_[0:64, :], reads=[be_])
        end_phase(g)


def l1_p2(g):
    fw = g.fw
    NB = N_BISECT
    with contextlib.ExitStack() as st:
        KI2 = sb(g, st, "KI2", [128, S], BF16)
        bKI2 = Buf()
        fw.dma(fw.sp, KI2[0:64, :], g.KI[:, :], writes=[bKI2])
        fw.dma(fw.sp, KI2[64:128, :], g.KI[:, :], writes=[bKI2])
        trif = sb(g, st, "trif", [128, 128], F32)
        pw2 = sb(g, st, "pw2", [128, 32], F32)
        zt = sb(g, st, "zt", [128, 128], BF16)
        btrif, bpw2, bzt = Buf(), Buf(), Buf()
        fw.dma(fw.sp, trif[:], g.I.c_trif[:, :], writes=[btrif])
        fw.dma(fw.sp, pw2[:], g.I.c_pow2[0:1, :].partition_broadcast(128), writes=[bpw2])
        fw.op(fw.dve, "memset", zt[:], 0.0, writes=[bzt])
        MBv = g.MBT.rearrange("c k q -> k c q")
        fw.dma(fw.sp, g.MBT[0, :, 0:128], g.tri[:], reads=[g.b_tri])
        fw.dma(fw.sp, g.MBT[0, :, 128:256], zt[:], reads=[bzt])
        fw.dma(fw.sp, g.MBT[1, :, 128:256], g.tri[:], reads=[g.b_tri])

        qi_r = rot_sb(g, st, "qi", [128, 4, 128], BF16, 2)
        wt_r = rot_sb(g, st, "wt", [128, 8], F32, 2)
        dg_r = rot_sb(g, st, "dg", [128, 8, 128], BF16, 2)
        sc_r = rot_sb(g, st, "score", [128, S], F32, 2)
        rl_r = rot_sb(g, st, "rl", [128, 512], BF16, 4)
        junk = sb(g, st, "bjunk", [128, S], BF16)
        bjunk = Buf()
        sm_r = rot_sb(g, st, "bsm", [128, 40], F32, 2)
        mb_r = rot_sb(g, st, "mb", [128, S], BF16, 2)
        mbT_r = rot_sb(g, st, "mbT", [128, NT, 128], BF16, 2)
        pD = rot_ps(g, st, "pD", [128, 512], F32, 4)
        pSc = rot_ps(g, st, "pSc", [128, 512], F32, 2)
        ptp = rot_ps(g, st, "ptp", [128, 8, 128], BF16, 2)
        QIv = g.QI.rearrange("(ip r) t -> r ip t", r=128)

        def score_tile(i):
            nk = (i + 1) * 128
            qi, bqi = qi_r.next()
            wt, bwt = wt_r.next()
            dg, bdg = dg_r.next()
            sc, bsc = sc_r.next()
            fw.dma(fw.sp, qi[:], QIv[:, :, i * 128:(i + 1) * 128], writes=[bqi])
            fw.dma(fw.sp, wt[:], g.WI[i * 128:(i + 1) * 128, :], writes=[bwt])
            for h in range(8):
                ACT(g, dg[:, h, :], g.ident[:], AF.Copy, reads=[g.b_ident, bwt], writes=[bdg], scale=wt[:, h:h + 1])
            for kb in range((nk + 511) // 512):
                w = min(512, nk - kb * 512)
                ps_, bps_ = pSc.next()
                for h in range(8):
                    half = slice((h % 2) * 64, (h % 2) * 64 + 64)
                    pd, bpd = pD.next()
                    MM(g, pd[:, 0:w], qi[half, h // 2, :], KI2[half, kb * 512:kb * 512 + w], True, True, reads=[bqi, bKI2], writes=[bpd])
                    rl, brl = rl_r.next()
                    ACT(g, rl[:, 0:w], pd[:, 0:w], AF.Relu, reads=[bpd], writes=[brl])
                    MM(g, ps_[:, 0:w], dg[:, h, :], rl[:, 0:w], h == 0, h == 7, reads=[bdg, brl], writes=[bps_])
                CP(g, fw.act, sc[:, kb * 512:kb * 512 + w], ps_[:, 0:w], reads=[bps_], writes=[bsc])
            return sc, bsc

        def bisect_tile(i, sc, bsc):
            nk = (i + 1) * 128
            sm, bsm = sm_r.next()
            dve = fw.dve
            TT(g, dve, sc[:, i * 128:nk], sc[:, i * 128:nk], trif[:], ALU.add, reads=[bsc, btrif], writes=[bsc])
            TS(g, dve, junk[:, 0:nk], sc[:, 0:nk], 1.0, -3.0e38, ALU.mult, ALU.max, reads=[bsc], writes=[bjunk, bsm], accum_out=sm[:, 0:1])
            TS(g, dve, junk[:, 0:i * 128], sc[:, 0:i * 128], -1.0, -3.0e38, ALU.mult, ALU.max, reads=[bsc], writes=[bjunk, bsm], accum_out=sm[:, 1:2])
            TS(g, dve, sm[:, 1:2], sm[:, 1:2], -1.0, None, ALU.mult, None, reads=[bsm], writes=[bsm])
            TT(g, dve, sm[:, 2:3], sm[:, 0:1], sm[:, 1:2], ALU.subtract, reads=[bsm], writes=[bsm])
            TS(g, dve, sm[:, 8:40], pw2[:], sm[:, 2:3], None, ALU.mult, None, reads=[bsm, bpw2], writes=[bsm])
            TT(g, dve, sm[:, 3:4], sm[:, 1:2], sm[:, 8:9], ALU.add, reads=[bsm], writes=[bsm])
            for n in range(NB):
                TS(g, dve, junk[:, 0:nk], sc[:, 0:nk], sm[:, 3:4], 0.0, ALU.is_ge, ALU.add, reads=[bsc, bsm], writes=[bjunk, bsm], accum_out=sm[:, 4:5])
                TS(g, dve, sm[:, 5:6], sm[:, 4:5], float(TOPK), 0.5, ALU.is_ge, ALU.subtract, reads=[bsm], writes=[bsm])
                STT(g, dve, sm[:, 3:4], sm[:, 5:6], sm[:, 8 + n:9 + n], sm[:, 3:4], ALU.mult, ALU.add, reads=[bsm], writes=[bsm])
            TT(g, dve, sm[:, 6:7], sm[:, 3:4], sm[:, 8 + NB:9 + NB], ALU.subtract, reads=[bsm], writes=[bsm])
            return sm, bsm

        def mask_tile(i, sc, bsc, sm, bsm):
            nk = (i + 1) * 128
            mb, bmb = mb_r.next()
            TS(g, fw.dve, mb[:, 0:nk], sc[:, 0:nk], sm[:, 6:7], NEG, ALU.is_lt, ALU.mult, reads=[bsc, bsm], writes=[bmb])
            mbT, bmbT = mbT_r.next()
            for c0 in range(0, i + 1, 8):
                n_ = min(8, i + 1 - c0)
                tp, btp = ptp.next()
                for c in range(n_):
                    fw.op(fw.pe, "transpose", tp[:, c, :], mb[:, (c0 + c) * 128:(c0 + c + 1) * 128], g.ident[:],
                          reads=[bmb, g.b_ident], writes=[btp], signal=(c == n_ - 1))
                CP(g, fw.act, mbT[:, c0:c0 + n_, :], tp[:, 0:n_, :], reads=[btp], writes=[bmbT])
            fw.dma(fw.sp, MBv[:, 0:i + 1, i * 128:(i + 1) * 128], mbT[:, 0:i + 1, :], reads=[bmbT])

        cur = score_tile(2)
        for i in range(2, NT):
            nxt = score_tile(i + 1) if i + 1 < NT else None
            sm, bsm = bisect_tile(i, cur[0], cur[1])
            mask_tile(i, cur[0], cur[1], sm, bsm)
            cur = nxt
        end_phase(g)


def l1_p3(g):
    fw = g.fw
    with contextlib.ExitStack() as st:
        A = attn_bufs(g, st)
        Bt = sb(g, st, "Bt", [128, 16, 2, 128], BF16)
        bBts = [Buf() for _ in range(32)]
        J = sb(g, st, "antiI", [128, 128], BF16)
        bJ = Buf()
        fw.dma(fw.pool, J[:], g.I.c_antiident[:, :], writes=[bJ])
        with contextlib.ExitStack() as t:
            rbf = sb(g, t, "rbf", [32, 16], F32)
            rbh = sb(g, t, "rbh", [32, 16], BF16)
            rbhf = sb(g, t, "rbhf", [32, 16], F32)
            rbl = sb(g, t, "rbl", [32, 16], BF16)
            oh = sb(g, t, "oh", [32, 384], BF16)
            tb = sb(g, t, "tb", [16, 384], F32)
            b1, b2, b3, b4, b5, b6 = Buf(), Buf(), Buf(), Buf(), Buf(), Buf()
            fw.dma(fw.sp, rbf[:], g.I.rel_bias[:, :], writes=[b1])
            fw.dma(fw.pool, oh[:], g.I.c_oh[:, :], writes=[b5])
            CP(g, fw.dve, rbh[:], rbf[:], reads=[b1], writes=[b2])
            CP(g, fw.dve, rbhf[:], rbh[:], reads=[b2], writes=[b3])
            TT(g, fw.dve, rbl[:], rbf[:], rbhf[:], ALU.subtract, reads=[b1, b3], writes=[b4])
            pt, bpt = A.pS.next()
            MM(g, pt[0:16, 0:384], rbh[:], oh[:], True, False, reads=[b2, b5], writes=[bpt], signal=False)
            MM(g, pt[0:16, 0:384], rbl[:], oh[:], False, True, reads=[b4, b5], writes=[bpt])
            CP(g, fw.dve, tb[:], pt[0:16, 0:384], reads=[bpt], writes=[b6])
            TS(g, fw.dve, tb[:], tb[:], tb[:, 383:384], None, ALU.subtract, None, reads=[b6], writes=[b6])
            fw.dma(fw.sp, g.TB[:, :], tb[:], reads=[b6], writes=[g.b_TB])
            for hd in range(16):
                for v, base in ((0, 1), (1, 129)):
                    src = bass.AP(tensor=g.TB.tensor, offset=hd * 384 + base, ap=[[1, 128], [1, 128]])
                    fw.dma(fw.pool, Bt[:, hd, v, :], src, reads=[g.b_TB], writes=[bBts[hd * 2 + v]])
            end_phase(g)
        KT2 = sb(g, st, "KT2", [128, 4, S], BF16)
        bKT2 = Buf()
        for n in range(4):
            fw.dma(fw.sp, KT2[0:64, n, :], g.KT1[n * 64:(n + 1) * 64, :], writes=[bKT2])
            fw.dma(fw.sp, KT2[64:128, n, :], g.KT1[n * 64:(n + 1) * 64, :], writes=[bKT2])
        V1 = sb(g, st, "V1all", [128, NT, 4 * 65], BF16)
        bV1 = Buf()
        Vv = g.V1.rearrange("(c p) h e -> p c (h e)", p=128)
        fw.dma(fw.sp, V1[:], Vv[:, :, :], writes=[bV1])
        Qe_r = rot_sb(g, st, "Qe", [128, 8, 512], BF16, 2)
        Qo_r = rot_sb(g, st, "Qo", [128, 8, 512], BF16, 2)
        for (t_, b_) in Qe_r.items + Qo_r.items:
            fw.op(fw.pool, "memset", t_[:], 0.0, writes=[b_])
        mb_r = rot_sb(g, st, "mbq", [128, NT, 512], BF16, 2)
        QTv = g.QT1.rearrange("(p r) t -> r p t", r=128)
        MBv = g.MBT.rearrange("c k q -> k c q")
        for qb in range(NS):
            nkc = 4 * qb + 4
            Qe, bQe = Qe_r.next()
            Qo, bQo = Qo_r.next()
            mbq, bmbq = mb_r.next()
            fw.dma(fw.sp, Qe[0:64, :, :], QTv[0:64, :, qb * 512:(qb + 1) * 512], writes=[bQe])
            fw.dma(fw.sp, Qo[64:128, :, :], QTv[64:128, :, qb * 512:(qb + 1) * 512], writes=[bQo])
            for c0 in range(0, nkc, 8):
                c1 = min(nkc, c0 + 8)
                fw.dma(fw.sp, mbq[:, c0:c1, :], MBv[:, c0:c1, qb * 512:(qb + 1) * 512], writes=[bmbq])
            steps = []
            for kc in range(nkc):
                r = kc - 4 * qb
                steps.append((kc, 128 * r if r > 0 else 0))
            for hd in range(16):
                n = hd // 4
                Qb, bQb = (Qe, bQe) if hd % 2 == 0 else (Qo, bQo)
                po, bpo = A.pO.next()

                def S_fn(kc, off, pS, bpS, hd=hd, n=n, Qb=Qb, bQb=bQb, mbq=mbq, bmbq=bmbq, qb=qb):
                    adds = []
                    for v in (0, 1):
                        j = kc + v - 4 * qb
                        if 0 <= j < 4 and j * 128 >= off:
                            adds.append((v, j))
                    MM(g, pS[:, off:512], KT2[:, n, kc * 128:(kc + 1) * 128], Qb[:, hd // 2, off:512], True, False,
                       reads=[bKT2, bQb], writes=[bpS], signal=False)
                    MM(g, pS[:, off:512], g.ident[:], mbq[:, kc, off:512], False, len(adds) == 0, reads=[g.b_ident, bmbq], writes=[bpS])
                    for ai, (v, j) in enumerate(adds):
                        MM(g, pS[:, j * 128:(j + 1) * 128], J[:], Bt[:, hd, v, :], False, ai == len(adds) - 1, reads=[bJ, bBts[hd * 2 + v]], writes=[bpS])

                def V_fn(kc, n=n):
                    return V1[:, kc, n * 65:(n + 1) * 65], [bV1]

                attn_qblock(g, A, steps, S_fn, V_fn, po, bpo)
                attn_epilogue(g, A, po, bpo, hd, qb)
        attn_flush(A)
        end_phase(g)


def layer1(g):
    if "skip_l1" in g.debug:
        return
    src = g.xc
    l1_p1(g, src)
    if "stop_q1" in g.debug:
        return
    l1_p2(g)
    if "stop_q2" in g.debug:
        return
    l1_p3(g)
    if "stop_q3" in g.debug:
        return
    with contextlib.ExitStack() as st:
        proj_out_ln(g, st, g.OT.rearrange("h d t -> (h d) t"), g.I.dsa_wo, 3, src, g.xa)
        end_phase(g)
    if "stop_q4" in g.debug:
        return
    cross_attn(g, 1, g.xa, g.xb)
    if "stop_q5" in g.debug:
        return
    ffn(g, 1, g.xb, g.out, is_output=True)


def t5_bucket_np(d):
    n = np.maximum(d, 0)
    nf = np.maximum(n, 1).astype(np.float32)
    large = 16 + (np.log(nf / np.float32(16)) / np.float32(math.log(128 / 16)) * np.float32(16)).astype(np.int32)
    large = np.minimum(large, 31)
    return np.where(n < 16, n, large)


def make_consts():
    c = {}
    c["c_ident"] = np.eye(128, dtype=np.float32)
    k = np.arange(128)[:, None]
    q = np.arange(128)[None, :]
    c["c_tri"] = np.where(k <= q, 0.0, NEG).astype(np.float32)
    c["c_trif"] = np.where(np.arange(128)[None, :] <= np.arange(128)[:, None], 0.0, -1e30).astype(np.float32)
    inv = (np.float32(10000.0) ** (-np.arange(0, 32, 2, dtype=np.float32) / np.float32(32))).astype(np.float32)
    invf = np.zeros((32, 2), np.float32)
    invf[:, 0] = np.concatenate([inv, inv])
    invf[:, 1] = np.concatenate([-np.ones(16), np.ones(16)])
    c["c_invf"] = invf
    d = np.arange(-128, 256)
    b = t5_bucket_np(d)
    oh = np.zeros((32, 384), np.float32)
    oh[b, np.arange(384)] = 1.0
    c["c_oh"] = oh
    c["c_antiident"] = np.ascontiguousarray(np.eye(128, dtype=np.float32)[::-1])
    c["c_pow2"] = (2.0 ** -(np.arange(32) + 1.0)).astype(np.float32)[None, :]
    return c


def prep_weights(inp):
    w = {}
    f = lambda a: np.ascontiguousarray(a, dtype=np.float32)
    win = inp["mla_w_in"][0]
    w["mla_in"] = f(np.concatenate([win, win[:, 784:800], win[:, 768:784]], axis=1))
    uq = inp["mla_w_uq"][0].reshape(512, 16, 96)
    nope, x1, x2 = uq[:, :, 0:64], uq[:, :, 64:80], uq[:, :, 80:96]
    main = np.concatenate([nope, x1, x2], axis=2).reshape(512, 1536)
    sw = np.concatenate([nope, x2, x1], axis=2).reshape(512, 1536)
    w["mla_uq"] = f(np.concatenate([main, sw], axis=1))
    ukv = inp["mla_w_ukv"][0].reshape(256, 16, 128)
    w["mla_ukv"] = f(np.concatenate([ukv[:, :, 0:64].reshape(256, 1024), ukv[:, :, 64:128].reshape(256, 1024)], axis=1))
    w["mla_qn"] = f(inp["mla_q_norm"][0][None, :])
    w["mla_kvn"] = f(inp["mla_kv_norm"][0][None, :])
    w["mla_wo"] = f(inp["mla_w_o"][0])
    w["dsa_in"] = f(inp["dsa_w_in"][0])
    w["dsa_wo"] = f(inp["dsa_w_o"][0])
    w["xa_wq"] = f(inp["xa_w_q"])
    w["xa_wkv"] = f(inp["xa_w_kv"])
    w["xa_wo"] = f(inp["xa_w_o"])
    w["ffn_in"] = f(inp["ffn_w_in"])
    w["ffn_dn"] = f(inp["ffn_w_down"])
    w["ln_g"] = f(inp["ln_g"].reshape(6, D))
    w["ln_b"] = f(inp["ln_b"].reshape(6, D))
    w["rel_bias"] = f(inp["rel_bias"])
    return w


_CACHE = {}


def run(inputs, debug=(), n_cores=8):
    key = tuple(debug)
    if key not in _CACHE:
        _CACHE[key] = build_program(debug)
    nc = _CACHE[key]
    shared = prep_weights(inputs)
    shared.update(make_consts())
    in_maps = []
    for b in range(n_cores):
        m = dict(shared)
        m["x"] = np.ascontiguousarray(inputs["x"][b], dtype=np.float32)
        m["mem"] = np.ascontiguousarray(inputs["mem"][b], dtype=np.float32)
        m["pos"] = np.ascontiguousarray(inputs["positions"][b][None, :], dtype=np.int32)
        in_maps.append(m)
    res = run_bass_kernel_spmd(nc, in_maps, core_ids=list(range(n_cores)))
    return res


def kernel(**inputs):
    res = run(inputs)
    return np.stack([np.asarray(r["out"], dtype=np.float32) for r in res.results], axis=0)
```

```python
import contextlib
import math

import numpy as np
import concourse.bass as bass
import concourse.mybir as mybir
from concourse.bass_utils import run_bass_kernel_spmd

F32 = mybir.dt.float32
BF16 = mybir.dt.bfloat16
I32 = mybir.dt.int32
AF = mybir.ActivationFunctionType
ALU = mybir.AluOpType

S = 4096
D = 1024
NT = S // 128
NS = S // 512
ALPHA = 4 ** 0.25
LN_EPS = 1e-5
RMS_EPS = 1e-6
FFN_H = 2816
NEG = -30000.0
TOPK = 256
N_BISECT = 24


class Buf:
    __slots__ = ("name", "w", "r")

    def __init__(self, name=""):
        self.name = name
        self.w = None
        self.r = []


class Eng:
    def __init__(self, name, inorder_safe=False):
        self.name = name
        self.ops = []
        self.sem = None
        self.count = 0
        self.seen = {}
        self.inorder_safe = inorder_safe
        self.pending_unsignaled = False
        self.dma_sems = []
        self.dma_n = 0
        self.dma_evs = []


class FW:
    SEM_EPOCH = 30000

    def __init__(self, nc, stack, n_dma_sems=(32, 12)):
        self.nc = nc
        self.stack = stack
        self.pe = Eng("pe", inorder_safe=True)
        self.act = Eng("act")
        self.dve = Eng("dve")
        self.pool = Eng("pool")
        self.sp = Eng("sp")
        self.engs = [self.pe, self.act, self.dve, self.pool, self.sp]
        self.nsem = 0
        for e in self.engs:
            self._new_sem(e)
        for e, n in ((self.sp, n_dma_sems[0]), (self.pool, n_dma_sems[1])):
            for i in range(n):
                e.dma_sems.append(self._alloc_sem(f"dma_{e.name}_{i}"))
                e.dma_evs.append(None)
        self.out_evs = []
        self.n_ops = 0

    def _alloc_sem(self, name):
        self.nsem += 1
        return self.stack.enter_context(self.nc.semaphore(name))

    def _new_sem(self, e):
        e.sem = self._alloc_sem(f"s_{e.name}_{self.nsem}")
        e.count = 0

    def _need(self, eng, ev, waits):
        if ev is None:
            return
        sem, val, src = ev
        if src is eng and eng.inorder_safe:
            return
        if src is eng:
            assert not (sem is eng.sem and val > eng.count), "self-wait on unsignaled op"
        k = id(sem)
        if eng.seen.get(k, 0) >= val:
            return
        if k in waits:
            if waits[k][1] < val:
                waits[k] = (sem, val)
        else:
            waits[k] = (sem, val)

    def _collect(self, eng, reads, writes):
        waits = {}
        for b in reads:
            self._need(eng, b.w, waits)
        for b in writes:
            self._need(eng, b.w, waits)
            for ev in b.r:
                self._need(eng, ev, waits)
        return waits

    def _commit_waits(self, eng, waits):
        lst = []
        for k, (sem, val) in waits.items():
            eng.seen[k] = max(eng.seen.get(k, 0), val)
            lst.append((sem, val))
        return lst

    def _record(self, ev, reads, writes):
        for b in reads:
            b.r.append(ev)
        for b in writes:
            b.w = ev
            b.r = []

    def op(self, eng, meth, *args, reads=(), writes=(), signal=True, **kw):
        def fn(h, meth=meth, args=args, kw=kw):
            return getattr(h, meth)(*args, **kw)
        waits = self._collect(eng, reads, writes)
        if signal and eng.count >= self.SEM_EPOCH and not eng.pending_unsignaled:
            self._new_sem(eng)
        lst = self._commit_waits(eng, waits)
        if signal:
            eng.count += 1
            ev = (eng.sem, eng.count, eng)
            upd = (eng.sem, 1)
            eng.pending_unsignaled = False
        else:
            eng.pending_unsignaled = True
            ev = (eng.sem, eng.count + 1, eng)
            upd = None
        eng.ops.append((lst, fn, upd))
        self._record(ev, reads, writes)
        self.n_ops += 1
        return ev

    def dma(self, eng, out, in_, reads=(), writes=(), is_output=False):
        waits = self._collect(eng, reads, writes)
        slot = eng.dma_n % len(eng.dma_sems)
        self._need(eng, eng.dma_evs[slot], waits)
        lst = self._commit_waits(eng, waits)
        sem = eng.dma_sems[slot]
        val = 16 * (eng.dma_n // len(eng.dma_sems) + 1)
        ev = (sem, val, None)
        eng.dma_evs[slot] = ev
        eng.dma_n += 1

        def fn(h, out=out, in_=in_):
            return h.dma_start(out=out, in_=in_)
        eng.ops.append((lst, fn, (sem, 16)))
        self._record(ev, reads, writes)
        if is_output:
            self.out_evs.append(ev)
        self.n_ops += 1
        return ev

    def wait_events(self, eng, evs):
        waits = {}
        for ev in evs:
            self._need(eng, ev, waits)
        lst = self._commit_waits(eng, waits)
        if lst:
            eng.ops.append((lst, None, None))

    def barrier(self):
        evs = []
        for e in self.engs:
            assert not e.pending_unsignaled
            if e.count > 0:
                evs.append((e.sem, e.count, None))
            for d in e.dma_evs:
                if d is not None:
                    evs.append(d)
        for e in self.engs:
            self.wait_events(e, evs)

    def emit(self):
        nc = self.nc
        handles = {"pe": "tensor", "act": "scalar", "dve": "vector", "pool": "gpsimd", "sp": "sync"}
        with nc.Block() as block:
            def runner(eng):
                ops = eng.ops

                def body(h):
                    for (waits, fn, upd) in ops:
                        for (sem, val) in waits:
                            h.wait_ge(sem, val)
                        if fn is not None:
                            ins = fn(h)
                            if upd is not None:
                                ins.then_inc(upd[0], upd[1])
                return body
            for e in self.engs:
                getattr(block, handles[e.name])(runner(e))
        for e in self.engs:
            e.ops = []


class Rot:
    def __init__(self, items):
        self.items = items
        self.i = 0

    def next(self):
        it = self.items[self.i % len(self.items)]
        self.i += 1
        return it


class Ctx:
    pass


def build_program(debug=()):
    nc = bass.Bass("TRN2", target_bir_lowering=False)
    g = Ctx()
    g.nc = nc
    g.debug = debug

    def din(name, shape, dt=F32):
        return nc.dram_tensor(name, list(shape), dt, kind="ExternalInput").ap()

    def dscr(name, shape, dt):
        kind = "ExternalOutput" if name in debug else "Internal"
        return nc.dram_tensor(name, list(shape), dt, kind=kind).ap()

    I = Ctx()
    g.I = I
    I.x = din("x", [S, D])
    I.mem = din("mem", [256, D])
    I.pos = din("pos", [1, S], I32)
    I.rel_bias = din("rel_bias", [32, 16])
    I.mla_in = din("mla_in", [D, 832])
    I.mla_qn = din("mla_qn", [1, 512])
    I.mla_kvn = din("mla_kvn", [1, 256])
    I.mla_uq = din("mla_uq", [512, 3072])
    I.mla_ukv = din("mla_ukv", [256, 2048])
    I.mla_wo = din("mla_wo", [D, D])
    I.dsa_in = din("dsa_in", [D, 2120])
    I.dsa_wo = din("dsa_wo", [D, D])
    I.xa_wq = din("xa_wq", [2, D, D])
    I.xa_wkv = din("xa_wkv", [2, D, 2 * D])
    I.xa_wo = din("xa_wo", [2, D, D])
    I.ffn_in = din("ffn_in", [2, D, 2 * FFN_H])
    I.ffn_dn = din("ffn_dn", [2, FFN_H, D])
    I.ln_g = din("ln_g", [6, D])
    I.ln_b = din("ln_b", [6, D])
    I.c_ident = din("c_ident", [128, 128])
    I.c_tri = din("c_tri", [128, 128])
    I.c_trif = din("c_trif", [128, 128])
    I.c_invf = din("c_invf", [32, 2])
    I.c_oh = din("c_oh", [32, 384])
    I.c_pow2 = din("c_pow2", [1, 32])
    I.c_antiident = din("c_antiident", [128, 128])

    g.out = nc.dram_tensor("out", [S, D], F32, kind="ExternalOutput").ap()
    g.xa = dscr("xa", [S, D], F32)
    g.xb = dscr("xb", [S, D], F32)
    g.xc = dscr("xc", [S, D], F32)
    g.QT = dscr("QT", [16, 96, S], BF16)
    g.KT = dscr("KT", [16, 64, S], BF16)
    g.KR = dscr("KR", [32, S], BF16)
    g.V = dscr("V", [S, 16, 65], BF16)
    g.OT = dscr("OT", [16, 64, S], BF16)
    g.QI = dscr("QI", [512, S], BF16)
    g.KI = dscr("KI", [64, S], BF16)
    g.WI = dscr("WI", [S, 8], F32)
    g.MBT = dscr("MBT", [NT, 128, S], BF16)
    g.TB = dscr("TB", [16, 384], F32)
    g.b_TB = Buf("TB")
    g.QT1 = dscr("QT1", [1024, S], BF16)
    g.KT1 = dscr("KT1", [256, S], BF16)
    g.V1 = dscr("V1", [S, 4, 65], BF16)

    with contextlib.ExitStack() as gst:
        fw = FW(nc, gst)
        g.fw = fw
        g.gst = gst
        setup_globals(g)
        layer0(g)
        layer1(g)
        fw.wait_events(fw.sp, fw.out_evs)
        fw.emit()
    g.n_ops = fw.n_ops
    return nc


def _uname(g, name):
    g.uid = getattr(g, "uid", 0) + 1
    return f"{name}_{g.uid}"


def sb(g, st, name, shape, dt):
    return st.enter_context(g.nc.sbuf_tensor(_uname(g, name), list(shape), dt))


def ps(g, st, name, shape, dt=F32):
    return st.enter_context(g.nc.psum_tensor(_uname(g, name), list(shape), dt))


def rot_sb(g, st, name, shape, dt, n):
    return Rot([(sb(g, st, f"{name}{i}", shape, dt), Buf(f"{name}{i}")) for i in range(n)])


def rot_ps(g, st, name, shape, dt, n):
    return Rot([(ps(g, st, f"{name}{i}", shape, dt), Buf(f"{name}{i}")) for i in range(n)])


def end_phase(g):
    g.fw.barrier()
    g.fw.emit()


def MM(g, out, lhsT, rhs, start, stop, reads, writes, signal=None):
    g.fw.op(g.fw.pe, "matmul", out, lhsT, rhs, start=start, stop=stop, reads=reads, writes=writes,
            signal=(stop if signal is None else signal))


def ACT(g, out, in_, func, reads, writes, **kw):
    g.fw.op(g.fw.act, "activation", out=out, in_=in_, func=func, reads=reads, writes=writes, **kw)


def TT(g, eng, out, in0, in1, op, reads, writes):
    g.fw.op(eng, "tensor_tensor", out=out, in0=in0, in1=in1, op=op, reads=reads, writes=writes)


def STT(g, eng, out, in0, scalar, in1, op0, op1, reads, writes):
    g.fw.op(eng, "scalar_tensor_tensor", out=out, in0=in0, scalar=scalar, in1=in1, op0=op0, op1=op1, reads=reads, writes=writes)


def TS(g, eng, out, in0, s1, s2, op0, op1, reads, writes, **kw):
    if op1 is None:
        g.fw.op(eng, "tensor_scalar", out=out, in0=in0, scalar1=s1, scalar2=s2, op0=op0, reads=reads, writes=writes, **kw)
    else:
        g.fw.op(eng, "tensor_scalar", out=out, in0=in0, scalar1=s1, scalar2=s2, op0=op0, op1=op1, reads=reads, writes=writes, **kw)


def CP(g, eng, out, in_, reads, writes):
    if eng is g.fw.act:
        g.fw.op(eng, "copy", out, in_, reads=reads, writes=writes)
    else:
        g.fw.op(eng, "tensor_copy", out, in_, reads=reads, writes=writes)


def load_w(g, dst, bdst, src, K, ncols, col0=0, split=1):
    fw = g.fw
    nch = K // 128
    v = src.rearrange("(c p) n -> p c n", p=128)
    step = (nch + split - 1) // split
    for c0 in range(0, nch, step):
        c1 = min(nch, c0 + step)
        fw.dma(fw.pool, dst[:, c0:c1, :], v[:, c0:c1, col0:col0 + ncols], writes=[bdst])


def setup_globals(g):
    fw, st = g.fw, g.gst
    g.ident = sb(g, st, "ident", [128, 128], BF16)
    g.b_ident = Buf("ident")
    g.tri = sb(g, st, "tri", [128, 128], BF16)
    g.b_tri = Buf("tri")
    g.ones = sb(g, st, "ones", [128, 64], BF16)
    g.b_ones = Buf("ones")
    fw.dma(fw.pool, g.ident[:], g.I.c_ident[:, :], writes=[g.b_ident])
    fw.dma(fw.pool, g.tri[:], g.I.c_tri[:, :], writes=[g.b_tri])
    fw.op(fw.dve, "memset", g.ones[:], 1.0, writes=[g.b_ones])


def transpose_into(g, src, bsrc, xbf_rot, tp_rot, dstT, bdstT, j, nch, src_is_bf16=False):
    fw = g.fw
    if src_is_bf16:
        xb, bxb = src, bsrc
    else:
        xb, bxb = xbf_rot.next()
        CP(g, fw.act, xb[:, 0:nch * 128], src[:, 0:nch * 128], reads=[bsrc], writes=[bxb])
    tp, btp = tp_rot.next()
    for c in range(nch):
        fw.op(fw.pe, "transpose", tp[:, c, :], xb[:, c * 128:(c + 1) * 128], g.ident[:],
              reads=[bxb, g.b_ident], writes=[btp], signal=(c == nch - 1))
    CP(g, fw.dve, dstT[:, 0:nch, j * 128:(j + 1) * 128], tp[:, 0:nch, :], reads=[btp], writes=[bdstT])


class XLoader:
    def __init__(self, g, st, n_xs=4, n_xT=1, n_xbf=2):
        self.g = g
        self.xs = rot_sb(g, st, "xs", [128, D], F32, n_xs)
        self.xbf = rot_sb(g, st, "xbf", [128, D], BF16, n_xbf)
        self.xT = Rot([(sb(g, st, f"xT{i}", [128, 8, 512], BF16), [Buf(f"xT{i}_{j}") for j in range(4)]) for i in range(n_xT)])
        self.tp = rot_ps(g, st, "tpx", [128, 8, 128], BF16, 2)

    def load(self, src, s):
        g, fw = self.g, self.g.fw
        xT, bxT = self.xT.next()
        tiles = []
        for j in range(4):
            t0 = s * 512 + j * 128
            xs, bxs = self.xs.next()
            fw.dma(fw.sp, xs[:], src[t0:t0 + 128, :], writes=[bxs])
            tiles.append((xs, bxs))
            transpose_into(g, xs, bxs, self.xbf, self.tp, xT, bxT[j], j, 8)
        return xT, bxT, tiles


class LNUnit:
    def __init__(self, g, st, ln_idx):
        self.g = g
        fw = g.fw
        self.gam = sb(g, st, "ln_gam", [128, D], F32)
        self.bet = sb(g, st, "ln_bet", [128, D], F32)
        self.bgam, self.bbet = Buf("ln_g"), Buf("ln_b")
        fw.dma(fw.sp, self.gam[:], g.I.ln_g[ln_idx:ln_idx + 1, :].partition_broadcast(128), writes=[self.bgam])
        fw.dma(fw.sp, self.bet[:], g.I.ln_b[ln_idx:ln_idx + 1, :].partition_broadcast(128), writes=[self.bbet])
        self.stats = rot_sb(g, st, "ln_st", [128, 2, 6], F32, 3)
        self.mv = rot_sb(g, st, "ln_mv", [128, 4], F32, 3)

    def run(self, xs, bxs, y0, by0, y1, by1, dst, is_output=False):
        g, fw = self.g, self.g.fw
        stt, bst = self.stats.next()
        mv, bmv = self.mv.next()
        for hf, (y, by) in enumerate(((y0, by0), (y1, by1))):
            sl = slice(hf * 512, (hf + 1) * 512)
            STT(g, fw.dve, xs[:, sl], xs[:, sl], ALPHA, y[:, 0:512], ALU.mult, ALU.add, reads=[bxs, by], writes=[bxs])
        for hf in range(2):
            fw.op(fw.dve, "bn_stats", stt[:, hf, :], xs[:, hf * 512:(hf + 1) * 512], reads=[bxs], writes=[bst])
        fw.op(fw.dve, "bn_aggr", mv[:, 0:2], stt[:], reads=[bst], writes=[bmv])
        ACT(g, mv[:, 2:3], mv[:, 1:2], AF.Ln, reads=[bmv], writes=[bmv], bias=LN_EPS, scale=1.0)
        ACT(g, mv[:, 2:3], mv[:, 2:3], AF.Exp, reads=[bmv], writes=[bmv], scale=-0.5)
        STT(g, fw.dve, mv[:, 3:4], mv[:, 0:1], -1.0, mv[:, 2:3], ALU.mult, ALU.mult, reads=[bmv], writes=[bmv])
        ACT(g, xs[:], xs[:], AF.Identity, reads=[bxs, bmv], writes=[bxs], bias=mv[:, 3:4], scale=mv[:, 2:3])
        TT(g, fw.pool, xs[:], xs[:], self.gam[:], ALU.mult, reads=[bxs, self.bgam], writes=[bxs])
        TT(g, fw.pool, xs[:], xs[:], self.bet[:], ALU.add, reads=[bxs, self.bbet], writes=[bxs])
        fw.dma(fw.sp, dst, xs[:], reads=[bxs], is_output=is_output)


def proj_out_ln(g, st, OTrows, w_src, ln_idx, x_src, x_dst, is_output=False):
    fw = g.fw
    Wo = sb(g, st, "Wo", [128, 8, D], BF16)
    bWo = Buf("Wo")
    load_w(g, Wo, bWo, w_src, D, D, split=2)
    ln = LNUnit(g, st, ln_idx)
    ots = rot_sb(g, st, "ots", [128, 8, 512], BF16, 3)
    xsr = rot_sb(g, st, "pxs", [128, D], F32, 5)
    py = rot_ps(g, st, "py", [128, 512], F32, 4)
    otv = OTrows.rearrange("(p r) t -> r p t", r=128)
    ot_tiles = {}
    x_tiles = {}

    def prefetch(idx):
        if idx >= NT:
            return
        s_, j_ = divmod(idx, 4)
        if j_ == 0:
            ot_tiles[s_] = ots.next()
            fw.dma(fw.sp, ot_tiles[s_][0][:], otv[:, :, s_ * 512:(s_ + 1) * 512], writes=[ot_tiles[s_][1]])
        x_tiles[idx] = xsr.next()
        fw.dma(fw.sp, x_tiles[idx][0][:], x_src[idx * 128:(idx + 1) * 128, :], writes=[x_tiles[idx][1]])

    prefetch(0)
    prefetch(1)
    for idx in range(NT):
        prefetch(idx + 2)
        s_, j = divmod(idx, 4)
        ot, bot = ot_tiles[s_]
        xs, bxs = x_tiles.pop(idx)
        t0 = idx * 128
        ys = []
        for hf in range(2):
            y, by = py.next()
            for p in range(8):
                MM(g, y[:], ot[:, p, j * 128:(j + 1) * 128], Wo[:, p, hf * 512:(hf + 1) * 512], p == 0, p == 7, reads=[bot, bWo], writes=[by])
            ys.append((y, by))
        ln.run(xs, bxs, ys[0][0], ys[0][1], ys[1][0], ys[1][1], x_dst[t0:t0 + 128, :], is_output=is_output)


def attn_bufs(g, st, n_po=4, filler=False):
    A = Ctx()
    A.pJ = ps(g, st, "pJunk", [128, 512], F32) if filler else None
    A.bJ = Buf("pJunk")
    A.pS = rot_ps(g, st, "pS", [128, 512], F32, 3)
    A.pO = rot_ps(g, st, "pO", [65, 512], F32, n_po)
    A.pb = rot_ps(g, st, "pB", [64, 512], F32, 1)
    A.pT = rot_sb(g, st, "pT", [128, 512], BF16, 4)
    A.rd = rot_sb(g, st, "rd", [65, 512], F32, 2)
    A.rh = rot_sb(g, st, "rh", [65, 512], BF16, 2)
    A.rl = rot_sb(g, st, "rl", [65, 512], BF16, 2)
    A.rf = rot_sb(g, st, "rf", [65, 512], F32, 2)
    A.rb = rot_sb(g, st, "rb", [64, 512], F32, 2)
    A.ot = rot_sb(g, st, "ot", [64, 512], BF16, 2)
    A.deferred = []
    return A


def attn_epilogue(g, A, po, bpo, h, qb):
    fw = g.fw
    rd, brd = A.rd.next()
    fw.op(fw.dve, "reciprocal", rd[64:65, :], po[64:65, :], reads=[bpo], writes=[brd])
    rh, brh = A.rh.next()
    rl, brl = A.rl.next()
    rf, brf = A.rf.next()
    CP(g, fw.dve, rh[64:65, :], rd[64:65, :], reads=[brd], writes=[brh])
    CP(g, fw.dve, rf[64:65, :], rh[64:65, :], reads=[brh], writes=[brf])
    TT(g, fw.dve, rl[64:65, :], rd[64:65, :], rf[64:65, :], ALU.subtract, reads=[brd, brf], writes=[brl])
    def part2():
        pb, bpb = A.pb.next()
        MM(g, pb[:], g.ones[64:65, 0:64], rh[64:65, :], True, False, reads=[g.b_ones, brh], writes=[bpb], signal=False)
        MM(g, pb[:], g.ones[64:65, 0:64], rl[64:65, :], False, True, reads=[g.b_ones, brl], writes=[bpb])
        rb, brb = A.rb.next()
        CP(g, fw.dve, rb[:], pb[:], reads=[bpb], writes=[brb])
        ot, bot = A.ot.next()
        TT(g, fw.dve, ot[:], po[0:64, :], rb[:], ALU.mult, reads=[bpo, brb], writes=[bot])
        fw.dma(fw.sp, g.OT[h, :, qb * 512:(qb + 1) * 512], ot[:], reads=[bot])
    A.deferred.append(part2)


def attn_flush(A):
    while A.deferred:
        A.deferred.pop(0)()


def rope_tables(g, st):
    fw = g.fw
    C = sb(g, st, "ropeC", [96, S], F32)[64:96, :]
    Sm = sb(g, st, "ropeS", [96, S], F32)[64:96, :]
    bC, bS = Buf("ropeC"), Buf("ropeS")
    with contextlib.ExitStack() as t:
        posi = sb(g, t, "posi", [96, S], I32)[64:96, :]
        a = sb(g, t, "r_a", [96, S], F32)[64:96, :]
        kf = sb(g, t, "r_k", [96, S], F32)[64:96, :]
        ki = sb(g, t, "r_ki", [96, S], I32)[64:96, :]
        invf = sb(g, t, "invf", [96, 2], F32)[64:96, :]
        bposi, ba, bkf, bki, binvf = Buf(), Buf(), Buf(), Buf(), Buf()
        dve = fw.dve
        fw.dma(fw.sp, posi, g.I.pos[0:1, :].partition_broadcast(32), writes=[bposi])
        fw.dma(fw.sp, invf[:], g.I.c_invf, writes=[binvf])
        CP(g, dve, a, posi, reads=[bposi], writes=[ba])
        TS(g, dve, a, a, invf[:, 0:1], None, ALU.mult, None, reads=[ba, binvf], writes=[ba])
        TS(g, dve, kf, a, 1.0 / (2 * math.pi), 0.5, ALU.mult, ALU.add, reads=[ba], writes=[bkf])
        CP(g, dve, ki, kf, reads=[bkf], writes=[bki])
        CP(g, dve, kf, ki, reads=[bki], writes=[bkf])
        C1 = 6.28125
        C2 = 2 * math.pi - C1
        STT(g, dve, a, kf, -C1, a, ALU.mult, ALU.add, reads=[bkf, ba], writes=[ba])
        STT(g, dve, a, kf, -C2, a, ALU.mult, ALU.add, reads=[bkf, ba], writes=[ba])
        TS(g, dve, kf, a, -math.pi, 2 * math.pi, ALU.is_lt, ALU.mult, reads=[ba], writes=[bkf])
        TT(g, dve, a, a, kf, ALU.add, reads=[ba, bkf], writes=[ba])
        TS(g, dve, kf, a, math.pi, -2 * math.pi, ALU.is_gt, ALU.mult, reads=[ba], writes=[bkf])
        TT(g, dve, a, a, kf, ALU.add, reads=[ba, bkf], writes=[ba])
        TS(g, dve, a, a, -3.1415925, 3.1415925, ALU.max, ALU.min, reads=[ba], writes=[ba])
        ACT(g, Sm, a, AF.Sin, reads=[ba], writes=[bS])
        STT(g, dve, kf, a, -1.0, a, ALU.mult, ALU.max, reads=[ba], writes=[bkf])
        TS(g, dve, kf, kf, -1.0, math.pi / 2, ALU.mult, ALU.add, reads=[bkf], writes=[bkf])
        ACT(g, C, kf, AF.Sin, reads=[bkf], writes=[bC])
        TS(g, dve, Sm, Sm, invf[:, 1:2], None, ALU.mult, None, reads=[bS, binvf], writes=[bS])
        end_phase(g)
    return C, bC, Sm, bS


def l0_p1(g):
    fw = g.fw
    with contextlib.ExitStack() as st:
        C, bC, Sm, bS = rope_tables(g, st)
        Win = sb(g, st, "Win", [128, 8, 832], BF16)
        Wuq = sb(g, st, "Wuq", [128, 4, 3072], BF16)
        Wukv = sb(g, st, "Wukv", [128, 2, 2048], BF16)
        bWin, bWuq, bWukv = Buf(), Buf(), Buf()
        load_w(g, Win, bWin, g.I.mla_in, D, 832)
        load_w(g, Wuq, bWuq, g.I.mla_uq, 512, 3072)
        load_w(g, Wukv, bWukv, g.I.mla_ukv, 256, 2048)
        gq = sb(g, st, "gq", [128, 512], F32)
        gkv = sb(g, st, "gkv", [128, 256], F32)
        bgq, bgkv = Buf(), Buf()
        fw.dma(fw.sp, gq[:], g.I.mla_qn[0:1, :].partition_broadcast(128), writes=[bgq])
        fw.dma(fw.sp, gkv[:], g.I.mla_kvn[0:1, :].partition_broadcast(128), writes=[bgkv])
        xl = XLoader(g, st, n_xs=3, n_xT=2)
        ph = rot_ps(g, st, "ph", [128, 512], F32, 2)
        pv = rot_ps(g, st, "pv", [128, 512], F32, 2)
        pq = rot_ps(g, st, "pq", [128, 512], F32, 2)
        nrm = rot_sb(g, st, "nrm", [128, 768], BF16, 2)
        junk = rot_sb(g, st, "junk", [128, 512], BF16, 2)
        ss = rot_sb(g, st, "ss", [128, 4], F32, 4)
        nT = Rot([(sb(g, st, f"nT{i}", [128, 6, 512], BF16), [Buf() for j in range(4)]) for i in range(2)])
        vsb = rot_sb(g, st, "vsb", [128, 16, 65], BF16, 2)
        for (t, b) in vsb.items:
            fw.op(fw.pool, "memset", t[:], 1.0, writes=[b])
        t1 = rot_sb(g, st, "t1", [96, 512], F32, 2)
        t2 = rot_sb(g, st, "t2", [96, 512], F32, 2)
        qt = rot_sb(g, st, "qt", [96, 512], BF16, 3)
        kt = rot_sb(g, st, "kt", [128, 512], BF16, 2)
        kr = rot_sb(g, st, "kr", [96, 512], BF16, 2)
        qscale = 96 ** -0.5
        KTv = g.KT.rearrange("h d t -> (h d) t")
        pre = xl.load(g.I.x, 0)
        for s in range(NS):
            xT, bxT, tiles = pre
            nt, bnt = nT.next()
            cols = slice(s * 512, (s + 1) * 512)
            for j in range(4):
                tsl = slice(j * 128, (j + 1) * 128)
                hq, bhq = ph.next()
                hkv, bhkv = ph.next()
                for c in range(8):
                    MM(g, hq[:], xT[:, c, tsl], Win[:, c, 0:512], c == 0, c == 7, reads=[bxT[j], bWin], writes=[bhq])
                for c in range(8):
                    MM(g, hkv[:, 0:256], xT[:, c, tsl], Win[:, c, 512:768], c == 0, c == 7, reads=[bxT[j], bWin], writes=[bhkv])
                sq, bsq = ss.next()
                jk, bjk = junk.next()
                ACT(g, jk[:], hq[:], AF.Square, reads=[bhq], writes=[bjk, bsq], accum_out=sq[:, 0:1])
                ACT(g, jk[:, 0:256], hkv[:, 0:256], AF.Square, reads=[bhkv], writes=[bjk, bsq], accum_out=sq[:, 1:2])
                ACT(g, sq[:, 2:3], sq[:, 0:1], AF.Ln, reads=[bsq], writes=[bsq], bias=RMS_EPS, scale=1.0 / 512)
                ACT(g, sq[:, 3:4], sq[:, 1:2], AF.Ln, reads=[bsq], writes=[bsq], bias=RMS_EPS, scale=1.0 / 256)
                ACT(g, sq[:, 2:4], sq[:, 2:4], AF.Exp, reads=[bsq], writes=[bsq], scale=-0.5)
                nr, bnr = nrm.next()
                STT(g, fw.dve, nr[:, 0:512], hq[:], sq[:, 2:3], gq[:], ALU.mult, ALU.mult, reads=[bhq, bsq, bgq], writes=[bnr])
                STT(g, fw.dve, nr[:, 512:768], hkv[:, 0:256], sq[:, 3:4], gkv[:], ALU.mult, ALU.mult, reads=[bhkv, bsq, bgkv], writes=[bnr])
                transpose_into(g, nr, bnr, None, xl.tp, nt, bnt[j], j, 6, src_is_bf16=True)
                vt, bvt = vsb.next()
                for hf in range(2):
                    p_, bp_ = pv.next()
                    for r in range(2):
                        MM(g, p_[:], nt[:, 4 + r, tsl], Wukv[:, r, 1024 + hf * 512:1024 + (hf + 1) * 512], r == 0, r == 1, reads=[bnt[j], bWukv], writes=[bp_])
                    CP(g, fw.act, vt[:, hf * 8:(hf + 1) * 8, 0:64], p_[:].rearrange("p (a b) -> p a b", b=64), reads=[bp_], writes=[bvt])
                t0 = s * 512 + j * 128
                fw.dma(fw.sp, g.V[t0:t0 + 128, :, :], vt[:], reads=[bvt])
            if s + 1 < NS:
                pre = xl.load(g.I.x, s + 1)
            pa, bpa = pq.next()
            pb_, bpb_ = pq.next()
            for c in range(8):
                MM(g, pa[0:96, :], Win[:, c, 704:800], xT[:, c, :], c == 0, c == 7, reads=bxT + [bWin], writes=[bpa])
            for c in range(8):
                MM(g, pb_[0:96, :], Win[:, c, 736:832], xT[:, c, :], c == 0, c == 7, reads=bxT + [bWin], writes=[bpb_])
            a1, ba1 = t1.next()
            a2, ba2 = t2.next()
            TT(g, fw.dve, a1[64:96, :], pa[64:96, :], C[:, cols], ALU.mult, reads=[bpa, bC], writes=[ba1])
            TT(g, fw.dve, a2[64:96, :], pb_[64:96, :], Sm[:, cols], ALU.mult, reads=[bpb_, bS], writes=[ba2])
            krt, bkrt = kr.next()
            TT(g, fw.pool, krt[64:96, :], a1[64:96, :], a2[64:96, :], ALU.add, reads=[ba1, ba2], writes=[bkrt])
            fw.dma(fw.sp, g.KR[:, cols], krt[64:96, :], reads=[bkrt])
            for hd in range(16):
                pm, bpm = pq.next()
                psw, bpsw = pq.next()
                for r in range(4):
                    MM(g, pm[0:96, :], Wuq[:, r, hd * 96:(hd + 1) * 96], nt[:, r, :], r == 0, r == 3, reads=bnt + [bWuq], writes=[bpm])
                for r in range(4):
                    MM(g, psw[0:96, :], Wuq[:, r, 1536 + hd * 96:1536 + (hd + 1) * 96], nt[:, r, :], r == 0, r == 3, reads=bnt + [bWuq], writes=[bpsw])
                q_, bq_ = qt.next()
                a1, ba1 = t1.next()
                a2, ba2 = t2.next()
                ACT(g, q_[0:64, :], pm[0:64, :], AF.Copy, reads=[bpm], writes=[bq_], scale=qscale)
                STT(g, fw.dve, a1[64:96, :], pm[64:96, :], qscale, C[:, cols], ALU.mult, ALU.mult, reads=[bpm, bC], writes=[ba1])
                STT(g, fw.dve, a2[64:96, :], psw[64:96, :], qscale, Sm[:, cols], ALU.mult, ALU.mult, reads=[bpsw, bS], writes=[ba2])
                TT(g, fw.pool, q_[64:96, :], a1[64:96, :], a2[64:96, :], ALU.add, reads=[ba1, ba2], writes=[bq_])
                fw.dma(fw.sp, g.QT[hd, :, cols], q_[:], reads=[bq_])
            for p in range(8):
                pk, bpk = pq.next()
                for r in range(2):
                    MM(g, pk[:], Wukv[:, r, p * 128:(p + 1) * 128], nt[:, 4 + r, :], r == 0, r == 1, reads=bnt + [bWukv], writes=[bpk])
                k_, bk_ = kt.next()
                CP(g, fw.act, k_[:], pk[:], reads=[bpk], writes=[bk_])
                fw.dma(fw.sp, KTv[p * 128:(p + 1) * 128, cols], k_[:], reads=[bk_])
        end_phase(g)


def attn_qblock(g, A, steps, S_fn, V_fn, po, bpo):
    fw = g.fw
    n = len(steps)
    pend = []

    def do_pv(item):
        (i, kc, off), pT, bpT = item
        lhsT, vreads = V_fn(kc)
        MM(g, po[:, off:512], lhsT, pT[:, off:512], i == 0, i == n - 1, reads=[bpT] + vreads, writes=[bpo])

    for i, (kc, off) in enumerate(steps):
        pS, bpS = A.pS.next()
        S_fn(kc, off, pS, bpS)
        if A.pJ is not None:
            A.filler_fn(kc)
        pT, bpT = A.pT.next()
        ACT(g, pT[:, off:512], pS[:, off:512], AF.Exp, reads=[bpS], writes=[bpT])
        pend.append(((i, kc, off), pT, bpT))
        if len(pend) > 2:
            do_pv(pend.pop(0))
        if i == min(3, n - 1):
            attn_flush(A)
    while pend:
        do_pv(pend.pop(0))


def l0_p2(g):
    fw = g.fw
    with contextlib.ExitStack() as st:
        A = attn_bufs(g, st, n_po=4, filler=False)
        Kh = rot_sb(g, st, "Kh", [96, S], BF16, 2)
        Qh = rot_sb(g, st, "Qh", [96, S], BF16, 2)
        Vall = sb(g, st, "Vall", [128, NT, 16 * 65], BF16)
        bVall = Buf("Vall")
        Vv = g.V.rearrange("(c p) h e -> p c (h e)", p=128)
        for c0 in range(0, NT, 8):
            fw.dma(fw.sp, Vall[:, c0:c0 + 8, :], Vv[:, c0:c0 + 8, :], writes=[bVall])
        Kbufs = {}

        def load_head(hd):
            K_, bK = Kh.next()
            Q_, bQ = Qh.next()
            bKr = Kbufs.setdefault(id(K_), Buf("Kr"))
            fw.dma(fw.sp, K_[64:96, :], g.KR[:, :], writes=[bKr])
            fw.dma(fw.sp, K_[0:64, :], g.KT[hd, :, :], writes=[bK])
            fw.dma(fw.sp, Q_[:], g.QT[hd, :, :], writes=[bQ])
            return K_, bK, bKr, Q_, bQ

        nxt_head = load_head(0)
        for hd in range(16):
            K_, bK, bKr, Q_, bQ = nxt_head
            if hd + 1 < 16:
                nxt_head = load_head(hd + 1)
            for qb in range(NS):
                po, bpo = A.pO.next()
                steps = []
                for kc in range(4 * qb + 4):
                    r = kc - 4 * qb
                    steps.append((kc, 128 * r if r > 0 else 0))

                def S_fn(kc, off, pS, bpS, K_=K_, Q_=Q_, bK=bK, bKr=bKr, bQ=bQ, qb=qb):
                    r = kc - 4 * qb
                    ks = slice(kc * 128, (kc + 1) * 128)
                    q0 = qb * 512
                    if r < 0:
                        MM(g, pS[:, 0:512], K_[:, ks], Q_[:, q0:q0 + 512], True, True, reads=[bK, bKr, bQ], writes=[bpS])
                    else:
                        MM(g, pS[:, off:off + 128], K_[:, ks], Q_[:, q0 + off:q0 + off + 128], True, False, reads=[bK, bKr, bQ], writes=[bpS], signal=False)
                        last = off + 128 >= 512
                        MM(g, pS[:, off:off + 128], g.ident[:], g.tri[:], False, True, reads=[g.b_ident, g.b_tri], writes=[bpS], signal=last)
                        if not last:
                            MM(g, pS[:, off + 128:512], K_[:, ks], Q_[:, q0 + off + 128:q0 + 512], True, True, reads=[bK, bKr, bQ], writes=[bpS])

                def V_fn(kc, hd=hd):
                    return Vall[:, kc, hd * 65:(hd + 1) * 65], [bVall]

                def filler_fn(kc, K_=K_, Q_=Q_, bK=bK, bKr=bKr, bQ=bQ, qb=qb):
                    MM(g, A.pJ[:, 0:384], K_[:, kc * 128:(kc + 1) * 128], Q_[:, qb * 512:qb * 512 + 384], True, True,
                       reads=[bK, bKr, bQ], writes=[A.bJ], signal=False)
                A.filler_fn = filler_fn

                attn_qblock(g, A, steps, S_fn, V_fn, po, bpo)
                attn_epilogue(g, A, po, bpo, hd, qb)
        attn_flush(A)
        end_phase(g)


def layer0(g):
    l0_p1(g)
    if "stop_p1" in g.debug:
        return
    l0_p2(g)
    if "stop_p2" in g.debug:
        return
    with contextlib.ExitStack() as st:
        proj_out_ln(g, st, g.OT.rearrange("h d t -> (h d) t"), g.I.mla_wo, 0, g.I.x, g.xa)
        end_phase(g)
    if "stop_p3" in g.debug:
        return
    cross_attn(g, 0, g.xa, g.xb)
    if "stop_p4" in g.debug:
        return
    ffn(g, 0, g.xb, g.xc, is_output=False)


def cross_attn(g, li, x_src, x_dst):
    fw = g.fw
    with contextlib.ExitStack() as st:
        Wq = sb(g, st, "Wq", [128, 8, D], BF16)
        Wo = sb(g, st, "Wo", [128, 8, D], BF16)
        bWq, bWo = Buf(), Buf()
        KmT = sb(g, st, "KmT", [128, 4, 2, 256], BF16)
        Vm = sb(g, st, "Vm", [128, 2, 4, 257], BF16)
        bKmT, bVm = Buf(), Buf()
        xl = XLoader(g, st, n_xs=9, n_xT=2)
        pA = rot_ps(g, st, "pA", [128, 512], F32, 4)
        py = rot_ps(g, st, "py", [128, 512], F32, 2)
        with contextlib.ExitStack() as t:
            Wkv = sb(g, t, "Wkv", [128, 8, 2 * D], BF16)
            bWkv = Buf()
            load_w(g, Wkv, bWkv, g.I.xa_wkv[li], D, 2 * D, split=4)
            load_w(g, Wq, bWq, g.I.xa_wq[li], D, D, split=2)
            load_w(g, Wo, bWo, g.I.xa_wo[li], D, D, split=2)
            mt = rot_sb(g, t, "memt", [128, D], F32, 2)
            memT = sb(g, t, "memT", [128, 8, 256], BF16)
            bmemT = Buf()
            fw.op(fw.pool, "memset", Vm[:], 1.0, writes=[bVm])
            for mc in range(2):
                m_, bm_ = mt.next()
                fw.dma(fw.sp, m_[:], g.I.mem[mc * 128:(mc + 1) * 128, :], writes=[bm_])
                transpose_into(g, m_, bm_, xl.xbf, xl.tp, memT, bmemT, mc, 8)
            for hd in range(4):
                for e in range(2):
                    p_, bp_ = pA.next()
                    c0 = hd * 256 + e * 128
                    for c in range(8):
                        MM(g, p_[:, 0:256], Wkv[:, c, c0:c0 + 128], memT[:, c, :], c == 0, c == 7, reads=[bWkv, bmemT], writes=[bp_])
                    CP(g, fw.act, KmT[:, hd, e, :], p_[:, 0:256], reads=[bp_], writes=[bKmT])
            for mc in range(2):
                for hf in range(2):
                    p_, bp_ = pA.next()
                    for c in range(8):
                        MM(g, p_[:], memT[:, c, mc * 128:(mc + 1) * 128], Wkv[:, c, D + hf * 512:D + (hf + 1) * 512], c == 0, c == 7, reads=[bWkv, bmemT], writes=[bp_])
                    CP(g, fw.dve, Vm[:, mc, 2 * hf:2 * hf + 2, 0:256], p_[:].rearrange("p (a b) -> p a b", b=256), reads=[bp_], writes=[bVm])
            end_phase(g)
        ln = LNUnit(g, st, li * 3 + 1)
        qT = sb(g, st, "qT", [128, 8, 512], BF16)
        bqT = Buf()
        pTs = [(sb(g, st, f"xpT{i}", [128, 512], BF16), Buf()) for i in range(8)]
        o_r = rot_sb(g, st, "xo", [128, D], BF16, 2)
        rdn = rot_sb(g, st, "xrd", [128, 4], F32, 2)
        oT = rot_sb(g, st, "xoT", [128, 8, 128], BF16, 2)
        pre = xl.load(x_src, 0)
        for s in range(NS):
            xT, bxT, tiles = pre
            for cc in range(8):
                p_, bp_ = pA.next()
                for c in range(8):
                    MM(g, p_[:], Wq[:, c, cc * 128:(cc + 1) * 128], xT[:, c, :], c == 0, c == 7, reads=[bWq] + bxT, writes=[bp_])
                ACT(g, qT[:, cc, :], p_[:], AF.Copy, reads=[bp_], writes=[bqT], scale=1.0 / 16)
            for hd in range(4):
                for mc in range(2):
                    p_, bp_ = pA.next()
                    for e in range(2):
                        MM(g, p_[:], KmT[:, hd, e, mc * 128:(mc + 1) * 128], qT[:, 2 * hd + e, :], e == 0, e == 1, reads=[bKmT, bqT], writes=[bp_])
                    pT, bpT = pTs[hd * 2 + mc]
                    ACT(g, pT[:], p_[:], AF.Exp, reads=[bp_], writes=[bpT])
            if s + 1 < NS:
                pre = xl.load(x_src, s + 1)
            for j in range(4):
                xs, bxs = tiles[j]
                o_, bo_ = o_r.next()
                rd, brd = rdn.next()
                for hd in range(4):
                    p_, bp_ = pA.next()
                    for mc in range(2):
                        pT, bpT = pTs[hd * 2 + mc]
                        MM(g, p_[:, 0:257], pT[:, j * 128:(j + 1) * 128], Vm[:, mc, hd, :], mc == 0, mc == 1, reads=[bpT, bVm], writes=[bp_])
                    fw.op(fw.dve, "reciprocal", rd[:, hd:hd + 1], p_[:, 256:257], reads=[bp_], writes=[brd])
                    TS(g, fw.dve, o_[:, hd * 256:(hd + 1) * 256], p_[:, 0:256], rd[:, hd:hd + 1], None, ALU.mult, None, reads=[bp_, brd], writes=[bo_])
                ot, bot = oT.next()
                transpose_into(g, o_, bo_, None, xl.tp, ot, bot, 0, 8, src_is_bf16=True)
                ys = []
                for hf in range(2):
                    y, by = py.next()
                    for c in range(8):
                        MM(g, y[:], ot[:, c, :], Wo[:, c, hf * 512:(hf + 1) * 512], c == 0, c == 7, reads=[bot, bWo], writes=[by])
                    ys.append((y, by))
                t0 = s * 512 + j * 128
                ln.run(xs, bxs, ys[0][0], ys[0][1], ys[1][0], ys[1][1], x_dst[t0:t0 + 128, :])
        end_phase(g)


def ffn(g, li, x_src, x_dst, is_output):
    fw = g.fw
    NF = FFN_H // 128
    with contextlib.ExitStack() as st:
        W1 = sb(g, st, "W1", [128, 8, 2 * FFN_H], BF16)
        W2 = sb(g, st, "W2", [128, NF, D], BF16)
        bW1, bW2 = Buf(), Buf()
        load_w(g, W1, bW1, g.I.ffn_in[li], D, 2 * FFN_H, split=8)
        load_w(g, W2, bW2, g.I.ffn_dn[li], FFN_H, D, split=2)
        ln = LNUnit(g, st, li * 3 + 2)
        xl = XLoader(g, st, n_xs=4, n_xT=1)
        pA = rot_ps(g, st, "pA", [128, 512], F32, 4)
        py = rot_ps(g, st, "py", [128, 512], F32, 2)
        aT = sb(g, st, "aT", [128, NF, 512], BF16)
        baT = Buf()
        sg = rot_sb(g, st, "sg", [128, 512], F32, 2)
        for s in range(NS):
            xT, bxT, tiles = xl.load(x_src, s)
            for fc in range(NF):
                pg, bpg = pA.next()
                pu, bpu = pA.next()
                for c in range(8):
                    MM(g, pg[:], W1[:, c, fc * 128:(fc + 1) * 128], xT[:, c, :], c == 0, c == 7, reads=[bW1] + bxT, writes=[bpg])
                for c in range(8):
                    MM(g, pu[:], W1[:, c, FFN_H + fc * 128:FFN_H + (fc + 1) * 128], xT[:, c, :], c == 0, c == 7, reads=[bW1] + bxT, writes=[bpu])
                s_, bs_ = sg.next()
                ACT(g, s_[:], pg[:], AF.Silu, reads=[bpg], writes=[bs_])
                TT(g, fw.dve, aT[:, fc, :], s_[:], pu[:], ALU.mult, reads=[bs_, bpu], writes=[baT])
            for j in range(4):
                xs, bxs = tiles[j]
                ys = []
                for hf in range(2):
                    y, by = py.next()
                    for fc in range(NF):
                        MM(g, y[:], aT[:, fc, j * 128:(j + 1) * 128], W2[:, fc, hf * 512:(hf + 1) * 512], fc == 0, fc == NF - 1, reads=[baT, bW2], writes=[by])
                    ys.append((y, by))
                t0 = s * 512 + j * 128
                ln.run(xs, bxs, ys[0][0], ys[0][1], ys[1][0], ys[1][1], x_dst[t0:t0 + 128, :], is_output=is_output)
        end_phase(g)


def l1_p1(g, x_src):
    fw = g.fw
    with contextlib.ExitStack() as st:
        Win = sb(g, st, "Win1", [128, 8, 2120], BF16)
        bWin = Buf()
        load_w(g, Win, bWin, g.I.dsa_in, D, 2120, split=2)
        xl = XLoader(g, st, n_xs=3, n_xT=2)
        pq = rot_ps(g, st, "pq", [128, 512], F32, 4)
        pv = rot_ps(g, st, "pv", [128, 512], F32, 2)
        ev = rot_sb(g, st, "ev", [128, 512], BF16, 4)
        vsb = rot_sb(g, st, "vsb1", [128, 4, 65], BF16, 2)
        for (t, b) in vsb.items:
            fw.op(fw.pool, "memset", t[:], 1.0, writes=[b])
        wsb = rot_sb(g, st, "wsb", [128, 8], F32, 2)
        pre = xl.load(x_src, 0)
        for s in range(NS):
            xT, bxT, tiles = pre
            cols = slice(s * 512, (s + 1) * 512)
            for j in range(4):
                tsl = slice(j * 128, (j + 1) * 128)
                t0 = s * 512 + j * 128
                p_, bp_ = pv.next()
                for c in range(8):
                    MM(g, p_[:, 0:256], xT[:, c, tsl], Win[:, c, 1280:1536], c == 0, c == 7, reads=[bxT[j], bWin], writes=[bp_])
                vt, bvt = vsb.next()
                CP(g, fw.act, vt[:, :, 0:64], p_[:, 0:256].rearrange("p (a b) -> p a b", b=64), reads=[bp_], writes=[bvt])
                fw.dma(fw.sp, g.V1[t0:t0 + 128, :, :], vt[:], reads=[bvt])
                p_, bp_ = pv.next()
                for c in range(8):
                    MM(g, p_[:, 0:8], xT[:, c, tsl], Win[:, c, 2112:2120], c == 0, c == 7, reads=[bxT[j], bWin], writes=[bp_])
                wt, bwt = wsb.next()
                ACT(g, wt[:], p_[:, 0:8], AF.Copy, reads=[bp_], writes=[bwt], scale=8 ** -0.5)
                fw.dma(fw.sp, g.WI[t0:t0 + 128, :], wt[:], reads=[bwt])
            if s + 1 < NS:
                pre = xl.load(x_src, s + 1)
            jobs = [(0, 8, 0.125, g.QT1), (1024, 2, 1.0, g.KT1), (1536, 4, 0.125, g.QI)]
            for (c0, nch, sc, dst) in jobs:
                for p in range(nch):
                    p_, bp_ = pq.next()
                    for c in range(8):
                        MM(g, p_[:], Win[:, c, c0 + p * 128:c0 + (p + 1) * 128], xT[:, c, :], c == 0, c == 7, reads=bxT + [bWin], writes=[bp_])
                    e_, be_ = ev.next()
                    ACT(g, e_[:], p_[:], AF.Copy, reads=[bp_], writes=[be_], scale=sc)
                    fw.dma(fw.sp, dst[p * 128:(p + 1) * 128, cols], e_[:], reads=[be_])
            p_, bp_ = pq.next()
            for c in range(8):
                MM(g, p_[0:64, :], Win[:, c, 2048:2112], xT[:, c, :], c == 0, c == 7, reads=bxT + [bWin], writes=[bp_])
            e_, be_ = ev.next()
            ACT(g, e_[0:64, :], p_[0:64, :], AF.Copy, reads=[bp_], writes=[be_])
            fw.dma(fw.sp, g.KI[:, cols], e_[0:64, :], reads=[be_])
        end_phase(g)


def l1_p2(g):
    fw = g.fw
    NB = N_BISECT
    with contextlib.ExitStack() as st:
        KI2 = sb(g, st, "KI2", [128, S], BF16)
        bKI2 = Buf()
        fw.dma(fw.sp, KI2[0:64, :], g.KI[:, :], writes=[bKI2])
        fw.dma(fw.sp, KI2[64:128, :], g.KI[:, :], writes=[bKI2])
        trif = sb(g, st, "trif", [128, 128], F32)
        pw2 = sb(g, st, "pw2", [128, 32], F32)
        zt = sb(g, st, "zt", [128, 128], BF16)
        btrif, bpw2, bzt = Buf(), Buf(), Buf()
        fw.dma(fw.sp, trif[:], g.I.c_trif[:, :], writes=[btrif])
        fw.dma(fw.sp, pw2[:], g.I.c_pow2[0:1, :].partition_broadcast(128), writes=[bpw2])
        fw.op(fw.dve, "memset", zt[:], 0.0, writes=[bzt])
        MBv = g.MBT.rearrange("c k q -> k c q")
        fw.dma(fw.sp, g.MBT[0, :, 0:128], g.tri[:], reads=[g.b_tri])
        fw.dma(fw.sp, g.MBT[0, :, 128:256], zt[:], reads=[bzt])
        fw.dma(fw.sp, g.MBT[1, :, 128:256], g.tri[:], reads=[g.b_tri])

        qi_r = rot_sb(g, st, "qi", [128, 4, 128], BF16, 2)
        wt_r = rot_sb(g, st, "wt", [128, 8], F32, 2)
        dg_r = rot_sb(g, st, "dg", [128, 8, 128], BF16, 2)
        sc_r = rot_sb(g, st, "score", [128, S], F32, 2)
        rl_r = rot_sb(g, st, "rl", [128, 512], BF16, 4)
        junk = sb(g, st, "bjunk", [128, S], BF16)
        bjunk = Buf()
        sm_r = rot_sb(g, st, "bsm", [128, 40], F32, 2)
        mb_r = rot_sb(g, st, "mb", [128, S], BF16, 2)
        mbT_r = rot_sb(g, st, "mbT", [128, NT, 128], BF16, 2)
        pD = rot_ps(g, st, "pD", [128, 512], F32, 4)
        pSc = rot_ps(g, st, "pSc", [128, 512], F32, 2)
        ptp = rot_ps(g, st, "ptp", [128, 8, 128], BF16, 2)
        QIv = g.QI.rearrange("(ip r) t -> r ip t", r=128)

        def score_tile(i):
            nk = (i + 1) * 128
            qi, bqi = qi_r.next()
            wt, bwt = wt_r.next()
            dg, bdg = dg_r.next()
            sc, bsc = sc_r.next()
            fw.dma(fw.sp, qi[:], QIv[:, :, i * 128:(i + 1) * 128], writes=[bqi])
            fw.dma(fw.sp, wt[:], g.WI[i * 128:(i + 1) * 128, :], writes=[bwt])
            for h in range(8):
                ACT(g, dg[:, h, :], g.ident[:], AF.Copy, reads=[g.b_ident, bwt], writes=[bdg], scale=wt[:, h:h + 1])
            for kb in range((nk + 511) // 512):
                w = min(512, nk - kb * 512)
                ps_, bps_ = pSc.next()
                for h in range(8):
                    half = slice((h % 2) * 64, (h % 2) * 64 + 64)
                    pd, bpd = pD.next()
                    MM(g, pd[:, 0:w], qi[half, h // 2, :], KI2[half, kb * 512:kb * 512 + w], True, True, reads=[bqi, bKI2], writes=[bpd])
                    rl, brl = rl_r.next()
                    ACT(g, rl[:, 0:w], pd[:, 0:w], AF.Relu, reads=[bpd], writes=[brl])
                    MM(g, ps_[:, 0:w], dg[:, h, :], rl[:, 0:w], h == 0, h == 7, reads=[bdg, brl], writes=[bps_])
                CP(g, fw.act, sc[:, kb * 512:kb * 512 + w], ps_[:, 0:w], reads=[bps_], writes=[bsc])
            return sc, bsc

        def bisect_tile(i, sc, bsc):
            nk = (i + 1) * 128
            sm, bsm = sm_r.next()
            dve = fw.dve
            TT(g, dve, sc[:, i * 128:nk], sc[:, i * 128:nk], trif[:], ALU.add, reads=[bsc, btrif], writes=[bsc])
            TS(g, dve, junk[:, 0:nk], sc[:, 0:nk], 1.0, -3.0e38, ALU.mult, ALU.max, reads=[bsc], writes=[bjunk, bsm], accum_out=sm[:, 0:1])
            TS(g, dve, junk[:, 0:i * 128], sc[:, 0:i * 128], -1.0, -3.0e38, ALU.mult, ALU.max, reads=[bsc], writes=[bjunk, bsm], accum_out=sm[:, 1:2])
            TS(g, dve, sm[:, 1:2], sm[:, 1:2], -1.0, None, ALU.mult, None, reads=[bsm], writes=[bsm])
            TT(g, dve, sm[:, 2:3], sm[:, 0:1], sm[:, 1:2], ALU.subtract, reads=[bsm], writes=[bsm])
            TS(g, dve, sm[:, 8:40], pw2[:], sm[:, 2:3], None, ALU.mult, None, reads=[bsm, bpw2], writes=[bsm])
            TT(g, dve, sm[:, 3:4], sm[:, 1:2], sm[:, 8:9], ALU.add, reads=[bsm], writes=[bsm])
            for n in range(NB):
                TS(g, dve, junk[:, 0:nk], sc[:, 0:nk], sm[:, 3:4], 0.0, ALU.is_ge, ALU.add, reads=[bsc, bsm], writes=[bjunk, bsm], accum_out=sm[:, 4:5])
                TS(g, dve, sm[:, 5:6], sm[:, 4:5], float(TOPK), 0.5, ALU.is_ge, ALU.subtract, reads=[bsm], writes=[bsm])
                STT(g, dve, sm[:, 3:4], sm[:, 5:6], sm[:, 8 + n:9 + n], sm[:, 3:4], ALU.mult, ALU.add, reads=[bsm], writes=[bsm])
            TT(g, dve, sm[:, 6:7], sm[:, 3:4], sm[:, 8 + NB:9 + NB], ALU.subtract, reads=[bsm], writes=[bsm])
            return sm, bsm

        def mask_tile(i, sc, bsc, sm, bsm):
            nk = (i + 1) * 128
            mb, bmb = mb_r.next()
            TS(g, fw.dve, mb[:, 0:nk], sc[:, 0:nk], sm[:, 6:7], NEG, ALU.is_lt, ALU.mult, reads=[bsc, bsm], writes=[bmb])
            mbT, bmbT = mbT_r.next()
            for c0 in range(0, i + 1, 8):
                n_ = min(8, i + 1 - c0)
                tp, btp = ptp.next()
                for c in range(n_):
                    fw.op(fw.pe, "transpose", tp[:, c, :], mb[:, (c0 + c) * 128:(c0 + c + 1) * 128], g.ident[:],
                          reads=[bmb, g.b_ident], writes=[btp], signal=(c == n_ - 1))
                CP(g, fw.act, mbT[:, c0:c0 + n_, :], tp[:, 0:n_, :], reads=[btp], writes=[bmbT])
            fw.dma(fw.sp, MBv[:, 0:i + 1, i * 128:(i + 1) * 128], mbT[:, 0:i + 1, :], reads=[bmbT])

        cur = score_tile(2)
        for i in range(2, NT):
            nxt = score_tile(i + 1) if i + 1 < NT else None
            sm, bsm = bisect_tile(i, cur[0], cur[1])
            mask_tile(i, cur[0], cur[1], sm, bsm)
            cur = nxt
        end_phase(g)


def l1_p3(g):
    fw = g.fw
    with contextlib.ExitStack() as st:
        A = attn_bufs(g, st)
        Bt = sb(g, st, "Bt", [128, 16, 2, 128], BF16)
        bBts = [Buf() for _ in range(32)]
        J = sb(g, st, "antiI", [128, 128], BF16)
        bJ = Buf()
        fw.dma(fw.pool, J[:], g.I.c_antiident[:, :], writes=[bJ])
        with contextlib.ExitStack() as t:
            rbf = sb(g, t, "rbf", [32, 16], F32)
            rbh = sb(g, t, "rbh", [32, 16], BF16)
            rbhf = sb(g, t, "rbhf", [32, 16], F32)
            rbl = sb(g, t, "rbl", [32, 16], BF16)
            oh = sb(g, t, "oh", [32, 384], BF16)
            tb = sb(g, t, "tb", [16, 384], F32)
            b1, b2, b3, b4, b5, b6 = Buf(), Buf(), Buf(), Buf(), Buf(), Buf()
            fw.dma(fw.sp, rbf[:], g.I.rel_bias[:, :], writes=[b1])
            fw.dma(fw.pool, oh[:], g.I.c_oh[:, :], writes=[b5])
            CP(g, fw.dve, rbh[:], rbf[:], reads=[b1], writes=[b2])
            CP(g, fw.dve, rbhf[:], rbh[:], reads=[b2], writes=[b3])
            TT(g, fw.dve, rbl[:], rbf[:], rbhf[:], ALU.subtract, reads=[b1, b3], writes=[b4])
            pt, bpt = A.pS.next()
            MM(g, pt[0:16, 0:384], rbh[:], oh[:], True, False, reads=[b2, b5], writes=[bpt], signal=False)
            MM(g, pt[0:16, 0:384], rbl[:], oh[:], False, True, reads=[b4, b5], writes=[bpt])
            CP(g, fw.dve, tb[:], pt[0:16, 0:384], reads=[bpt], writes=[b6])
            TS(g, fw.dve, tb[:], tb[:], tb[:, 383:384], None, ALU.subtract, None, reads=[b6], writes=[b6])
            fw.dma(fw.sp, g.TB[:, :], tb[:], reads=[b6], writes=[g.b_TB])
            for hd in range(16):
                for v, base in ((0, 1), (1, 129)):
                    src = bass.AP(tensor=g.TB.tensor, offset=hd * 384 + base, ap=[[1, 128], [1, 128]])
                    fw.dma(fw.pool, Bt[:, hd, v, :], src, reads=[g.b_TB], writes=[bBts[hd * 2 + v]])
            end_phase(g)
        KT2 = sb(g, st, "KT2", [128, 4, S], BF16)
        bKT2 = Buf()
        for n in range(4):
            fw.dma(fw.sp, KT2[0:64, n, :], g.KT1[n * 64:(n + 1) * 64, :], writes=[bKT2])
            fw.dma(fw.sp, KT2[64:128, n, :], g.KT1[n * 64:(n + 1) * 64, :], writes=[bKT2])
        V1 = sb(g, st, "V1all", [128, NT, 4 * 65], BF16)
        bV1 = Buf()
        Vv = g.V1.rearrange("(c p) h e -> p c (h e)", p=128)
        fw.dma(fw.sp, V1[:], Vv[:, :, :], writes=[bV1])
        Qe_r = rot_sb(g, st, "Qe", [128, 8, 512], BF16, 2)
        Qo_r = rot_sb(g, st, "Qo", [128, 8, 512], BF16, 2)
        for (t_, b_) in Qe_r.items + Qo_r.items:
            fw.op(fw.pool, "memset", t_[:], 0.0, writes=[b_])
        mb_r = rot_sb(g, st, "mbq", [128, NT, 512], BF16, 2)
        QTv = g.QT1.rearrange("(p r) t -> r p t", r=128)
        MBv = g.MBT.rearrange("c k q -> k c q")
        def load_qb(qb):
            nkc_ = 4 * qb + 4
            Qe, bQe = Qe_r.next()
            Qo, bQo = Qo_r.next()
            mbq, bmbq = mb_r.next()
            fw.dma(fw.sp, Qe[0:64, :, :], QTv[0:64, :, qb * 512:(qb + 1) * 512], writes=[bQe])
            fw.dma(fw.sp, Qo[64:128, :, :], QTv[64:128, :, qb * 512:(qb + 1) * 512], writes=[bQo])
            for c0 in range(0, nkc_, 8):
                c1 = min(nkc_, c0 + 8)
                fw.dma(fw.sp, mbq[:, c0:c1, :], MBv[:, c0:c1, qb * 512:(qb + 1) * 512], writes=[bmbq])
            return Qe, bQe, Qo, bQo, mbq, bmbq

        nxt_qb = load_qb(0)
        for qb in range(NS):
            nkc = 4 * qb + 4
            Qe, bQe, Qo, bQo, mbq, bmbq = nxt_qb
            if qb + 1 < NS:
                nxt_qb = load_qb(qb + 1)
            steps = []
            for kc in range(nkc):
                r = kc - 4 * qb
                steps.append((kc, 128 * r if r > 0 else 0))
            for hd in range(16):
                n = hd // 4
                Qb, bQb = (Qe, bQe) if hd % 2 == 0 else (Qo, bQo)
                po, bpo = A.pO.next()

                def S_fn(kc, off, pS, bpS, hd=hd, n=n, Qb=Qb, bQb=bQb, mbq=mbq, bmbq=bmbq, qb=qb):
                    adds = []
                    for v in (0, 1):
                        j = kc + v - 4 * qb
                        if 0 <= j < 4 and j * 128 >= off:
                            adds.append((v, j))
                    MM(g, pS[:, off:512], KT2[:, n, kc * 128:(kc + 1) * 128], Qb[:, hd // 2, off:512], True, False,
                       reads=[bKT2, bQb], writes=[bpS], signal=False)
                    MM(g, pS[:, off:512], g.ident[:], mbq[:, kc, off:512], False, len(adds) == 0, reads=[g.b_ident, bmbq], writes=[bpS])
                    for ai, (v, j) in enumerate(adds):
                        MM(g, pS[:, j * 128:(j + 1) * 128], J[:], Bt[:, hd, v, :], False, ai == len(adds) - 1, reads=[bJ, bBts[hd * 2 + v]], writes=[bpS])

                def V_fn(kc, n=n):
                    return V1[:, kc, n * 65:(n + 1) * 65], [bV1]

                attn_qblock(g, A, steps, S_fn, V_fn, po, bpo)
                attn_epilogue(g, A, po, bpo, hd, qb)
        attn_flush(A)
        end_phase(g)


def layer1(g):
    if "skip_l1" in g.debug:
        return
    src = g.xc
    l1_p1(g, src)
    if "stop_q1" in g.debug:
        return
    l1_p2(g)
    if "stop_q2" in g.debug:
        return
    l1_p3(g)
    if "stop_q3" in g.debug:
        return
    with contextlib.ExitStack() as st:
        proj_out_ln(g, st, g.OT.rearrange("h d t -> (h d) t"), g.I.dsa_wo, 3, src, g.xa)
        end_phase(g)
    if "stop_q4" in g.debug:
        return
    cross_attn(g, 1, g.xa, g.xb)
    if "stop_q5" in g.debug:
        return
    ffn(g, 1, g.xb, g.out, is_output=True)


def t5_bucket_np(d):
    n = np.maximum(d, 0)
    nf = np.maximum(n, 1).astype(np.float32)
    large = 16 + (np.log(nf / np.float32(16)) / np.float32(math.log(128 / 16)) * np.float32(16)).astype(np.int32)
    large = np.minimum(large, 31)
    return np.where(n < 16, n, large)


def make_consts():
    c = {}
    c["c_ident"] = np.eye(128, dtype=np.float32)
    k = np.arange(128)[:, None]
    q = np.arange(128)[None, :]
    c["c_tri"] = np.where(k <= q, 0.0, NEG).astype(np.float32)
    c["c_trif"] = np.where(np.arange(128)[None, :] <= np.arange(128)[:, None], 0.0, -1e30).astype(np.float32)
    inv = (np.float32(10000.0) ** (-np.arange(0, 32, 2, dtype=np.float32) / np.float32(32))).astype(np.float32)
    invf = np.zeros((32, 2), np.float32)
    invf[:, 0] = np.concatenate([inv, inv])
    invf[:, 1] = np.concatenate([-np.ones(16), np.ones(16)])
    c["c_invf"] = invf
    d = np.arange(-128, 256)
    b = t5_bucket_np(d)
    oh = np.zeros((32, 384), np.float32)
    oh[b, np.arange(384)] = 1.0
    c["c_oh"] = oh
    c["c_antiident"] = np.ascontiguousarray(np.eye(128, dtype=np.float32)[::-1])
    c["c_pow2"] = (2.0 ** -(np.arange(32) + 1.0)).astype(np.float32)[None, :]
    return c


def prep_weights(inp):
    w = {}
    f = lambda a: np.ascontiguousarray(a, dtype=np.float32)
    win = inp["mla_w_in"][0]
    w["mla_in"] = f(np.concatenate([win, win[:, 784:800], win[:, 768:784]], axis=1))
    uq = inp["mla_w_uq"][0].reshape(512, 16, 96)
    nope, x1, x2 = uq[:, :, 0:64], uq[:, :, 64:80], uq[:, :, 80:96]
    main = np.concatenate([nope, x1, x2], axis=2).reshape(512, 1536)
    sw = np.concatenate([nope, x2, x1], axis=2).reshape(512, 1536)
    w["mla_uq"] = f(np.concatenate([main, sw], axis=1))
    ukv = inp["mla_w_ukv"][0].reshape(256, 16, 128)
    w["mla_ukv"] = f(np.concatenate([ukv[:, :, 0:64].reshape(256, 1024), ukv[:, :, 64:128].reshape(256, 1024)], axis=1))
    w["mla_qn"] = f(inp["mla_q_norm"][0][None, :])
    w["mla_kvn"] = f(inp["mla_kv_norm"][0][None, :])
    w["mla_wo"] = f(inp["mla_w_o"][0])
    w["dsa_in"] = f(inp["dsa_w_in"][0])
    w["dsa_wo"] = f(inp["dsa_w_o"][0])
    w["xa_wq"] = f(inp["xa_w_q"])
    w["xa_wkv"] = f(inp["xa_w_kv"])
    w["xa_wo"] = f(inp["xa_w_o"])
    w["ffn_in"] = f(inp["ffn_w_in"])
    w["ffn_dn"] = f(inp["ffn_w_down"])
    w["ln_g"] = f(inp["ln_g"].reshape(6, D))
    w["ln_b"] = f(inp["ln_b"].reshape(6, D))
    w["rel_bias"] = f(inp["rel_bias"])
    return w


_CACHE = {}


def run(inputs, debug=(), n_cores=8):
    key = tuple(debug)
    if key not in _CACHE:
        _CACHE[key] = build_program(debug)
    nc = _CACHE[key]
    shared = prep_weights(inputs)
    shared.update(make_consts())
    in_maps = []
    for b in range(n_cores):
        m = dict(shared)
        m["x"] = np.ascontiguousarray(inputs["x"][b], dtype=np.float32)
        m["mem"] = np.ascontiguousarray(inputs["mem"][b], dtype=np.float32)
        m["pos"] = np.ascontiguousarray(inputs["positions"][b][None, :], dtype=np.int32)
        in_maps.append(m)
    res = run_bass_kernel_spmd(nc, in_maps, core_ids=list(range(n_cores)))
    return res


def kernel(**inputs):
    res = run(inputs)
    return np.stack([np.asarray(r["out"], dtype=np.float32) for r in res.results], axis=0)
```
